# Optimizing a Trainium2 kernel written in Bass

```python
import jax
import jax.numpy as jnp
from jax import lax
import numpy as np

D_MODEL = 1024
BATCH = 4
SEQ = 8192
DEPTH = 2

GRID_W = 64
CTX_LEN = 256
N_MOD = 9
D_FF = 2816
EPS = 1e-6
S5_WIDTH = D_MODEL // 2
S5_GROUP = 16
S5_GROUPS = S5_WIDTH // S5_GROUP
S5_STATE = 64
POOL_WINDOWS = (2, 4, 8, 16)
POOL_WIDTH = D_MODEL - S5_WIDTH
POOL_GROUP = POOL_WIDTH // len(POOL_WINDOWS)
EVEN_MIX_WIDTH = S5_WIDTH + POOL_WIDTH
GDN_HEADS = 8
GDN_HEAD_DIM = D_MODEL // GDN_HEADS
GDN_WIDTH = GDN_HEADS * GDN_HEAD_DIM
GDN_CONV = 7
GDN_CHUNK = 64
GDN_IN_WIDTH = 4 * GDN_WIDTH + 4 * GDN_HEADS
N_EVEN = (DEPTH + 1) // 2
N_ODD = DEPTH // 2

kernel_name = 'hybrid_s5_pool_gdn_diffusion_block'


def _rmsnorm(x, g):
    x32 = x.astype(jnp.float32)
    y = x32 * lax.rsqrt(jnp.mean(x32 * x32, axis=-1, keepdims=True) + EPS)
    return (y * g.astype(jnp.float32)).astype(x.dtype)


def _modulate(s, g, shift, scale):
    return _rmsnorm(s, g) * (1 + scale) + shift


def _swiglu(h, w13, w2):
    a, b = jnp.split(h @ w13, 2, axis=-1)
    return (jax.nn.silu(a) * b) @ w2


def _ffn_half(s, g, mod, k0, w13, w2):
    h = _modulate(s, g, mod[:, :, k0], mod[:, :, k0 + 1])
    return s + 0.5 * mod[:, :, k0 + 2] * _swiglu(h, w13, w2)


def _l2norm(t):
    return t * lax.rsqrt(jnp.sum(t * t, axis=-1, keepdims=True) + EPS)


def _centred_window_mean(x, w):
    n = x.shape[-2]
    left = w // 2
    right = w - 1 - left
    t = np.arange(n)
    lo = np.clip(t - left, 0, n)
    hi = np.clip(t + right + 1, 0, n)
    cs = lax.cumsum(x, axis=x.ndim - 2)
    cs = jnp.concatenate([jnp.zeros_like(cs[..., :1, :]), cs], axis=-2)
    total = jnp.take(cs, hi, axis=-2) - jnp.take(cs, lo, axis=-2)
    count = jnp.asarray((hi - lo).astype(np.float32))[:, None]
    return total / count


def _pool_mixer(u, pool_w, pool_scale, rows, row_len):
    B_, L, _ = u.shape
    ug = u.astype(jnp.float32).reshape(B_, rows, row_len, POOL_WIDTH)
    parts = []
    for j, w in enumerate(POOL_WINDOWS):
        grp = ug[..., j * POOL_GROUP:(j + 1) * POOL_GROUP]
        parts.append(_centred_window_mean(grp, w) - grp)
    y = jnp.stack(parts, axis=-2).reshape(B_, L, len(POOL_WINDOWS), POOL_GROUP)
    y = jnp.einsum('blgi,gio->blgo', y, pool_w.astype(jnp.float32)).reshape(B_, L, POOL_WIDTH)
    return (y * pool_scale.astype(jnp.float32)).astype(u.dtype)


def _s5_discretise(lam_re, lam_im, log_step):
    lam_re = jnp.minimum(lam_re, -1e-4)
    dt = jnp.exp(log_step)[:, None]
    mag = jnp.exp(lam_re * dt)
    lb_re = mag * jnp.cos(lam_im * dt)
    lb_im = mag * jnp.sin(lam_im * dt)
    den = lam_re * lam_re + lam_im * lam_im
    n_re = lb_re - 1.0
    f_re = (n_re * lam_re + lb_im * lam_im) / den
    f_im = (lb_im * lam_re - n_re * lam_im) / den
    return lb_re, lb_im, f_re, f_im


def _cscan_op(e1, e2):
    a1r, a1i, b1r, b1i = e1
    a2r, a2i, b2r, b2i = e2
    return (a1r * a2r - a1i * a2i, a1r * a2i + a1i * a2r,
            a2r * b1r - a2i * b1i + b2r, a2r * b1i + a2i * b1r + b2i)


def _s5_input(u, b_re, b_im):
    ug = u.reshape(u.shape[0], u.shape[1], S5_GROUPS, S5_GROUP)
    return (jnp.einsum('blgi,gpi->blgp', ug, b_re), jnp.einsum('blgi,gpi->blgp', ug, b_im))


def _s5_states(bu, disc, h0, reverse):
    bu_re, bu_im = bu
    lb_re, lb_im, f_re, f_im = disc
    b_re = f_re * bu_re - f_im * bu_im
    b_im = f_re * bu_im + f_im * bu_re
    if h0 is not None:
        h_re, h_im = h0
        start = -1 if reverse else 0
        b_re = b_re.at[:, start].add(lb_re * h_re - lb_im * h_im)
        b_im = b_im.at[:, start].add(lb_re * h_im + lb_im * h_re)
    shape = (1, bu_re.shape[1]) + lb_re.shape
    a_re = jnp.broadcast_to(lb_re, shape)
    a_im = jnp.broadcast_to(lb_im, shape)
    _, _, x_re, x_im = lax.associative_scan(_cscan_op, (a_re, a_im, b_re, b_im), reverse=reverse, axis=1)
    return x_re, x_im


def _s5_readout(xs, u, c_re, c_im, d, w_glu, b_glu):
    f32 = jnp.float32
    x_re, x_im = xs
    y = (jnp.einsum('blgp,gip->blgi', x_re, c_re.astype(f32))
         - jnp.einsum('blgp,gip->blgi', x_im, c_im.astype(f32)))
    y = y.reshape(u.shape) + d.astype(f32) * u
    g = jax.nn.gelu(y)
    return g * jax.nn.sigmoid(g @ w_glu.astype(f32) + b_glu.astype(f32))


def _even_mixer(h_lat, h_ctx, rows, w_in, lam_re, lam_im, log_step, b_re, b_im, c_re, c_im, d,
                w_glu, b_glu, pool_w, pool_scale, w_out, with_ctx_out):
    f32 = jnp.float32
    dtype = h_lat.dtype
    p_lat = h_lat @ w_in
    p_ctx = h_ctx @ (w_in if with_ctx_out else w_in[:, :S5_WIDTH])
    u_lat = p_lat[..., :S5_WIDTH].astype(f32)
    u_ctx = p_ctx[..., :S5_WIDTH].astype(f32)
    bu_lat = _s5_input(u_lat, b_re.astype(f32), b_im.astype(f32))
    bu_ctx = _s5_input(u_ctx, b_re.astype(f32), b_im.astype(f32))
    x_lat = None
    x_ctx = None
    for dirn, reverse in ((0, False), (1, True)):
        disc = _s5_discretise(lam_re[dirn].astype(f32), lam_im[dirn].astype(f32), log_step[dirn].astype(f32))
        xc = _s5_states(bu_ctx, disc, None, reverse)
        end = 0 if reverse else -1
        xl = _s5_states(bu_lat, disc, (xc[0][:, end], xc[1][:, end]), reverse)
        x_lat = xl if x_lat is None else (x_lat[0] + xl[0], x_lat[1] + xl[1])
        if with_ctx_out:
            x_ctx = xc if x_ctx is None else (x_ctx[0] + xc[0], x_ctx[1] + xc[1])
    y_a = _s5_readout(x_lat, u_lat, c_re, c_im, d, w_glu, b_glu).astype(dtype)
    y_b = _pool_mixer(p_lat[..., S5_WIDTH:], pool_w, pool_scale, rows, GRID_W)
    out_lat = jnp.concatenate([y_a, y_b], axis=-1) @ w_out
    if not with_ctx_out:
        return out_lat, None
    y_a_c = _s5_readout(x_ctx, u_ctx, c_re, c_im, d, w_glu, b_glu).astype(dtype)
    y_b_c = _pool_mixer(p_ctx[..., S5_WIDTH:], pool_w, pool_scale, 1, h_ctx.shape[1])
    out_ctx = jnp.concatenate([y_a_c, y_b_c], axis=-1) @ w_out
    return out_lat, out_ctx


def _depthwise_conv(x, w):
    k = w.shape[0]
    return lax.conv_general_dilated(x, w[:, None, :], window_strides=(1,), padding=[(k // 2, k // 2)],
                                    dimension_numbers=('NWC', 'WIO', 'NWC'), feature_group_count=x.shape[-1])


def _gdn_project(h, w_in, conv_w, with_query):
    f32 = jnp.float32
    B_, L, _ = h.shape
    W, H, Dh = GDN_WIDTH, GDN_HEADS, GDN_HEAD_DIM
    n_state = 2 * W + 4 * H
    p = h @ (w_in if with_query else w_in[:, :n_state])
    kv = jax.nn.silu(_depthwise_conv(p[..., :2 * W], conv_w[:, :2 * W])).astype(f32)
    k = _l2norm(kv[..., :W].reshape(B_, L, H, Dh))
    v = kv[..., W:].reshape(B_, L, H, Dh)
    a = p[..., 2 * W:2 * W + 2 * H].astype(f32).reshape(B_, L, 2, H)
    b = p[..., 2 * W + 2 * H:n_state].astype(f32).reshape(B_, L, 2, H)
    if not with_query:
        return k, v, a, b, None, None
    q = jax.nn.silu(_depthwise_conv(p[..., n_state:n_state + W], conv_w[:, 2 * W:])).astype(f32)
    q = _l2norm(q.reshape(B_, L, H, Dh)) * (Dh ** -0.5)
    z = p[..., n_state + W:].reshape(B_, L, H, Dh)
    return k, v, a, b, q, z


def _gdn_gates(a, b, a_log, dt_bias, dirn):
    f32 = jnp.float32
    g = -jnp.exp(a_log[dirn].astype(f32)) * jax.nn.softplus(a[:, :, dirn] + dt_bias[dirn].astype(f32))
    beta = jax.nn.sigmoid(b[:, :, dirn])
    return g, beta


def _flip(t, rev):
    return jnp.flip(t, axis=1) if (rev and t is not None) else t


def _gdn_scan(q, k, v, g, beta, s0):
    B_, L, H, _ = k.shape
    Dv = v.shape[-1]
    n = L // GDN_CHUNK

    def blocks(t):
        t = t.reshape((B_, n, GDN_CHUNK, H) + t.shape[3:])
        return jnp.moveaxis(t, (1, 3), (0, 2))

    kc, vc, gb, bb = blocks(k), blocks(v), blocks(g), blocks(beta)
    gcum = lax.cumsum(gb, axis=3)
    idx = np.arange(GDN_CHUNK)
    lower = idx[:, None] >= idx[None, :]
    strict = idx[:, None] > idx[None, :]
    decay = jnp.exp(jnp.where(lower, gcum[..., :, None] - gcum[..., None, :], -jnp.inf))
    kb = kc * bb[..., None]
    lmat = jnp.where(strict, jnp.einsum('nbhik,nbhjk->nbhij', kb, kc) * decay, 0.0)
    rhs = jnp.concatenate([vc * bb[..., None], kb * jnp.exp(gcum)[..., None]], axis=-1)
    sol = lax.linalg.triangular_solve(lmat, rhs, left_side=True, lower=True, unit_diagonal=True)
    u, w = sol[..., :Dv], sol[..., Dv:]
    with_out = q is not None
    if with_out:
        qc = blocks(q)
        aqk = jnp.where(lower, jnp.einsum('nbhik,nbhjk->nbhij', qc, kc) * decay, 0.0)
        xs = (u, w, kc, gcum, qc * jnp.exp(gcum)[..., None], aqk)
    else:
        xs = (u, w, kc, gcum)

    def step(S, blk):
        u_, w_, k_, g_ = blk[:4]
        v_new = u_ - jnp.einsum('bhck,bhkv->bhcv', w_, S)
        g_last = g_[..., -1]
        S_next = (S * jnp.exp(g_last)[..., None, None]
                  + jnp.einsum('bhck,bhcv->bhkv', k_ * jnp.exp(g_last[..., None] - g_)[..., None], v_new))
        if with_out:
            o = jnp.einsum('bhck,bhkv->bhcv', blk[4], S) + jnp.einsum('bhij,bhjv->bhiv', blk[5], v_new)
            return S_next, o
        return S_next, None

    s_final, o = lax.scan(step, s0, xs)
    if not with_out:
        return None, s_final
    o = jnp.transpose(o, (1, 0, 3, 2, 4)).reshape(B_, L, H, Dv)
    return o, s_final


def _gdn_out(o, z, norm_g, w_out, dtype):
    f32 = jnp.float32
    o = o * lax.rsqrt(jnp.mean(o * o, axis=-1, keepdims=True) + EPS) * norm_g.astype(f32)
    o = o * jax.nn.silu(z.astype(f32))
    return o.reshape(o.shape[0], o.shape[1], GDN_WIDTH).astype(dtype) @ w_out


def _odd_mixer(h_lat, h_ctx, w_in, conv_w, a_log, dt_bias, norm_g, w_out, with_ctx_out):
    dtype = h_lat.dtype
    k_l, v_l, a_l, b_l, q_l, z_l = _gdn_project(h_lat, w_in, conv_w, True)
    k_c, v_c, a_c, b_c, q_c, z_c = _gdn_project(h_ctx, w_in, conv_w, with_ctx_out)
    s_zero = jnp.zeros((h_lat.shape[0], GDN_HEADS, GDN_HEAD_DIM, GDN_HEAD_DIM), jnp.float32)
    o_lat = None
    o_ctx = None
    for dirn in (0, 1):
        rev = dirn == 1
        g_c, beta_c = _gdn_gates(a_c, b_c, a_log, dt_bias, dirn)
        oc, s_c = _gdn_scan(_flip(q_c, rev), _flip(k_c, rev), _flip(v_c, rev),
                            _flip(g_c, rev), _flip(beta_c, rev), s_zero)
        g_l, beta_l = _gdn_gates(a_l, b_l, a_log, dt_bias, dirn)
        ol, _ = _gdn_scan(_flip(q_l, rev), _flip(k_l, rev), _flip(v_l, rev),
                          _flip(g_l, rev), _flip(beta_l, rev), s_c)
        ol = _flip(ol, rev)
        o_lat = ol if o_lat is None else o_lat + ol
        if with_ctx_out:
            oc = _flip(oc, rev)
            o_ctx = oc if o_ctx is None else o_ctx + oc
    out_lat = _gdn_out(o_lat, z_l, norm_g, w_out, dtype)
    if not with_ctx_out:
        return out_lat, None
    return out_lat, _gdn_out(o_ctx, z_c, norm_g, w_out, dtype)


def setup_inputs(seed: int = 0) -> dict:
    key = jax.random.key(seed)
    ks = iter(jax.random.split(key, 48))
    f32 = jnp.float32
    D, F = D_MODEL, D_FF

    def nrm(shape, scale):
        return jax.random.normal(next(ks), shape, f32) * scale

    def unif(shape, lo, hi):
        return jax.random.uniform(next(ks), shape, f32, minval=lo, maxval=hi)

    x = nrm((BATCH, SEQ, D), 1.0)
    c = nrm((BATCH, D), 1.0)
    ctx = nrm((BATCH, CTX_LEN, D), 1.0)
    c_ctx = nrm((D,), 1.0)
    w_mod = nrm((DEPTH, D, N_MOD * D), 0.5 * D ** -0.5)
    b_mod = nrm((DEPTH, N_MOD * D), 0.02)
    norm_g = 1.0 + nrm((DEPTH, 3, D), 0.02)
    ffn1_w13 = nrm((DEPTH, D, 2 * F), D ** -0.5)
    ffn1_w2 = nrm((DEPTH, F, D), F ** -0.5)
    ffn2_w13 = nrm((DEPTH, D, 2 * F), D ** -0.5)
    ffn2_w2 = nrm((DEPTH, F, D), F ** -0.5)
    ev_w_in = nrm((N_EVEN, D, EVEN_MIX_WIDTH), D ** -0.5)
    s5_lambda_re = -0.5 + nrm((N_EVEN, 2, S5_GROUPS, S5_STATE), 0.01)
    s5_lambda_im = jnp.asarray(np.pi * np.arange(S5_STATE), f32) + nrm((N_EVEN, 2, S5_GROUPS, S5_STATE), 0.01)
    s5_log_step = unif((N_EVEN, 2, S5_GROUPS), float(np.log(1e-3)), float(np.log(1e-1)))
    s5_b_re = nrm((N_EVEN, S5_GROUPS, S5_STATE, S5_GROUP), (2 * S5_GROUP) ** -0.5)
    s5_b_im = nrm((N_EVEN, S5_GROUPS, S5_STATE, S5_GROUP), (2 * S5_GROUP) ** -0.5)
    s5_c_re = nrm((N_EVEN, S5_GROUPS, S5_GROUP, S5_STATE), (2 * S5_STATE) ** -0.5)
    s5_c_im = nrm((N_EVEN, S5_GROUPS, S5_GROUP, S5_STATE), (2 * S5_STATE) ** -0.5)
    s5_d = nrm((N_EVEN, S5_WIDTH), 1.0)
    s5_w_glu = nrm((N_EVEN, S5_WIDTH, S5_WIDTH), S5_WIDTH ** -0.5)
    s5_b_glu = nrm((N_EVEN, S5_WIDTH), 0.02)
    pool_w = nrm((N_EVEN, len(POOL_WINDOWS), POOL_GROUP, POOL_GROUP), POOL_GROUP ** -0.5)
    pool_scale = 1.0 + nrm((N_EVEN, POOL_WIDTH), 0.02)
    ev_w_out = nrm((N_EVEN, EVEN_MIX_WIDTH, D), EVEN_MIX_WIDTH ** -0.5)
    gdn_w_in = nrm((N_ODD, D, GDN_IN_WIDTH), D ** -0.5)
    gdn_conv_w = nrm((N_ODD, GDN_CONV, 3 * GDN_WIDTH), GDN_CONV ** -0.5)
    gdn_a_log = jnp.log(unif((N_ODD, 2, GDN_HEADS), 1.0, 16.0))
    dt = jnp.exp(unif((N_ODD, 2, GDN_HEADS), float(np.log(1e-3)), float(np.log(1e-1))))
    gdn_dt_bias = dt + jnp.log(-jnp.expm1(-dt))
    gdn_norm_g = 1.0 + nrm((N_ODD, GDN_HEAD_DIM), 0.02)
    gdn_w_out = nrm((N_ODD, GDN_WIDTH, D), GDN_WIDTH ** -0.5)
    final_norm_g = 1.0 + nrm((D,), 0.02)
    return {'x': x, 'c': c, 'ctx': ctx, 'c_ctx': c_ctx, 'w_mod': w_mod, 'b_mod': b_mod, 'norm_g': norm_g,
            'ffn1_w13': ffn1_w13, 'ffn1_w2': ffn1_w2, 'ffn2_w13': ffn2_w13, 'ffn2_w2': ffn2_w2,
            'ev_w_in': ev_w_in, 's5_lambda_re': s5_lambda_re, 's5_lambda_im': s5_lambda_im,
            's5_log_step': s5_log_step, 's5_b_re': s5_b_re, 's5_b_im': s5_b_im, 's5_c_re': s5_c_re,
            's5_c_im': s5_c_im, 's5_d': s5_d, 's5_w_glu': s5_w_glu, 's5_b_glu': s5_b_glu,
            'pool_w': pool_w, 'pool_scale': pool_scale, 'ev_w_out': ev_w_out,
            'gdn_w_in': gdn_w_in, 'gdn_conv_w': gdn_conv_w, 'gdn_a_log': gdn_a_log, 'gdn_dt_bias': gdn_dt_bias,
            'gdn_norm_g': gdn_norm_g, 'gdn_w_out': gdn_w_out, 'final_norm_g': final_norm_g}


def reference(x, c, ctx, c_ctx, w_mod, b_mod, norm_g, ffn1_w13, ffn1_w2, ffn2_w13, ffn2_w2,
              ev_w_in, s5_lambda_re, s5_lambda_im, s5_log_step, s5_b_re, s5_b_im, s5_c_re, s5_c_im,
              s5_d, s5_w_glu, s5_b_glu, pool_w, pool_scale, ev_w_out,
              gdn_w_in, gdn_conv_w, gdn_a_log, gdn_dt_bias, gdn_norm_g, gdn_w_out, final_norm_g):
    B_ = x.shape[0]
    rows = x.shape[1] // GRID_W
    sc = jax.nn.silu(c)
    sc_ctx = jax.nn.silu(c_ctx)
    for i in range(DEPTH):
        last = i == DEPTH - 1
        j = i // 2
        mod_l = (sc @ w_mod[i] + b_mod[i]).reshape(B_, 1, N_MOD, D_MODEL)
        mod_c = (sc_ctx @ w_mod[i] + b_mod[i]).reshape(1, 1, N_MOD, D_MODEL)
        x = _ffn_half(x, norm_g[i, 0], mod_l, 0, ffn1_w13[i], ffn1_w2[i])
        ctx = _ffn_half(ctx, norm_g[i, 0], mod_c, 0, ffn1_w13[i], ffn1_w2[i])
        h_l = _modulate(x, norm_g[i, 1], mod_l[:, :, 3], mod_l[:, :, 4])
        h_c = _modulate(ctx, norm_g[i, 1], mod_c[:, :, 3], mod_c[:, :, 4])
        if i % 2 == 0:
            y_l, y_c = _even_mixer(h_l, h_c, rows, ev_w_in[j], s5_lambda_re[j], s5_lambda_im[j], s5_log_step[j],
                                   s5_b_re[j], s5_b_im[j], s5_c_re[j], s5_c_im[j], s5_d[j], s5_w_glu[j],
                                   s5_b_glu[j], pool_w[j], pool_scale[j], ev_w_out[j], not last)
        else:
            y_l, y_c = _odd_mixer(h_l, h_c, gdn_w_in[j], gdn_conv_w[j], gdn_a_log[j], gdn_dt_bias[j],
                                  gdn_norm_g[j], gdn_w_out[j], not last)
        x = x + mod_l[:, :, 5] * y_l
        x = _ffn_half(x, norm_g[i, 2], mod_l, 6, ffn2_w13[i], ffn2_w2[i])
        if not last:
            ctx = ctx + mod_c[:, :, 5] * y_c
            ctx = _ffn_half(ctx, norm_g[i, 2], mod_c, 6, ffn2_w13[i], ffn2_w2[i])
    return _rmsnorm(x, final_norm_g)
```

```python
import numpy as np
from contextlib import ExitStack, contextmanager
import concourse.bass as bass
import concourse.mybir as mybir
from concourse.bass_utils import run_bass_kernel_spmd

F32 = mybir.dt.float32
BF16 = mybir.dt.bfloat16
I32 = mybir.dt.int32
AF = mybir.ActivationFunctionType
ALU = mybir.AluOpType

NDMASEM = 24
DEBUG_SCRATCH = False
ENG = ("pe", "act", "dve", "pool")


class Prog:
    def __init__(self):
        self.nc = bass.Bass("TRN2", target_bir_lowering=False)
        self.root = ExitStack()
        self.es = self.root
        nc = self.nc
        self.eng = {"pe": nc.tensor, "act": nc.scalar, "dve": nc.vector, "pool": nc.gpsimd, "sp": nc.sync}
        self.sem = {e: self.root.enter_context(nc.semaphore("s_" + e)) for e in ENG}
        self.cnt = {e: 0 for e in ENG}
        self.dsem = [self.root.enter_context(nc.semaphore(f"d{i}")) for i in range(NDMASEM)]
        self.dtot = [0] * NDMASEM
        self.dnext = 0
        self.seen = {e: {} for e in self.eng}
        self.lastw = {}
        self.readers = {}
        self.uid = 0

    def _semobj(self, k):
        return self.sem[k] if isinstance(k, str) else self.dsem[k]

    def _wait(self, e, tok):
        if tok is None:
            return
        k, val, src = tok
        if src == e and e == "pe":
            return
        if self.seen[e].get(k, 0) >= val:
            return
        self.eng[e].wait_ge(self._semobj(k), val)
        self.seen[e][k] = val

    def _deps(self, e, reads, writes, dma=False):
        for k in reads:
            self._wait(e, self.lastw.get(k))
        for k in writes:
            self._wait(e, self.lastw.get(k))
            for t in self.readers.get(k, ()):
                if t[2] == e and not dma:
                    continue
                self._wait(e, t)

    def _commit(self, tok, reads, writes):
        for k in reads:
            self.readers.setdefault(k, []).append(tok)
        for k in writes:
            self.lastw[k] = tok
            self.readers[k] = []

    def op(self, e, fn, reads=(), writes=()):
        self._deps(e, reads, writes)
        ins = fn(self.eng[e])
        self.cnt[e] += 1
        ins.then_inc(self.sem[e], 1)
        self._commit((e, self.cnt[e], e), reads, writes)

    def dma(self, q, out, in_, reads=(), writes=()):
        s = self.dnext
        self.dnext = (self.dnext + 1) % NDMASEM
        if self.dtot[s] > 0:
            self._wait(q, (s, self.dtot[s], None))
        self._deps(q, reads, writes, dma=True)
        ins = self.eng[q].dma_start(out=out, in_=in_)
        self.dtot[s] += 16
        ins.then_inc(self.dsem[s], 16)
        self._commit((s, self.dtot[s], None), reads, writes)

    def barrier(self):
        for e in ("pe", "act", "dve", "pool", "sp"):
            for e2 in ENG:
                if self.cnt[e2]:
                    self._wait(e, (e2, self.cnt[e2], None))
            for s in range(NDMASEM):
                if self.dtot[s]:
                    self._wait(e, (s, self.dtot[s], None))

    @contextmanager
    def scope(self):
        outer = self.es
        self.es = ExitStack()
        try:
            yield
        finally:
            self.barrier()
            self.es.close()
            self.es = outer

    def sb(self, shape, dtype=F32):
        self.uid += 1
        return self.es.enter_context(self.nc.sbuf_tensor(f"sb{self.uid}", list(shape), dtype))

    def ps(self, shape, dtype=F32):
        self.uid += 1
        return self.es.enter_context(self.nc.psum_tensor(f"ps{self.uid}", list(shape), dtype))

    def din(self, name, shape, dtype=F32):
        return self.nc.dram_tensor(name, list(shape), dtype, kind="ExternalInput").ap()

    def dout(self, name, shape, dtype=F32):
        return self.nc.dram_tensor(name, list(shape), dtype, kind="ExternalOutput").ap()

    def dscr(self, name, shape, dtype=F32):
        kind = "ExternalOutput" if DEBUG_SCRATCH else "Internal"
        return self.nc.dram_tensor(name, list(shape), dtype, kind=kind).ap()

    def finish(self):
        self.barrier()
        self.root.close()
        return self.nc


D = 1024
NFF = 22
TW = 256
T_C = 256


def vec_cols(v):
    v = np.asarray(v, np.float32).reshape(-1, 128)
    return np.ascontiguousarray(v.T)


def common_setup(P):
    C = {}
    C["ones"] = P.sb([128, 128])
    P.op("dve", lambda e: e.memset(C["ones"][:], 1.0), writes=["ones"])
    C["sq"] = P.sb([128, 8, TW])
    C["rstd"] = P.sb([128, TW])
    C["lnt"] = P.sb([128, TW])
    C["tmp"] = [P.sb([128, TW]) for _ in range(2)]
    C["h"] = P.sb([128, 8, TW], BF16)
    C["ss_ps"] = P.ps([128, TW])
    C["pa"] = [P.ps([128, 512]) for _ in range(2)]
    C["pb"] = [P.ps([128, TW]) for _ in range(2)]
    C["py"] = [P.ps([128, TW]) for _ in range(2)]
    C["n"] = 0
    return C


def ffn_setup(P, C, wd13, wd2, tag):
    S = {"tag": tag}
    S["w13"] = P.sb([128, 8, 5632], BF16)
    S["w2"] = P.sb([128, NFF, 1024], BF16)
    C["hid"] = P.sb([128, NFF, TW], BF16)
    C["sa"] = [P.sb([128, TW]) for _ in range(2)]
    for kc in range(8):
        for q in range(4):
            P.dma("pool", S["w13"][:, kc, q * 1408:(q + 1) * 1408], wd13[q][kc * 128:(kc + 1) * 128, :], writes=[("w13", kc, q)])
    for j in range(NFF):
        P.dma("pool", S["w2"][:, j, :], wd2[j // 11][(j % 11) * 128:(j % 11 + 1) * 128, :], writes=[("w2", j)])
    return S


def rms_rstd(P, C, xT, xkey):
    P.op("act", lambda e: e.activation(out=C["sq"][:], in_=xT[:], func=AF.Square), reads=[xkey], writes=["sq"])
    for c in range(8):
        P.op("pe", lambda e: e.matmul(C["ss_ps"][:], C["ones"][:], C["sq"][:, c, :], start=(c == 0), stop=(c == 7)),
             reads=["sq", "ones"], writes=["ss_ps"])
    P.op("act", lambda e: e.activation(out=C["lnt"][:], in_=C["ss_ps"][:], func=AF.Ln, bias=1e-6, scale=1.0 / 1024),
         reads=["ss_ps"], writes=["lnt"])
    P.op("act", lambda e: e.activation(out=C["rstd"][:], in_=C["lnt"][:], func=AF.Exp, scale=-0.5), reads=["lnt"], writes=["rstd"])


def modulate(P, C, xT, xkey, A, B, vkey, hout, hkey):
    rms_rstd(P, C, xT, xkey)
    for c in range(8):
        t = C["tmp"][c % 2]
        tk = ("tmp", c % 2)
        P.op("dve", lambda e: e.scalar_tensor_tensor(out=t[:], in0=xT[:, c, :], scalar=A[:, c:c + 1], in1=C["rstd"][:], op0=ALU.mult, op1=ALU.mult),
             reads=[xkey, "rstd", vkey], writes=[tk])
        P.op("act", lambda e: e.activation(out=hout[:, c, :], in_=t[:], func=AF.Identity, bias=B[:, c:c + 1], scale=1.0),
             reads=[tk, vkey], writes=[(hkey, c)])


def ffn_core(P, C, S, xT, xkey, A, B, G, vkey):
    modulate(P, C, xT, xkey, A, B, vkey, C["h"], "h")
    for j in range(NFF):
        n = C["n"]
        C["n"] += 1
        pa, pb, sa = C["pa"][n % 2], C["pb"][n % 2], C["sa"][n % 2]
        for c in range(8):
            P.op("pe", lambda e: e.matmul(pa[:, :TW], S["w13"][:, c, j * 128:(j + 1) * 128], C["h"][:, c, :], start=(c == 0), stop=(c == 7)),
                 reads=[("h", c), ("w13", c, (j * 128) // 1408), ("w13", c, (j * 128 + 127) // 1408)], writes=[("pa", n % 2)])
        for c in range(8):
            o = 2816 + j * 128
            P.op("pe", lambda e: e.matmul(pb[:], S["w13"][:, c, o:o + 128], C["h"][:, c, :], start=(c == 0), stop=(c == 7)),
                 reads=[("h", c), ("w13", c, o // 1408), ("w13", c, (o + 127) // 1408)], writes=[("pb", n % 2)])
        P.op("act", lambda e: e.activation(out=sa[:], in_=pa[:, :TW], func=AF.Silu), reads=[("pa", n % 2)], writes=[("sa", n % 2)])
        P.op("dve", lambda e: e.tensor_tensor(out=C["hid"][:, j, :], in0=sa[:], in1=pb[:], op=ALU.mult),
             reads=[("sa", n % 2), ("pb", n % 2)], writes=[("hid", j)])
    for oc in range(8):
        n = C["n"]
        C["n"] += 1
        py = C["py"][n % 2]
        for j in range(NFF):
            P.op("pe", lambda e: e.matmul(py[:], S["w2"][:, j, oc * 128:(oc + 1) * 128], C["hid"][:, j, :], start=(j == 0), stop=(j == NFF - 1)),
                 reads=[("hid", j), ("w2", j)], writes=[("py", n % 2)])
        P.op("dve", lambda e: e.scalar_tensor_tensor(out=xT[:, oc, :], in0=py[:], scalar=G[:, oc:oc + 1], in1=xT[:, oc, :], op0=ALU.mult, op1=ALU.add),
             reads=[("py", n % 2), vkey, xkey], writes=[xkey])


V_NG = 0
V_FNG = 48
V_BM = 56
V_BGLU = 200
V_PSC = 204
V_CONV = 208
V_CT = 376
NV = 392


def phase_mods(P, G):
    VEC, MOD = G["VEC"], G["MOD"]
    with P.scope():
        wsb = [P.sb([128, 8, 1024]) for _ in range(2)]
        s_sb = P.sb([128, 8, 2])
        pss = [P.ps([128, 8, 2]) for _ in range(2)]
        cview = VEC[:, V_CT:V_CT + 16].rearrange("p (a b) -> p a b", b=2)
        P.op("act", lambda e: e.activation(out=s_sb[:], in_=cview, func=AF.Silu), reads=["vec"], writes=["s_sb"])
        for l in range(2):
            for k in range(9):
                i = l * 9 + k
                w = wsb[i % 2]
                for kc in range(8):
                    P.dma("sp", w[:, kc, :], G["wm"][l][k][kc * 128:(kc + 1) * 128, :], writes=[("wm", i % 2, kc)])
                ps = pss[i % 2]
                for fc in range(8):
                    for kc in range(8):
                        P.op("pe", lambda e: e.matmul(ps[:, fc, :], w[:, kc, fc * 128:(fc + 1) * 128], s_sb[:, kc, :], start=(kc == 0), stop=(kc == 7)),
                             reads=["s_sb", ("wm", i % 2, kc)], writes=[("mps", i % 2)])
                    col = V_BM + i * 8 + fc
                    P.op("dve", lambda e: e.tensor_scalar(MOD[:, l, k, fc, :], ps[:, fc, :], VEC[:, col:col + 1], None, ALU.add),
                         reads=[("mps", i % 2), "vec"], writes=["mod"])


def derive(P, G, l, kn, k0):
    VEC, MOD, DER = G["VEC"], G["MOD"], G["DER"]
    out = []
    for m in range(2):
        base = G["dern"]
        G["dern"] += 16
        a = DER[:, base:base + 8]
        g = DER[:, base + 8:base + 16]
        ng = VEC[:, V_NG + (l * 3 + kn) * 8:V_NG + (l * 3 + kn) * 8 + 8]
        P.op("dve", lambda e: e.scalar_tensor_tensor(out=a, in0=MOD[:, l, k0 + 1, :, m], scalar=1.0, in1=ng, op0=ALU.add, op1=ALU.mult),
             reads=["mod", "vec"], writes=["der"])
        P.op("dve", lambda e: e.tensor_scalar(g, MOD[:, l, k0 + 2, :, m], 0.5, None, ALU.mult), reads=["mod"], writes=["der"])
        out.append((a, MOD[:, l, k0, :, m], g))
    return out


def tiles_of(T_L):
    return [(0, 1, -1)] + [(T_C + i * TW, 0, i) for i in range(T_L // TW)]


def load_x_tile(P, G, x, xk, s0, m, i):
    if m == 1:
        P.dma("sp", x[:], G["ctxT"].rearrange("(c p) t -> p c t", p=128), writes=[xk])
    else:
        for c in range(8):
            P.dma("sp", x[:, c, :], G["xT"][c][:, i * TW:(i + 1) * TW], writes=[xk])


def fm_tile(ap, s0):
    return ap[:, s0:s0 + TW].rearrange("(c p) t -> p c t", p=128)


def ffn_phase(P, G, fi, l, kn, k0, T_L, first=False, lat_only=False, final=False):
    with P.scope():
        C = common_setup(P)
        S = ffn_setup(P, C, G["w13"][fi], G["w2"][fi], "f")
        f1 = derive(P, G, l, kn, k0)
        x = P.sb([128, 8, TW])
        xk = "xtile"
        yo = P.sb([128, 8, TW]) if final else None
        for (s0, m, i) in tiles_of(T_L):
            if lat_only and m == 1:
                continue
            if first:
                load_x_tile(P, G, x, xk, s0, m, i)
            else:
                P.dma("sp", x[:], fm_tile(G["xres"], s0), reads=[("xres", s0)], writes=[xk])
            A, B, Gt = f1[m]
            ffn_core(P, C, S, x, xk, A, B, Gt, "der")
            if not final:
                P.dma("sp", fm_tile(G["xres"], s0), x[:], reads=[xk], writes=[("xres", s0)])
            else:
                rms_rstd(P, C, x, xk)
                for c in range(8):
                    col = V_FNG + c
                    P.op("dve", lambda e: e.scalar_tensor_tensor(out=yo[:, c, :], in0=x[:, c, :], scalar=G["VEC"][:, col:col + 1], in1=C["rstd"][:], op0=ALU.mult, op1=ALU.mult),
                         reads=[xk, "rstd", "vec"], writes=[("yo", c)])
                    P.dma("sp", G["outT"][c][:, i * TW:(i + 1) * TW], yo[:, c, :], reads=[("yo", c)], writes=[("out", c, i)])


def phase_inproj0(P, G, T_L):
    with P.scope():
        C = common_setup(P)
        win = P.sb([128, 8, 1024], BF16)
        for kc in range(8):
            P.dma("pool", win[:, kc, :], G["ev_w_in"][kc * 128:(kc + 1) * 128, :], writes=[("win", kc)])
        mx = derive(P, G, 0, 1, 3)
        x = P.sb([128, 8, TW])
        xk = "xtile"
        h2 = P.sb([128, 8, TW], BF16)
        pb16 = [P.sb([128, TW], BF16) for _ in range(2)]
        ptm = [P.sb([128, 512]) for _ in range(2)]
        for (s0, m, i) in tiles_of(T_L):
            P.dma("sp", x[:], fm_tile(G["xres"], s0), reads=[("xres", s0)], writes=[xk])
            A, B, _ = mx[m]
            modulate(P, C, x, xk, A, B, "der", h2, "h2")
            for oc in range(4, 8):
                n = C["n"]
                C["n"] += 1
                ps_, po = C["pb"][n % 2], pb16[n % 2]
                for c in range(8):
                    P.op("pe", lambda e: e.matmul(ps_[:], win[:, c, oc * 128:(oc + 1) * 128], h2[:, c, :], start=(c == 0), stop=(c == 7)),
                         reads=[("h2", c), ("win", c)], writes=[("pb", n % 2)])
                P.op("act", lambda e: e.activation(out=po[:], in_=ps_[:], func=AF.Identity), reads=[("pb", n % 2)], writes=[("pb16", n % 2)])
                P.dma("sp", G["ppool"][(oc - 4) * 128:(oc - 3) * 128, s0:s0 + TW], po[:], reads=[("pb16", n % 2)], writes=[("ppool", s0, oc)])
            for hf in range(2):
                n = C["n"]
                C["n"] += 1
                ps_, po = C["pa"][n % 2], ptm[n % 2]
                for c in range(8):
                    P.op("pe", lambda e: e.matmul(ps_[:], h2[:, c, hf * 128:(hf + 1) * 128], win[:, c, 0:512], start=(c == 0), stop=(c == 7)),
                         reads=[("h2", c), ("win", c)], writes=[("pa", n % 2)])
                P.op("dve", lambda e: e.tensor_copy(po[:], ps_[:]), reads=[("pa", n % 2)], writes=[("ptm", n % 2)])
                P.dma("sp", G["p_tm"][s0 + hf * 128:s0 + (hf + 1) * 128, :], po[:], reads=[("ptm", n % 2)], writes=[("p_tm", s0, hf)])


TWO_PI = 6.283185307179586


def s5_cpow(P, T, K, a, th, N):
    t1, t2, u, rf, mag, Lre, Lim = T["t1"], T["t2"], T["u"], T["rf"], T["mag"], T["Lre"], T["Lim"]
    ti = T["ti"]
    P.op("dve", lambda e: e.tensor_tensor(out=t1[:, :N], in0=a[:, :N], in1=K[:, :N], op=ALU.mult), reads=["s5a", "s5k"], writes=["t1"])
    P.op("act", lambda e: e.activation(out=mag[:, :N], in_=t1[:, :N], func=AF.Exp), reads=["t1"], writes=["mag"])
    P.op("dve", lambda e: e.tensor_tensor(out=t2[:, :N], in0=th[:, :N], in1=K[:, :N], op=ALU.mult), reads=["s5a", "s5k"], writes=["t2"])
    for (off, dst, key) in ((0.0, Lim, "Lim"), (0.25, Lre, "Lre")):
        P.op("dve", lambda e: e.tensor_scalar(u[:, :N], t2[:, :N], 1.0 / TWO_PI, off, ALU.mult, ALU.add), reads=["t2"], writes=["u"])
        P.op("dve", lambda e: e.tensor_copy(ti[:, :N], u[:, :N]), reads=["u"], writes=["ti"])
        P.op("dve", lambda e: e.tensor_copy(rf[:, :N], ti[:, :N]), reads=["ti"], writes=["rf"])
        P.op("dve", lambda e: e.tensor_tensor(out=u[:, :N], in0=u[:, :N], in1=rf[:, :N], op=ALU.subtract), reads=["u", "rf"], writes=["u"])
        P.op("act", lambda e: e.activation(out=dst[:, :N], in_=u[:, :N], func=AF.Sin, scale=TWO_PI), reads=["u"], writes=[key])
        P.op("dve", lambda e: e.tensor_tensor(out=dst[:, :N], in0=dst[:, :N], in1=mag[:, :N], op=ALU.mult), reads=[key, "mag"], writes=[key])


def cmul(P, ore, oim, are, aim, bre, bim, tmp, rk, wk):
    P.op("dve", lambda e: e.tensor_tensor(out=ore, in0=are, in1=bre, op=ALU.mult), reads=rk, writes=wk)
    P.op("dve", lambda e: e.tensor_tensor(out=tmp, in0=aim, in1=bim, op=ALU.mult), reads=rk, writes=["cm_tmp"])
    P.op("dve", lambda e: e.tensor_tensor(out=ore, in0=ore, in1=tmp, op=ALU.subtract), reads=wk + ["cm_tmp"], writes=wk)
    P.op("dve", lambda e: e.tensor_tensor(out=oim, in0=are, in1=bim, op=ALU.mult), reads=rk, writes=wk)
    P.op("dve", lambda e: e.tensor_tensor(out=tmp, in0=aim, in1=bre, op=ALU.mult), reads=rk, writes=["cm_tmp"])
    P.op("dve", lambda e: e.tensor_tensor(out=oim, in0=oim, in1=tmp, op=ALU.add), reads=wk + ["cm_tmp"], writes=wk)


def s5_precompute(P, G, W):
    N = 1024
    with P.scope():
        names = ["lre", "lim", "ls", "K", "cre", "cim", "bre", "bim", "a", "th", "t1", "t2", "u", "rf", "mag", "Lre", "Lim",
                 "fre", "fim", "x1", "x2", "x3", "x4", "cm"]
        T = {n: P.sb([128, N]) for n in names}
        T["ti"] = P.sb([128, N], I32)
        idn = G["IDN_F"]
        pp = [P.ps([128, 128]) for _ in range(4)]
        npp = [0]
        mre, mim = G["S5C"][:, 0:1], G["S5C"][:, 1:2]
        for d in range(2):
            for hf in range(4):
                sl = slice(hf * N, (hf + 1) * N)
                for nm, src in (("lre", G["s5_lre"][d]), ("lim", G["s5_lim"][d]), ("ls", G["s5_ls"][d]), ("cre", G["s5_cre"]), ("cim", G["s5_cim"]),
                                ("bre", G["s5_bre"]), ("bim", G["s5_bim"])):
                    P.dma("sp", T[nm][:], src[:, sl], writes=["s5in_" + nm])
                ink = ["s5in_" + nm for nm in ("lre", "lim", "ls", "cre", "cim", "bre", "bim")]
                P.op("dve", lambda e: e.tensor_scalar(T["lre"][:], T["lre"][:], -1e-4, None, ALU.min), reads=ink, writes=["s5in_lre"])
                P.op("act", lambda e: e.activation(out=T["ls"][:], in_=T["ls"][:], func=AF.Exp), reads=ink, writes=["s5in_ls"])
                P.op("dve", lambda e: e.tensor_tensor(out=T["a"][:], in0=T["lre"][:], in1=T["ls"][:], op=ALU.mult), reads=["s5in_lre", "s5in_ls"], writes=["s5a"])
                P.op("dve", lambda e: e.tensor_tensor(out=T["th"][:], in0=T["lim"][:], in1=T["ls"][:], op=ALU.mult), reads=["s5in_lim", "s5in_ls", "s5a"], writes=["s5a"])
                av = T["a"][:].rearrange("p (g r) -> p g r", r=128)[:, :, 0]
                tv = T["th"][:].rearrange("p (g r) -> p g r", r=128)[:, :, 0]
                gs = slice(hf * 8, hf * 8 + 8)
                P.op("act", lambda e: e.activation(out=W["R8"][d][:, gs], in_=av, func=AF.Exp, scale=8.0), reads=["s5a"], writes=["R8"])
                P.op("dve", lambda e: e.tensor_scalar(T["u"][:, :8], tv, 8.0 / TWO_PI, None, ALU.mult), reads=["s5a"], writes=["u"])
                P.op("dve", lambda e: e.tensor_copy(T["ti"][:, :8], T["u"][:, :8]), reads=["u"], writes=["ti"])
                P.op("dve", lambda e: e.tensor_copy(T["rf"][:, :8], T["ti"][:, :8]), reads=["ti"], writes=["rf"])
                P.op("dve", lambda e: e.tensor_tensor(out=W["PSI"][d][:, gs], in0=T["u"][:, :8], in1=T["rf"][:, :8], op=ALU.subtract), reads=["u", "rf"], writes=["PSI"])
                P.op("dve", lambda e: e.memset(T["K"][:], 1.0), reads=["s5k"], writes=["s5k"])
                s5_cpow(P, T, T["K"], T["a"], T["th"], N)
                P.op("dve", lambda e: e.tensor_scalar(T["Lre"][:], T["Lre"][:], -1.0, None, ALU.add), reads=["Lre"], writes=["Lre"])
                P.op("dve", lambda e: e.tensor_tensor(out=T["x1"][:], in0=T["lre"][:], in1=T["lre"][:], op=ALU.mult), reads=["s5in_lre"], writes=["x1"])
                P.op("dve", lambda e: e.tensor_tensor(out=T["x2"][:], in0=T["lim"][:], in1=T["lim"][:], op=ALU.mult), reads=["s5in_lim"], writes=["x2"])
                P.op("dve", lambda e: e.tensor_tensor(out=T["x1"][:], in0=T["x1"][:], in1=T["x2"][:], op=ALU.add), reads=["x1", "x2"], writes=["x1"])
                P.op("dve", lambda e: e.reciprocal(T["x1"][:], T["x1"][:]), reads=["x1"], writes=["x1"])
                P.op("dve", lambda e: e.tensor_scalar(T["x2"][:], T["lim"][:], -1.0, None, ALU.mult), reads=["s5in_lim", "x2"], writes=["x2"])
                cmul(P, T["fre"][:], T["fim"][:], T["Lre"][:], T["Lim"][:], T["lre"][:], T["x2"][:], T["cm"][:], ["Lre", "Lim", "s5in_lre", "x2"], ["f"])
                P.op("dve", lambda e: e.tensor_tensor(out=T["fre"][:], in0=T["fre"][:], in1=T["x1"][:], op=ALU.mult), reads=["f", "x1"], writes=["f"])
                P.op("dve", lambda e: e.tensor_tensor(out=T["fim"][:], in0=T["fim"][:], in1=T["x1"][:], op=ALU.mult), reads=["f", "x1"], writes=["f"])
                P.dma("sp", T["K"][:], G["s5_ke"][d][:, sl], reads=["s5k"], writes=["s5k"])
                s5_cpow(P, T, T["K"], T["a"], T["th"], N)
                cmul(P, T["x1"][:], T["x2"][:], T["cre"][:], T["cim"][:], T["Lre"][:], T["Lim"][:], T["cm"][:], ["s5in_cre", "s5in_cim", "Lre", "Lim"], ["x1", "x2"])
                P.op("dve", lambda e: e.tensor_scalar(T["x2"][:], T["x2"][:], mim, None, ALU.mult), reads=["x2", "s5c"], writes=["x2"])
                P.op("dve", lambda e: e.scalar_tensor_tensor(out=T["x3"][:], in0=T["x1"][:], scalar=mre, in1=T["x2"][:], op0=ALU.mult, op1=ALU.subtract),
                     reads=["x1", "x2", "s5c"], writes=["x3"])
                Ev = W["E"][d][:, hf * 8:(hf + 1) * 8, :].rearrange("p g r -> p (g r)")
                P.op("act", lambda e: e.activation(out=Ev, in_=T["x3"][:], func=AF.Identity), reads=["x3"], writes=["WE"])
                for which in ("A", "G"):
                    P.dma("sp", T["K"][:], (G["s5_ka"] if which == "A" else G["s5_kg"])[d][:, sl], reads=["s5k"], writes=["s5k"])
                    s5_cpow(P, T, T["K"], T["a"], T["th"], N)
                    cmul(P, T["x1"][:], T["x2"][:], T["Lre"][:], T["Lim"][:], T["fre"][:], T["fim"][:], T["cm"][:], ["Lre", "Lim", "f"], ["x1", "x2"])
                    cmul(P, T["Lre"][:], T["Lim"][:], T["x1"][:], T["x2"][:], T["bre"][:], T["bim"][:], T["cm"][:], ["x1", "x2", "s5in_bre", "s5in_bim"], ["Lre", "Lim"])
                    if which == "A":
                        P.op("dve", lambda e: e.tensor_scalar(T["x1"][:], T["Lim"][:], mim, None, ALU.mult), reads=["Lim", "s5c"], writes=["x1"])
                        P.op("dve", lambda e: e.scalar_tensor_tensor(out=T["x4"][:], in0=T["Lre"][:], scalar=mre, in1=T["x1"][:], op0=ALU.mult, op1=ALU.add),
                             reads=["Lre", "x1", "s5c"], writes=["x4"])
                        for gg in range(8):
                            g = hf * 8 + gg
                            ps = pp[npp[0] % 4]
                            pk = ("s5pp", npp[0] % 4)
                            npp[0] += 1
                            P.op("pe", lambda e: e.matmul(ps[:], T["x4"][:, gg * 128:(gg + 1) * 128], T["x3"][:, gg * 128:(gg + 1) * 128], start=True, stop=True),
                                 reads=["x4", "x3"], writes=[pk])
                            if d == 0:
                                P.op("dve", lambda e: e.tensor_tensor(out=T["cm"][:, :128], in0=ps[:], in1=G["S5MASK"][d][:], op=ALU.mult), reads=[pk, "s5c"], writes=["cm_tmp"])
                                P.op("dve", lambda e: e.scalar_tensor_tensor(out=W["KIN"][d][:, g, :], in0=G["IDN_F"][:], scalar=G["S5D"][:, g:g + 1], in1=T["cm"][:, :128],
                                                                             op0=ALU.mult, op1=ALU.add), reads=["cm_tmp", "s5c"], writes=["WK"])
                            else:
                                P.op("dve", lambda e: e.tensor_tensor(out=W["KIN"][d][:, g, :], in0=ps[:], in1=G["S5MASK"][d][:], op=ALU.mult), reads=[pk, "s5c"], writes=["WK"])
                    else:
                        P.op("dve", lambda e: e.tensor_scalar(T["x1"][:], T["Lim"][:], mim, None, ALU.mult), reads=["Lim", "s5c"], writes=["x1"])
                        P.op("dve", lambda e: e.scalar_tensor_tensor(out=T["x4"][:], in0=T["Lre"][:], scalar=mre, in1=T["x1"][:], op0=ALU.mult, op1=ALU.add),
                             reads=["Lre", "x1", "s5c"], writes=["x4"])
                        P.op("dve", lambda e: e.tensor_scalar(T["x1"][:], T["Lre"][:], mim, None, ALU.mult), reads=["Lre", "s5c"], writes=["x1"])
                        P.op("dve", lambda e: e.scalar_tensor_tensor(out=T["x3"][:], in0=T["Lim"][:], scalar=mre, in1=T["x1"][:], op0=ALU.mult, op1=ALU.subtract),
                             reads=["Lim", "x1", "s5c"], writes=["x3"])
                        for (srcT, dstW, rk) in ((T["x4"], W["GA"][d], "x4"), (T["x3"], W["GB"][d], "x3")):
                            for gg in range(8):
                                g = hf * 8 + gg
                                ps = pp[npp[0] % 4]
                                pk = ("s5pp", npp[0] % 4)
                                npp[0] += 1
                                P.op("pe", lambda e: e.matmul(ps[:], srcT[:, gg * 128:(gg + 1) * 128], idn[:], start=True, stop=True), reads=[rk, "s5c"], writes=[pk])
                                P.op("act", lambda e: e.activation(out=dstW[:, g, :], in_=ps[:], func=AF.Identity), reads=[pk], writes=["WG"])


def phase_s5(P, G, T_L):
    NBL = T_L // 1024
    NL = T_L // 8
    NCH = NL + 32
    NBLK = NBL + 1
    segs = [(c0, min(512, NL - c0)) for c0 in range(0, NL, 512)] + [(NL, 32)]
    with P.scope():
        load_cst(P, G)
        W = {k: [P.sb([128, 32, 128], BF16) for _ in range(2)] for k in ("GA", "GB", "E", "KIN")}
        W["R8"] = [P.sb([128, 32]) for _ in range(2)]
        W["PSI"] = [P.sb([128, 32]) for _ in range(2)]
        s5_precompute(P, G, W)
        wk = ["WE", "WK", "WG", "R8", "PSI"]
        idb, jb, j32b = G["IDN_B"], G["JREV_B"], G["J32_B"]
        UT = P.sb([128, NBLK, 8, 128], BF16)
        YT = P.sb([128, NBLK, 8, 128])
        Usb = [P.sb([128, NCH], BF16) for _ in range(2)]
        QV = P.sb([128, NCH])
        P.dma("sp", QV[:], G["s5_qv"][:, :NCH], writes=["qv"])
        cosT, sinT, inA, inB, zA, zB = (P.sb([128, NCH]) for _ in range(6))
        tt = [P.sb([128, NCH]) for _ in range(4)]
        ti = P.sb([128, NCH], I32)
        R8t = P.sb([128, NCH])
        onesN = P.sb([128, NCH])
        P.op("pool", lambda e: e.memset(onesN[:], 1.0), writes=["onesN"])
        Xin = P.sb([128, NCH], BF16)
        Ysb = P.sb([128, NCH])
        T1 = [P.sb([128, 128]) for _ in range(2)]
        PU = [P.ps([128, 512]) for _ in range(2)]
        PV = [P.ps([128, 512]) for _ in range(4)]
        PY = [P.ps([128, 512]) for _ in range(2)]
        cnt = {"u": 0, "v": 0, "y": 0, "t1": 0}

        def blk_info(b):
            return (0, 32) if b == NBL else (T_C + b * 1024, 128)

        def blk_col(b, d):
            if b == NBL:
                return NL
            return (b if d == 0 else NBL - 1 - b) * 128

        for cc in range(4):
            for b in range(NBLK):
                t0, nb = blk_info(b)
                for gl in range(8):
                    src = G["p_tm"][t0:t0 + 8 * nb, cc * 128 + gl * 16:cc * 128 + gl * 16 + 16].rearrange("(n s) j -> n s j", s=8)
                    dst = UT[:nb, b, gl, :].rearrange("n (s j) -> n s j", j=16)
                    P.dma("pool", dst, src, reads=[("p_tm", "all")], writes=[("UT", b, gl)])
            for gl in range(8):
                g = cc * 8 + gl
                for d in range(2):
                    U = Usb[d]
                    uk = ("Usb", d)
                    for (c0, w) in segs:
                        pu = PU[cnt["u"] % 2]
                        puk = ("PU", cnt["u"] % 2)
                        cnt["u"] += 1
                        blks = [b for b in range(NBLK) if c0 <= blk_col(b, d) < c0 + w]
                        for b in blks:
                            t0, nb = blk_info(b)
                            col = blk_col(b, d) - c0
                            rhs = (idb[:nb, :nb] if d == 0 else (j32b[:, :] if nb == 32 else jb[:, :]))
                            P.op("pe", lambda e: e.matmul(pu[:, col:col + nb], UT[:nb, b, gl, :], rhs, start=True, stop=True),
                                 reads=[("UT", b, gl), "s5c"], writes=[puk])
                        P.op("act", lambda e: e.activation(out=U[:, c0:c0 + w], in_=pu[:, :w], func=AF.Identity), reads=[puk], writes=[uk])
                    psi = W["PSI"][d][:, g:g + 1]
                    P.op("dve", lambda e: e.tensor_scalar(tt[2][:], QV[:], psi, None, ALU.mult), reads=["qv"] + wk, writes=["tt2"])
                    for (off, dst, key) in ((0.0, sinT, "sinT"), (0.25, cosT, "cosT")):
                        P.op("dve", lambda e: e.tensor_scalar(tt[0][:], tt[2][:], off, None, ALU.add), reads=["tt2"], writes=["tt0"])
                        P.op("dve", lambda e: e.tensor_copy(ti[:], tt[0][:]), reads=["tt0"], writes=["ti5"])
                        P.op("dve", lambda e: e.tensor_copy(tt[1][:], ti[:]), reads=["ti5"], writes=["tt1"])
                        P.op("pool", lambda e: e.tensor_tensor(out=tt[0][:], in0=tt[0][:], in1=tt[1][:], op=ALU.subtract), reads=["tt0", "tt1"], writes=["tt0"])
                        P.op("act", lambda e: e.activation(out=dst[:], in_=tt[0][:], func=AF.Sin, scale=TWO_PI), reads=["tt0"], writes=[key])
                    P.op("pool", lambda e: e.tensor_scalar(R8t[:], onesN[:], W["R8"][d][:, g:g + 1], None, ALU.mult), reads=["onesN"] + wk, writes=["R8t"])
                    for (c0, w) in segs:
                        i = cnt["v"] % 2
                        cnt["v"] += 1
                        pva, pvb = PV[2 * i], PV[2 * i + 1]
                        ka, kb = ("PV", 2 * i), ("PV", 2 * i + 1)
                        P.op("pe", lambda e: e.matmul(pva[:, :w], W["GA"][d][:, g, :], U[:, c0:c0 + w], start=True, stop=True), reads=[uk] + wk, writes=[ka])
                        P.op("pe", lambda e: e.matmul(pvb[:, :w], W["GB"][d][:, g, :], U[:, c0:c0 + w], start=True, stop=True), reads=[uk] + wk, writes=[kb])
                        s = slice(c0, c0 + w)
                        P.op("dve", lambda e: e.tensor_tensor(out=tt[0][:, s], in0=pva[:, :w], in1=cosT[:, s], op=ALU.mult), reads=[ka, "cosT"], writes=["tt0"])
                        P.op("dve", lambda e: e.tensor_tensor(out=tt[1][:, s], in0=pvb[:, :w], in1=sinT[:, s], op=ALU.mult), reads=[kb, "sinT"], writes=["tt1"])
                        P.op("dve", lambda e: e.tensor_tensor(out=tt[2][:, s], in0=pvb[:, :w], in1=cosT[:, s], op=ALU.mult), reads=[kb, "cosT"], writes=["tt2"])
                        P.op("dve", lambda e: e.tensor_tensor(out=tt[3][:, s], in0=pva[:, :w], in1=sinT[:, s], op=ALU.mult), reads=[ka, "sinT"], writes=["tt3"])
                    P.op("pool", lambda e: e.tensor_tensor(out=inA[:], in0=tt[0][:], in1=tt[1][:], op=ALU.add), reads=["tt0", "tt1"], writes=["inA"])
                    P.op("pool", lambda e: e.tensor_tensor(out=inB[:], in0=tt[2][:], in1=tt[3][:], op=ALU.subtract), reads=["tt2", "tt3"], writes=["inB"])
                    for (src, z, key) in ((inA, zA, "zA"), (inB, zB, "zB")):
                        P.op("dve", lambda e: e.tensor_tensor_scan(out=z[:, NL:NCH], data0=R8t[:, NL:NCH], data1=src[:, NL:NCH], initial=0.0, op0=ALU.mult, op1=ALU.add),
                             reads=["R8t", "inA", "inB"], writes=[key])
                        P.op("dve", lambda e: e.tensor_tensor_scan(out=z[:, 0:NL], data0=R8t[:, 0:NL], data1=src[:, 0:NL], initial=z[:, NCH - 1:NCH], op0=ALU.mult, op1=ALU.add),
                             reads=["R8t", "inA", "inB", key], writes=[key])
                    P.op("pool", lambda e: e.tensor_tensor(out=tt[0][:], in0=zA[:], in1=cosT[:], op=ALU.mult), reads=["zA", "cosT"], writes=["tt0"])
                    P.op("pool", lambda e: e.tensor_tensor(out=tt[1][:], in0=zB[:], in1=sinT[:], op=ALU.mult), reads=["zB", "sinT"], writes=["tt1"])
                    P.op("dve", lambda e: e.memset(Xin[:, NL:NL + 1], 0.0), reads=[], writes=["Xin"])
                    P.op("dve", lambda e: e.tensor_tensor(out=Xin[:, NL + 1:NCH], in0=tt[0][:, NL:NCH - 1], in1=tt[1][:, NL:NCH - 1], op=ALU.subtract), reads=["tt0", "tt1"], writes=["Xin"])
                    P.op("dve", lambda e: e.tensor_tensor(out=Xin[:, 1:NL], in0=tt[0][:, 0:NL - 1], in1=tt[1][:, 0:NL - 1], op=ALU.subtract), reads=["tt0", "tt1"], writes=["Xin"])
                    P.op("dve", lambda e: e.tensor_tensor(out=Xin[:, 0:1], in0=tt[0][:, NCH - 1:NCH], in1=tt[1][:, NCH - 1:NCH], op=ALU.subtract), reads=["tt0", "tt1"], writes=["Xin"])
                    for (c0, w) in segs:
                        py = PY[cnt["y"] % 2]
                        pyk = ("PY", cnt["y"] % 2)
                        cnt["y"] += 1
                        P.op("pe", lambda e: e.matmul(py[:, :w], W["E"][d][:, g, :], Xin[:, c0:c0 + w], start=True, stop=False), reads=["Xin"] + wk, writes=[pyk])
                        P.op("pe", lambda e: e.matmul(py[:, :w], W["KIN"][d][:, g, :], U[:, c0:c0 + w], start=False, stop=True), reads=[uk] + wk, writes=[pyk])
                        P.op("act", lambda e: e.activation(out=Ysb[:, c0:c0 + w], in_=py[:, :w], func=AF.Identity), reads=[pyk], writes=["Ysb"])
                    for b in range(NBLK):
                        t0, nb = blk_info(b)
                        col = blk_col(b, d)
                        pu = PU[cnt["u"] % 2]
                        puk = ("PU", cnt["u"] % 2)
                        cnt["u"] += 1
                        P.op("pe", lambda e: e.matmul(pu[:nb, :128], Ysb[:, col:col + nb], G["IDN_F"][:], start=True, stop=True), reads=["Ysb", "s5c"], writes=[puk])
                        if d == 0:
                            P.op("dve", lambda e: e.tensor_copy(YT[:nb, b, gl, :], pu[:nb, :128]), reads=[puk], writes=[("YT", b, gl)])
                        else:
                            t1 = T1[cnt["t1"] % 2]
                            t1k = ("T1", cnt["t1"] % 2)
                            cnt["t1"] += 1
                            P.op("act", lambda e: e.activation(out=t1[:nb, :], in_=pu[:nb, :128], func=AF.Identity), reads=[puk], writes=[t1k])
                            pu2 = PU[cnt["u"] % 2]
                            pu2k = ("PU", cnt["u"] % 2)
                            cnt["u"] += 1
                            jf = G["J32_F"] if nb == 32 else G["JREV_F"]
                            P.op("pe", lambda e: e.matmul(pu2[:nb, :128], jf[:, :], t1[:nb, :], start=True, stop=True), reads=[t1k, "s5c"], writes=[pu2k])
                            P.op("dve", lambda e: e.tensor_tensor(out=YT[:nb, b, gl, :], in0=pu2[:nb, :128], in1=YT[:nb, b, gl, :], op=ALU.add),
                                 reads=[pu2k, ("YT", b, gl)], writes=[("YT", b, gl)])
            for b in range(NBLK):
                t0, nb = blk_info(b)
                for gl in range(8):
                    dst = G["y_tm"][t0:t0 + 8 * nb, cc * 128 + gl * 16:cc * 128 + gl * 16 + 16].rearrange("(n s) j -> n s j", s=8)
                    src = YT[:nb, b, gl, :].rearrange("n (s j) -> n s j", j=16)
                    P.dma("sp", dst, src, reads=[("YT", b, gl)], writes=[("y_tm", "all")])


def phase_mixout0(P, G, T_L):
    with P.scope():
        load_cst(P, G)
        C = common_setup(P)
        wout = P.sb([128, 8, 1024], BF16)
        for kc in range(8):
            P.dma("pool", wout[:, kc, :], G["ev_w_out"][kc * 128:(kc + 1) * 128, :], writes=[("wout", kc)])
        wglu = P.sb([128, 4, 512], BF16)
        for kc in range(4):
            P.dma("pool", wglu[:, kc, :], G["s5_w_glu"][kc * 128:(kc + 1) * 128, :], writes=[("wglu", kc)])
        poolw = P.sb([128, 4, 128], BF16)
        for gi in range(4):
            P.dma("pool", poolw[:, gi, :], G["pool_w"][gi * 128:(gi + 1) * 128, :], writes=[("poolw", gi)])
        mwl = P.sb([128, 4, 128], BF16)
        mwc = P.sb([128, 16, 128], BF16)
        P.dma("pool", mwl[:], G["MWL"].rearrange("g t u -> t g u"), writes=["mw"])
        P.dma("pool", mwc[:], G["MWC"].rearrange("g t u -> t g u"), writes=["mw"])
        x = P.sb([128, 8, TW])
        xk = "xtile"
        ytm = P.sb([128, 2, 512])
        w1 = P.sb([128, 2, 512])
        w2_ = P.sb([128, 2, 512])
        gT = P.sb([128, 4, TW], BF16)
        ymix = P.sb([128, 8, TW], BF16)
        sig = [P.sb([128, TW]) for _ in range(2)]
        u = P.sb([128, 4, TW], BF16)
        ztm = [P.sb([128, 128], BF16) for _ in range(2)]
        MOD, VEC = G["MOD"], G["VEC"]
        for (s0, m, i) in tiles_of(T_L):
            P.dma("sp", x[:], fm_tile(G["xres"], s0), reads=[("xres", s0)], writes=[xk])
            P.dma("sp", ytm[:], G["y_tm"][s0:s0 + TW, :].rearrange("(h t) c -> t h c", t=128), reads=[("y_tm", "all")], writes=["ytm"])
            P.dma("sp", u[:], G["ppool"][:, s0:s0 + TW].rearrange("(g p) t -> p g t", p=128), reads=[("ppool", "all")], writes=["u"])
            P.op("pool", lambda e: e.tensor_tensor(out=w1[:], in0=ytm[:], in1=ytm[:], op=ALU.mult), reads=["ytm"], writes=["w1"])
            P.op("dve", lambda e: e.tensor_scalar(w1[:], w1[:], 0.044715, 1.0, ALU.mult, ALU.add), reads=["w1"], writes=["w1"])
            P.op("dve", lambda e: e.tensor_tensor(out=w1[:], in0=w1[:], in1=ytm[:], op=ALU.mult), reads=["w1", "ytm"], writes=["w1"])
            P.op("act", lambda e: e.activation(out=w2_[:], in_=w1[:], func=AF.Sigmoid, scale=1.5957691216), reads=["w1"], writes=["w2_"])
            P.op("dve", lambda e: e.tensor_tensor(out=w2_[:], in0=w2_[:], in1=ytm[:], op=ALU.mult), reads=["w2_", "ytm"], writes=["w2_"])
            for c in range(4):
                n = C["n"]
                C["n"] += 1
                ps_ = C["pb"][n % 2]
                for hf in range(2):
                    P.op("pe", lambda e: e.matmul(ps_[:, hf * 128:(hf + 1) * 128], w2_[:, hf, c * 128:(c + 1) * 128], G["IDN_F"][:], start=True, stop=True),
                         reads=["w2_", "s5c"], writes=[("pb", n % 2)])
                P.op("act", lambda e: e.activation(out=gT[:, c, :], in_=ps_[:], func=AF.Identity), reads=[("pb", n % 2)], writes=[("gT", c)])
            for oc in range(4):
                n = C["n"]
                C["n"] += 1
                ps_, sg = C["pb"][n % 2], sig[n % 2]
                for c in range(4):
                    P.op("pe", lambda e: e.matmul(ps_[:], wglu[:, c, oc * 128:(oc + 1) * 128], gT[:, c, :], start=(c == 0), stop=(c == 3)),
                         reads=[("gT", c), ("wglu", c)], writes=[("pb", n % 2)])
                col = V_BGLU + oc
                P.op("act", lambda e: e.activation(out=sg[:], in_=ps_[:], func=AF.Sigmoid, bias=VEC[:, col:col + 1], scale=1.0), reads=[("pb", n % 2), "vec"], writes=[("sig", n % 2)])
                P.op("dve", lambda e: e.tensor_tensor(out=ymix[:, oc, :], in0=gT[:, oc, :], in1=sg[:], op=ALU.mult), reads=[("gT", oc), ("sig", n % 2)], writes=[("ymix", oc)])
            for gi in range(4):
                n = C["n"]
                C["n"] += 1
                py = C["py"][n % 2]
                for tb in range(2):
                    pz = C["pa"][tb]
                    P.op("pe", lambda e: e.matmul(pz[:, :128], u[:, gi, tb * 128:(tb + 1) * 128], poolw[:, gi, :], start=True, stop=True),
                         reads=["u", ("poolw", gi)], writes=[("pa", tb)])
                    P.op("act", lambda e: e.activation(out=ztm[tb][:], in_=pz[:, :128], func=AF.Identity), reads=[("pa", tb)], writes=[("ztm", tb)])
                if m == 0:
                    for tb in range(2):
                        P.op("pe", lambda e: e.matmul(py[:, tb * 128:(tb + 1) * 128], ztm[tb][:], mwl[:, gi, :], start=True, stop=True),
                             reads=[("ztm", tb), "mw"], writes=[("py", n % 2)])
                else:
                    for mb in range(2):
                        for kb in range(2):
                            P.op("pe", lambda e: e.matmul(py[:, mb * 128:(mb + 1) * 128], ztm[kb][:], mwc[:, gi * 4 + kb * 2 + mb, :], start=(kb == 0), stop=(kb == 1)),
                                 reads=[("ztm", kb), "mw"], writes=[("py", n % 2)])
                col = V_PSC + gi
                P.op("dve", lambda e: e.tensor_scalar(ymix[:, 4 + gi, :], py[:], VEC[:, col:col + 1], None, ALU.mult), reads=[("py", n % 2), "vec"], writes=[("ymix", 4 + gi)])
            for oc in range(8):
                n = C["n"]
                C["n"] += 1
                py = C["py"][n % 2]
                for c in range(8):
                    P.op("pe", lambda e: e.matmul(py[:], wout[:, c, oc * 128:(oc + 1) * 128], ymix[:, c, :], start=(c == 0), stop=(c == 7)),
                         reads=[("ymix", c), ("wout", c)], writes=[("py", n % 2)])
                P.op("dve", lambda e: e.scalar_tensor_tensor(out=x[:, oc, :], in0=py[:], scalar=MOD[:, 0, 5, oc, m:m + 1], in1=x[:, oc, :], op0=ALU.mult, op1=ALU.add),
                     reads=[("py", n % 2), "mod", xk], writes=[xk])
            P.dma("sp", fm_tile(G["xres"], s0), x[:], reads=[xk], writes=[("xres", s0)])


B_ALOG = 0
B_DTB = 16
NBC = 32


def gdn_col0(ch):
    return ch * 128 if ch < 16 else 2080 + (ch - 16) * 128


def phase_inproj1(P, G, T_L):
    with P.scope():
        C = common_setup(P)
        w = P.sb([128, 8, 4128], BF16)
        for kc in range(8):
            for q in range(3):
                P.dma("pool", w[:, kc, q * 1376:(q + 1) * 1376], G["gdn_w_in"][q][kc * 128:(kc + 1) * 128, :], writes=[("gw", kc, q)])
        wkeys = [("gw", kc, q) for kc in range(8) for q in range(3)]
        mx = derive(P, G, 1, 1, 3)
        x = P.sb([128, 8, TW])
        xk = "xtile"
        h2 = P.sb([128, 8, TW], BF16)
        po = [P.sb([128, TW]) for _ in range(2)]
        pz = [P.sb([128, 512]) for _ in range(2)]
        gt = P.sb([128, 16])
        nea = P.sb([128, 16])
        BC = G["BC"]
        GB = P.sb([128, (T_C + T_L) // 128, 32])
        P.op("act", lambda e: e.activation(out=nea[:], in_=BC[:, B_ALOG:B_ALOG + 16], func=AF.Exp), reads=["bc"], writes=["nea"])
        P.op("dve", lambda e: e.tensor_scalar(nea[:], nea[:], -1.0, None, ALU.mult), reads=["nea"], writes=["nea"])
        for (s0, m, i) in tiles_of(T_L):
            P.dma("sp", x[:], fm_tile(G["xres"], s0), reads=[("xres", s0)], writes=[xk])
            A, B, _ = mx[m]
            modulate(P, C, x, xk, A, B, "der", h2, "h2")
            hk = [("h2", c) for c in range(8)]
            for ch in range(24):
                n = C["n"]
                C["n"] += 1
                ps_, o = C["pb"][n % 2], po[n % 2]
                c0 = gdn_col0(ch)
                for c in range(8):
                    P.op("pe", lambda e: e.matmul(ps_[:], w[:, c, c0:c0 + 128], h2[:, c, :], start=(c == 0), stop=(c == 7)), reads=hk + wkeys, writes=[("pb", n % 2)])
                P.op("act" if ch % 2 else "dve", (lambda e: e.activation(out=o[:], in_=ps_[:], func=AF.Identity)) if ch % 2 else (lambda e: e.tensor_copy(o[:], ps_[:])),
                     reads=[("pb", n % 2)], writes=[("po", n % 2)])
                P.dma("sp", G["pkvq"][ch * 128:(ch + 1) * 128, s0:s0 + TW], o[:], reads=[("po", n % 2)], writes=[("pkvq", s0, ch)])
            for hf in range(2):
                blk = (s0 + hf * 128) // 128
                if m == 0:
                    for zc in range(2):
                        n = C["n"]
                        C["n"] += 1
                        ps_, o = C["pa"][n % 2], pz[n % 2]
                        for c in range(8):
                            P.op("pe", lambda e: e.matmul(ps_[:], h2[:, c, hf * 128:(hf + 1) * 128], w[:, c, 3104 + zc * 512:3104 + (zc + 1) * 512], start=(c == 0), stop=(c == 7)),
                                 reads=hk + wkeys, writes=[("pa", n % 2)])
                        P.op("act", lambda e: e.activation(out=o[:], in_=ps_[:], func=AF.Silu), reads=[("pa", n % 2)], writes=[("pz", n % 2)])
                        P.dma("sp", G["sz_tm"][s0 - T_C + hf * 128:s0 - T_C + (hf + 1) * 128, zc * 512:(zc + 1) * 512], o[:], reads=[("pz", n % 2)], writes=[("sz", s0, hf, zc)])
                n = C["n"]
                C["n"] += 1
                ps_ = C["py"][n % 2]
                for c in range(8):
                    P.op("pe", lambda e: e.matmul(ps_[:, :32], h2[:, c, hf * 128:(hf + 1) * 128], w[:, c, 2048:2080], start=(c == 0), stop=(c == 7)),
                         reads=hk + wkeys, writes=[("py", n % 2)])
                P.op("dve", lambda e: e.tensor_tensor(out=gt[:], in0=ps_[:, 0:16], in1=BC[:, B_DTB:B_DTB + 16], op=ALU.add), reads=[("py", n % 2), "bc"], writes=["gt"])
                P.op("act", lambda e: e.activation(out=gt[:], in_=gt[:], func=AF.Exp), reads=["gt"], writes=["gt"])
                P.op("act", lambda e: e.activation(out=gt[:], in_=gt[:], func=AF.Ln, bias=1.0, scale=1.0), reads=["gt"], writes=["gt"])
                P.op("dve", lambda e: e.tensor_tensor(out=GB[:, blk, 0:16], in0=gt[:], in1=nea[:], op=ALU.mult), reads=["gt", "nea"], writes=[("GB", blk)])
                P.op("act", lambda e: e.activation(out=GB[:, blk, 16:32], in_=ps_[:, 16:32], func=AF.Sigmoid), reads=[("py", n % 2)], writes=[("GB", blk)])
        P.dma("sp", G["gb_d"], GB[:].rearrange("p a b -> p (a b)"), reads=[("GB", b_) for b_ in range((T_C + T_L) // 128)], writes=["gb_d"])


def phase_conv(P, G, T_L):
    T = T_C + T_L
    NBK = T // 128
    LOFF = T_C + 9
    BW = T + 12
    with P.scope():
        load_cst(P, G)
        ones = P.sb([128, 128])
        P.op("dve", lambda e: e.memset(ones[:], 1.0), writes=["ones"])
        buf = [P.sb([128, BW]) for _ in range(2)]
        for b in range(2):
            P.op("pool", lambda e: e.memset(buf[b][:], 0.0), writes=[("cbuf", b)])
        acc = P.sb([128, T])
        actb = P.sb([128, T], BF16)
        sq = P.sb([128, 512])
        lnt = P.sb([128, 512])
        rstd = P.sb([128, 512])
        tmT = [P.sb([128, 4, 128], BF16) for _ in range(2)]
        pss = P.ps([128, 512])
        ptr = [P.ps([128, 4, 128]) for _ in range(2)]
        VEC = G["VEC"]
        segs = [(0, 3, T_C), (T_C, LOFF, T_L)]
        nt = 0
        for ch in range(24):
            b = buf[ch % 2]
            bk = ("cbuf", ch % 2)
            eng = "dve"
            for (s0, b0, L) in segs:
                P.dma("sp", b[:, b0:b0 + L], G["pkvq"][ch * 128:(ch + 1) * 128, s0:s0 + L], reads=[("pkvq", "all")], writes=[bk])
            for (s0, b0, L) in segs:
                for tap in range(7):
                    col = V_CONV + ch * 7 + tap
                    src = b[:, b0 + tap - 3:b0 + tap - 3 + L]
                    if tap == 0:
                        P.op(eng, lambda e: e.tensor_scalar(acc[:, s0:s0 + L], src, VEC[:, col:col + 1], None, ALU.mult), reads=[bk, "vec"], writes=["acc"])
                    else:
                        P.op(eng, lambda e: e.scalar_tensor_tensor(out=acc[:, s0:s0 + L], in0=src, scalar=VEC[:, col:col + 1], in1=acc[:, s0:s0 + L], op0=ALU.mult, op1=ALU.add),
                             reads=[bk, "vec", "acc"], writes=["acc"])
            P.op("act", lambda e: e.activation(out=acc[:], in_=acc[:], func=AF.Silu), reads=["acc"], writes=["acc"])
            kind = "k" if ch < 8 else ("v" if ch < 16 else "q")
            if kind == "v":
                P.op("act", lambda e: e.activation(out=actb[:], in_=acc[:], func=AF.Identity), reads=["acc"], writes=["actb"])
            else:
                qs = 1.0 if kind == "k" else 128.0 ** -0.5
                for t0 in range(0, T, 512):
                    L = min(512, T - t0)
                    P.op("act", lambda e: e.activation(out=sq[:, :L], in_=acc[:, t0:t0 + L], func=AF.Square), reads=["acc"], writes=["csq"])
                    P.op("pe", lambda e: e.matmul(pss[:, :L], ones[:], sq[:, :L], start=True, stop=True), reads=["csq", "ones"], writes=["pss"])
                    P.op("act", lambda e: e.activation(out=lnt[:, :L], in_=pss[:, :L], func=AF.Ln, bias=1e-6, scale=1.0), reads=["pss"], writes=["clnt"])
                    P.op("act", lambda e: e.activation(out=rstd[:, :L], in_=lnt[:, :L], func=AF.Exp, scale=-0.5), reads=["clnt"], writes=["crstd"])
                    P.op("dve", lambda e: e.scalar_tensor_tensor(out=actb[:, t0:t0 + L], in0=acc[:, t0:t0 + L], scalar=qs, in1=rstd[:, :L], op0=ALU.mult, op1=ALU.mult),
                         reads=["acc", "crstd"], writes=["actb"])
                hh = ch if kind == "k" else ch - 16
                dst = G["KT"] if kind == "k" else G["QT"]
                P.dma("sp", dst[hh * 128:(hh + 1) * 128, :], actb[:], reads=["actb"], writes=[("KQT", kind, hh)])
            if kind != "q":
                hh = ch if kind == "k" else ch - 8
                dst = G["K_tm"] if kind == "k" else G["V_tm"]
                for b0 in range(0, NBK, 4):
                    nbk = min(4, NBK - b0)
                    pt = ptr[nt % 2]
                    tm = tmT[nt % 2]
                    pk, tk = ("ptr", nt % 2), ("tmT", nt % 2)
                    nt += 1
                    for j in range(nbk):
                        P.op("pe", lambda e: e.matmul(pt[:, j, :], actb[:, (b0 + j) * 128:(b0 + j + 1) * 128], G["IDN_B"][:], start=True, stop=True),
                             reads=["actb", "s5c"], writes=[pk])
                    P.op("dve" if nt % 2 else "act", (lambda e: e.tensor_copy(tm[:, :nbk, :], pt[:, :nbk, :])) if nt % 2 else
                         (lambda e: e.activation(out=tm[:, :nbk, :], in_=pt[:, :nbk, :], func=AF.Identity)), reads=[pk], writes=[tk])
                    P.dma("sp", dst[b0 * 128:(b0 + nbk) * 128, hh * 128:(hh + 1) * 128].rearrange("(j t) c -> t j c", t=128), tm[:, :nbk, :], reads=[tk], writes=[("KVtm", kind, hh, b0)])


def phase_gdn(P, G, T_L):
    NBK = (T_C + T_L) // 128
    NCB = T_C // 128
    with P.scope():
        load_cst(P, G)
        c4 = P.sb([128, 4, 512])
        P.dma("sp", c4[:], G["cst4"].rearrange("p (a b) -> p a b", b=512), writes=["s5c"])
        G["BD32x4"], G["OFF64x4"], G["OFF128x4"], G["IDNx4"] = c4[:, 0, :], c4[:, 1, :], c4[:, 2, :], c4[:, 3, :]
        GB = P.sb([128, NBK, 32])
        P.dma("sp", GB[:].rearrange("p a b -> p (a b)"), G["gb_d"], reads=["gb_d"], writes=[("GB", b_) for b_ in range(NBK)])
        idf, idb = G["IDN_F"], G["IDN_B"]
        onesf, nonesf = G["ONES_F"], G["NEGONES_F"]
        KTc = [P.sb([128, 8, 128], BF16) for _ in range(2)]
        QTc = [P.sb([128, 8, 128], BF16) for _ in range(2)]
        Ktm = [P.sb([128, 1024], BF16) for _ in range(2)]
        Vtm = [P.sb([128, 1024], BF16) for _ in range(2)]
        EX = P.sb([128, 24])
        negA = P.sb([128, 8])
        GT = P.sb([128, 4, 128])
        DT = P.sb([128, 4, 128])
        DTs = P.sb([128, 4, 128])
        Wt = P.sb([128, 4, 128])
        Wl = P.sb([128, 4, 128])
        W1t = P.sb([128, 4, 128])
        W1l = P.sb([128, 4, 128])
        W2l = P.sb([128, 4, 128])
        Ak = [P.sb([128, 4, 128]) for _ in range(2)]
        Bk = [P.sb([128, 4, 128]) for _ in range(2)]
        PtG = P.sb([128, 4, 128])
        PlG = P.sb([128, 4, 128])
        Ptb = P.sb([128, 8, 128], BF16)
        Aqk = P.sb([128, 8, 128], BF16)
        R = P.sb([128, 8, 128], BF16)
        vnew = P.sb([128, 8, 128], BF16)
        Kd = P.sb([128, 8, 128], BF16)
        O2 = P.sb([128, 4, 128])
        otile = [P.sb([128, 1024]) for _ in range(2)]
        Sf = P.sb([128, 8, 128])
        Sb = P.sb([128, 8, 128], BF16)
        pool = [P.ps([128, 4, 128]) for _ in range(8)]
        pn = [0]
        ce = [0]

        def nxt():
            i = pn[0] % 8
            pn[0] += 1
            return pool[i], ("gps", i)

        def ev():
            ce[0] += 1
            return ("dve", "pool")[ce[0] % 2]

        def load_chunk(blk, slot, lat):
            P.dma("sp", KTc[slot][:], G["KT"][:, blk * 128:(blk + 1) * 128].rearrange("(h p) t -> p h t", p=128), reads=[("KQT", "all")], writes=[("KTc", slot)])
            P.dma("sp", Ktm[slot][:], G["K_tm"][blk * 128:(blk + 1) * 128, :], reads=[("KVtm", "all")], writes=[("Ktm", slot)])
            P.dma("sp", Vtm[slot][:], G["V_tm"][blk * 128:(blk + 1) * 128, :], reads=[("KVtm", "all")], writes=[("Vtm", slot)])
            if lat:
                P.dma("sp", QTc[slot][:], G["QT"][:, blk * 128:(blk + 1) * 128].rearrange("(h p) t -> p h t", p=128), reads=[("KQT", "all")], writes=[("QTc", slot)])

        nchunk = 0
        for d in range(2):
            TRI, TRIS, MNEG, STR = G["TRI"][d], G["TRIS"][d], G["MASKNEG"][d], G["STRICT01"][d]
            order = list(range(NBK)) if d == 0 else ([NCB - 1 - i for i in range(NCB)] + [NBK - 1 - i for i in range(NBK - NCB)])
            P.op("pool", lambda e: e.memset(Sf[:], 0.0), reads=["Sf"], writes=["Sf"])
            P.op("pool", lambda e: e.memset(Sb[:], 0.0), reads=["Sb"], writes=["Sb"])
            load_chunk(order[0], nchunk % 2, order[0] >= NCB)
            for oi, blk in enumerate(order):
                slot = nchunk % 2
                nchunk += 1
                lat = blk >= NCB
                if oi + 1 < len(order):
                    load_chunk(order[oi + 1], nchunk % 2, order[oi + 1] >= NCB)
                kT, kTk = KTc[slot], ("KTc", slot)
                qT, qTk = QTc[slot], ("QTc", slot)
                gc = GB[:, blk, d * 8:(d + 1) * 8]
                bc = GB[:, blk, 16 + d * 8:16 + (d + 1) * 8]
                gbk = ("GB", blk)
                ps1, k1 = nxt()
                p1 = ps1[:].rearrange("p a b -> p (a b)")
                P.op("pe", lambda e: e.matmul(p1[:, 0:8], TRI[:], gc, start=True, stop=True), reads=[gbk, "s5c"], writes=[k1])
                P.op("pe", lambda e: e.matmul(p1[:, 8:16], TRIS[:], gc, start=True, stop=True), reads=[gbk, "s5c"], writes=[k1])
                P.op("pe", lambda e: e.matmul(p1[:, 16:24], onesf[:], gc, start=True, stop=True), reads=[gbk, "s5c"], writes=[k1])
                P.op("act", lambda e: e.activation(out=EX[:], in_=p1[:, 0:24], func=AF.Exp), reads=[k1], writes=["EX"])
                P.op("dve", lambda e: e.tensor_scalar(negA[:], EX[:, 0:8], -1.0, None, ALU.mult), reads=["EX"], writes=["negA"])
                for hg in range(2):
                    hs = [hg * 4 + j for j in range(4)]
                    for j, h in enumerate(hs):
                        P.op(ev(), lambda e: e.tensor_scalar(GT[:, j, :], TRI[:], gc[:, h:h + 1], None, ALU.mult), reads=[gbk, "s5c"], writes=[("GT", j)])
                    pd, kd_ = nxt()
                    for j, h in enumerate(hs):
                        P.op("pe", lambda e: e.matmul(pd[:, j, :], onesf[:], GT[:, j, :], start=True, stop=False), reads=[("GT", j), "s5c"], writes=[kd_])
                        P.op("pe", lambda e: e.matmul(pd[:, j, :], GT[:, j, :], nonesf[:], start=False, stop=True), reads=[("GT", j), "s5c"], writes=[kd_])
                    for j in range(4):
                        P.op("dve", lambda e: e.tensor_tensor(out=DT[:, j, :], in0=pd[:, j, :], in1=MNEG[:], op=ALU.add), reads=[kd_, "s5c"], writes=["DT"])
                    P.op("act", lambda e: e.activation(out=DT[:], in_=DT[:], func=AF.Exp), reads=["DT"], writes=["DT"])
                    for j in range(4):
                        P.op("pool", lambda e: e.tensor_tensor(out=DTs[:, j, :], in0=DT[:, j, :], in1=STR[:], op=ALU.mult), reads=["DT", "s5c"], writes=["DTs"])
                    pk, kk = nxt()
                    for j, h in enumerate(hs):
                        P.op("pe", lambda e: e.matmul(pk[:, j, :], kT[:, h, :], kT[:, h, :], start=True, stop=True), reads=[kTk], writes=[kk])
                    for j, h in enumerate(hs):
                        P.op("dve", lambda e: e.scalar_tensor_tensor(out=Wt[:, j, :], in0=pk[:, j, :], scalar=bc[:, h:h + 1], in1=DTs[:, j, :], op0=ALU.mult, op1=ALU.mult),
                             reads=[kk, gbk, "DTs"], writes=["Wt"])
                    pw, kw = nxt()
                    for j in range(4):
                        P.op("pe", lambda e: e.matmul(pw[:, j, :], Wt[:, j, :], idf[:], start=True, stop=True), reads=["Wt", "s5c"], writes=[kw])
                    P.op("act", lambda e: e.activation(out=Wl[:], in_=pw[:], func=AF.Identity), reads=[kw], writes=["Wl"])
                    if lat:
                        pq, kq = nxt()
                        for j, h in enumerate(hs):
                            P.op("pe", lambda e: e.matmul(pq[:, j, :], kT[:, h, :], qT[:, h, :], start=True, stop=True), reads=[kTk, qTk], writes=[kq])
                        P.op("dve", lambda e: e.tensor_tensor(out=Aqk[:, hg * 4:hg * 4 + 4, :], in0=pq[:], in1=DT[:], op=ALU.mult), reads=[kq, "DT"], writes=[("Aqk", hg)])
                    BD, MO1, MO2, I4 = G["BD32x4"], G["OFF64x4"], G["OFF128x4"], G["IDNx4"]
                    f = lambda t: t[:].rearrange("p a b -> p (a b)")
                    P.op("pool", lambda e: e.tensor_tensor(out=f(Ak[0]), in0=f(Wt), in1=BD, op=ALU.mult), reads=["Wt", "s5c"], writes=[("Ak", 0)])
                    P.op("pool", lambda e: e.tensor_tensor(out=f(Bk[0]), in0=f(Wl), in1=BD, op=ALU.mult), reads=["Wl", "s5c"], writes=[("Bk", 0)])
                    P.op("pool", lambda e: e.tensor_tensor(out=f(W1t), in0=f(Wt), in1=MO1, op=ALU.mult), reads=["Wt", "s5c"], writes=["W1t"])
                    P.op("pool", lambda e: e.tensor_tensor(out=f(W1l), in0=f(Wl), in1=MO1, op=ALU.mult), reads=["Wl", "s5c"], writes=["W1l"])
                    P.op("pool", lambda e: e.tensor_tensor(out=f(W2l), in0=f(Wl), in1=MO2, op=ALU.mult), reads=["Wl", "s5c"], writes=["W2l"])
                    P.op("dve", lambda e: e.tensor_tensor(out=f(PtG), in0=I4, in1=f(Ak[0]), op=ALU.subtract), reads=[("Ak", 0), "s5c"], writes=["PtG"])
                    P.op("dve", lambda e: e.tensor_tensor(out=f(PlG), in0=I4, in1=f(Bk[0]), op=ALU.subtract), reads=[("Bk", 0), "s5c"], writes=["PlG"])
                    Ap, Apk, Bp, Bpk = Ak[0], ("Ak", 0), Bk[0], ("Bk", 0)
                    for lev in range(1, 5):
                        Bn, Bnk = Bk[lev % 2], ("Bk", lev % 2)
                        pb_, kb_ = nxt()
                        for j in range(4):
                            P.op("pe", lambda e: e.matmul(pb_[:, j, :], Ap[:, j, :], Bp[:, j, :], start=True, stop=True), reads=[Apk, Bpk], writes=[kb_])
                        if lev < 4:
                            An, Ank = Ak[lev % 2], ("Ak", lev % 2)
                            pa_, ka_ = nxt()
                            for j in range(4):
                                P.op("pe", lambda e: e.matmul(pa_[:, j, :], Bp[:, j, :], Ap[:, j, :], start=True, stop=True), reads=[Apk, Bpk], writes=[ka_])
                        P.op("act", lambda e: e.activation(out=Bn[:], in_=pb_[:], func=AF.Identity), reads=[kb_], writes=[Bnk])
                        if lev < 4:
                            P.op("dve", lambda e: e.tensor_copy(An[:], pa_[:]), reads=[ka_], writes=[Ank])
                        pu_, ku_ = nxt()
                        pl_, kl_ = nxt()
                        for j in range(4):
                            P.op("pe", lambda e: e.matmul(pu_[:, j, :], Bn[:, j, :], PtG[:, j, :], start=True, stop=True), reads=[Bnk, "PtG"], writes=[ku_])
                            P.op("pe", lambda e: e.matmul(pl_[:, j, :], PtG[:, j, :], Bn[:, j, :], start=True, stop=True), reads=[Bnk, "PtG"], writes=[kl_])
                        P.op("dve", lambda e: e.tensor_tensor(out=PtG[:], in0=pu_[:], in1=PtG[:], op=ALU.add), reads=[ku_, "PtG"], writes=["PtG"])
                        P.op("dve", lambda e: e.tensor_tensor(out=PlG[:], in0=pl_[:], in1=PlG[:], op=ALU.add), reads=[kl_, "PlG"], writes=["PlG"])
                        if lev < 4:
                            Ap, Apk = An, Ank
                        Bp, Bpk = Bn, Bnk
                    px, kx = nxt()
                    px2, kx2 = nxt()
                    for j in range(4):
                        P.op("pe", lambda e: e.matmul(px[:, j, :], W1l[:, j, :], PtG[:, j, :], start=True, stop=True), reads=["W1l", "PtG"], writes=[kx])
                        P.op("pe", lambda e: e.matmul(px2[:, j, :], W1t[:, j, :], PlG[:, j, :], start=True, stop=True), reads=["W1t", "PlG"], writes=[kx2])
                    P.op("act", lambda e: e.activation(out=Ak[0][:], in_=px[:], func=AF.Identity), reads=[kx], writes=[("Ak", 0)])
                    P.op("dve", lambda e: e.tensor_copy(Ak[1][:], px2[:]), reads=[kx2], writes=[("Ak", 1)])
                    py_, ky_ = nxt()
                    py2, ky2 = nxt()
                    for j in range(4):
                        P.op("pe", lambda e: e.matmul(py_[:, j, :], PlG[:, j, :], Ak[0][:, j, :], start=True, stop=True), reads=["PlG", ("Ak", 0)], writes=[ky_])
                        P.op("pe", lambda e: e.matmul(py2[:, j, :], PtG[:, j, :], Ak[1][:, j, :], start=True, stop=True), reads=["PtG", ("Ak", 1)], writes=[ky2])
                    P.op("dve", lambda e: e.tensor_tensor(out=PtG[:], in0=PtG[:], in1=py_[:], op=ALU.subtract), reads=[ky_, "PtG"], writes=["PtG"])
                    P.op("dve", lambda e: e.tensor_tensor(out=PlG[:], in0=PlG[:], in1=py2[:], op=ALU.subtract), reads=[ky2, "PlG"], writes=["PlG"])
                    px, kx = nxt()
                    for j in range(4):
                        P.op("pe", lambda e: e.matmul(px[:, j, :], W2l[:, j, :], PtG[:, j, :], start=True, stop=True), reads=["W2l", "PtG"], writes=[kx])
                    P.op("act", lambda e: e.activation(out=Ak[0][:], in_=px[:], func=AF.Identity), reads=[kx], writes=[("Ak", 0)])
                    py_, ky_ = nxt()
                    for j in range(4):
                        P.op("pe", lambda e: e.matmul(py_[:, j, :], PlG[:, j, :], Ak[0][:, j, :], start=True, stop=True), reads=["PlG", ("Ak", 0)], writes=[ky_])
                    P.op("dve", lambda e: e.tensor_tensor(out=Ptb[:, hg * 4:hg * 4 + 4, :], in0=PtG[:], in1=py_[:], op=ALU.subtract), reads=[ky_, "PtG"], writes=[("Ptb", hg)])
                ot = otile[slot]
                otk = ("otile", slot)
                for hg in range(2):
                    hs = [hg * 4 + j for j in range(4)]
                    pm, km = nxt()
                    for j, h in enumerate(hs):
                        P.op("pe", lambda e: e.matmul(pm[:, j, :], kT[:, h, :], Sb[:, h, :], start=True, stop=True), reads=[kTk, "Sb"], writes=[km])
                    if lat:
                        po1, ko1 = nxt()
                        for j, h in enumerate(hs):
                            P.op("pe", lambda e: e.matmul(po1[:, j, :], qT[:, h, :], Sb[:, h, :], start=True, stop=True), reads=[qTk, "Sb"], writes=[ko1])
                    for j, h in enumerate(hs):
                        P.op("dve", lambda e: e.scalar_tensor_tensor(out=R[:, h, :], in0=pm[:, j, :], scalar=negA[:, h:h + 1], in1=Vtm[slot][:, h * 128:(h + 1) * 128], op0=ALU.mult, op1=ALU.add),
                             reads=[km, "negA", ("Vtm", slot)], writes=[("R", hg)])
                        P.op("pool", lambda e: e.tensor_scalar(Kd[:, h, :], Ktm[slot][:, h * 128:(h + 1) * 128], EX[:, 8 + h:9 + h], None, ALU.mult), reads=[("Ktm", slot), "EX"], writes=[("Kd", hg)])
                    pv, kv = nxt()
                    for j, h in enumerate(hs):
                        P.op("pe", lambda e: e.matmul(pv[:, j, :], Ptb[:, h, :], R[:, h, :], start=True, stop=True), reads=[("Ptb", hg), ("R", hg)], writes=[kv])
                    for j, h in enumerate(hs):
                        P.op("dve", lambda e: e.tensor_scalar(vnew[:, h, :], pv[:, j, :], bc[:, h:h + 1], None, ALU.mult), reads=[kv, gbk], writes=[("vnew", hg)])
                    if lat:
                        po2, ko2 = nxt()
                        for j, h in enumerate(hs):
                            P.op("pe", lambda e: e.matmul(po2[:, j, :], Aqk[:, h, :], vnew[:, h, :], start=True, stop=True), reads=[("Aqk", hg), ("vnew", hg)], writes=[ko2])
                        P.op("act", lambda e: e.activation(out=O2[:], in_=po2[:], func=AF.Identity), reads=[ko2], writes=["O2"])
                        for j, h in enumerate(hs):
                            P.op("dve", lambda e: e.scalar_tensor_tensor(out=ot[:, h * 128:(h + 1) * 128], in0=po1[:, j, :], scalar=EX[:, h:h + 1], in1=O2[:, j, :], op0=ALU.mult, op1=ALU.add),
                                 reads=[ko1, "EX", "O2"], writes=[otk])
                    pd2, kd2 = nxt()
                    for j, h in enumerate(hs):
                        P.op("pe", lambda e: e.matmul(pd2[:, j, :], Kd[:, h, :], vnew[:, h, :], start=True, stop=True), reads=[("Kd", hg), ("vnew", hg)], writes=[kd2])
                    for j, h in enumerate(hs):
                        P.op("dve", lambda e: e.scalar_tensor_tensor(out=Sf[:, h, :], in0=Sf[:, h, :], scalar=EX[:, 16 + h:17 + h], in1=pd2[:, j, :], op0=ALU.mult, op1=ALU.add),
                             reads=["Sf", "EX", kd2], writes=["Sf"])
                P.op("act", lambda e: e.activation(out=Sb[:], in_=Sf[:], func=AF.Identity), reads=["Sf"], writes=["Sb"])
                if lat:
                    od = G["o_f"] if d == 0 else G["o_b"]
                    P.dma("sp", od[(blk - NCB) * 128:(blk - NCB + 1) * 128, :], ot[:], reads=[otk], writes=[("o", d, blk)])


def phase_mixout1(P, G, T_L):
    with P.scope():
        load_cst(P, G)
        C = common_setup(P)
        wout = P.sb([128, 8, 1024], BF16)
        for kc in range(8):
            P.dma("pool", wout[:, kc, :], G["gdn_w_out"][kc * 128:(kc + 1) * 128, :], writes=[("wout", kc)])
        x = P.sb([128, 8, TW])
        xk = "xtile"
        of = P.sb([128, 2, 1024])
        ob = P.sb([128, 2, 1024])
        sz = P.sb([128, 2, 1024])
        sq = P.sb([128, 2, 1024])
        ss = P.sb([128, 16])
        ymix = P.sb([128, 8, TW], BF16)
        MOD = G["MOD"]
        NGB = P.sb([128, 1024])
        P.dma("sp", NGB[:], G["ngbc"], writes=["bc"])
        for (s0, m, i) in tiles_of(T_L):
            if m == 1:
                continue
            l0 = s0 - T_C
            P.dma("sp", x[:], fm_tile(G["xres"], s0), reads=[("xres", s0)], writes=[xk])
            P.dma("sp", of[:], G["o_f"][l0:l0 + TW, :].rearrange("(h t) c -> t h c", t=128), reads=[("o", "all")], writes=["of"])
            P.dma("sp", ob[:], G["o_b"][l0:l0 + TW, :].rearrange("(h t) c -> t h c", t=128), reads=[("o", "all")], writes=["ob"])
            P.dma("sp", sz[:], G["sz_tm"][l0:l0 + TW, :].rearrange("(h t) c -> t h c", t=128), reads=[("sz", "all")], writes=["szt"])
            P.op("pool", lambda e: e.tensor_tensor(out=of[:], in0=of[:], in1=ob[:], op=ALU.add), reads=["of", "ob"], writes=["of"])
            P.op("dve", lambda e: e.tensor_tensor(out=sq[:], in0=of[:], in1=of[:], op=ALU.mult), reads=["of"], writes=["sq1"])
            P.op("dve", lambda e: e.tensor_reduce(out=ss[:], in_=sq[:].rearrange("p a (h v) -> p (a h) v", v=128), axis=mybir.AxisListType.X, op=ALU.add), reads=["sq1"], writes=["ss1"])
            P.op("act", lambda e: e.activation(out=ss[:], in_=ss[:], func=AF.Ln, bias=1e-6, scale=1.0 / 128), reads=["ss1"], writes=["ss1"])
            P.op("act", lambda e: e.activation(out=ss[:], in_=ss[:], func=AF.Exp, scale=-0.5), reads=["ss1"], writes=["ss1"])
            for hf in range(2):
                for h in range(8):
                    P.op("dve" if h % 2 else "pool", lambda e: e.tensor_scalar(of[:, hf, h * 128:(h + 1) * 128], of[:, hf, h * 128:(h + 1) * 128], ss[:, hf * 8 + h:hf * 8 + h + 1], None, ALU.mult),
                         reads=["of", "ss1"], writes=["of"])
                P.op("dve", lambda e: e.tensor_tensor(out=of[:, hf, :], in0=of[:, hf, :], in1=NGB[:], op=ALU.mult), reads=["of", "bc"], writes=["of"])
            P.op("pool", lambda e: e.tensor_tensor(out=of[:], in0=of[:], in1=sz[:], op=ALU.mult), reads=["of", "szt"], writes=["of"])
            for c in range(8):
                n = C["n"]
                C["n"] += 1
                ps_ = C["pb"][n % 2]
                for hf in range(2):
                    P.op("pe", lambda e: e.matmul(ps_[:, hf * 128:(hf + 1) * 128], of[:, hf, c * 128:(c + 1) * 128], G["IDN_F"][:], start=True, stop=True),
                         reads=["of", "s5c"], writes=[("pb", n % 2)])
                P.op("act", lambda e: e.activation(out=ymix[:, c, :], in_=ps_[:], func=AF.Identity), reads=[("pb", n % 2)], writes=[("ymix", c)])
            for oc in range(8):
                n = C["n"]
                C["n"] += 1
                py = C["py"][n % 2]
                for c in range(8):
                    P.op("pe", lambda e: e.matmul(py[:], wout[:, c, oc * 128:(oc + 1) * 128], ymix[:, c, :], start=(c == 0), stop=(c == 7)),
                         reads=[("ymix", c), ("wout", c)], writes=[("py", n % 2)])
                P.op("dve", lambda e: e.scalar_tensor_tensor(out=x[:, oc, :], in0=py[:], scalar=MOD[:, 1, 5, oc, 0:1], in1=x[:, oc, :], op0=ALU.mult, op1=ALU.add),
                     reads=[("py", n % 2), "mod", xk], writes=[xk])
            P.dma("sp", fm_tile(G["xres"], s0), x[:], reads=[xk], writes=[("xres", s0)])


CST_MAP = {"IDN_F": 0, "JREV_F": 128, "ONES_F": 256, "NEGONES_F": 384, "TRI0": 512, "TRI1": 640, "TRIS0": 768, "TRIS1": 896,
           "MNEG0": 1024, "MNEG1": 1152, "STR0": 1280, "STR1": 1408, "S5M0": 1536, "S5M1": 1664}
C_S5C = 1792
C_S5D = 1796
C_J32 = 1828
NCST = 1860


def load_cst(P, G):
    cst = P.sb([128, NCST])
    P.dma("sp", cst[:], G["cst"], writes=["s5c"])
    for k, o in CST_MAP.items():
        G[k] = cst[:, o:o + 128]
    G["TRI"] = [G["TRI0"], G["TRI1"]]
    G["TRIS"] = [G["TRIS0"], G["TRIS1"]]
    G["MASKNEG"] = [G["MNEG0"], G["MNEG1"]]
    G["STRICT01"] = [G["STR0"], G["STR1"]]
    G["S5MASK"] = [G["S5M0"], G["S5M1"]]
    G["S5C"] = cst[:, C_S5C:C_S5C + 4]
    G["S5D"] = cst[:, C_S5D:C_S5D + 32]
    G["J32_F"] = cst[0:32, C_J32:C_J32 + 32]
    cb = P.sb([128, 288], BF16)
    P.op("act", lambda e: e.activation(out=cb[:, 0:256], in_=cst[:, 0:256], func=AF.Identity), reads=["s5c"], writes=["s5c"])
    P.op("act", lambda e: e.activation(out=cb[0:32, 256:288], in_=cst[0:32, C_J32:C_J32 + 32], func=AF.Identity), reads=["s5c"], writes=["s5c"])
    G["IDN_B"] = cb[:, 0:128]
    G["JREV_B"] = cb[:, 128:256]
    G["J32_B"] = cb[0:32, 256:288]


def build_program(T_L):
    T = T_C + T_L
    NCH = T // 8
    P = Prog()
    G = {"dern": 0}
    G["xT"] = [P.din(f"xT{c}", [128, T_L]) for c in range(8)]
    G["ctxT"] = P.din("ctxT", [1024, T_C])
    G["vec_d"] = P.din("vec", [128, NV])
    G["bc_d"] = P.din("bc", [128, NBC])
    G["cst"] = P.din("cst", [128, NCST])
    G["ngbc"] = P.din("ngbc", [128, 1024])
    G["cst4"] = P.din("cst4", [128, 2048])
    G["wm"] = [[P.din(f"wm{l}_{k}", [1024, 1024]) for k in range(9)] for l in range(2)]
    G["w13"] = [[P.din(f"w13_{f}_{q}", [1024, 1408]) for q in range(4)] for f in range(4)]
    G["w2"] = [[P.din(f"w2_{f}_{q}", [1408, 1024]) for q in range(2)] for f in range(4)]
    for nm, shp in (("ev_w_in", [1024, 1024]), ("ev_w_out", [1024, 1024]), ("s5_w_glu", [512, 512]), ("pool_w", [512, 128]),
                    ("MWL", [4, 128, 128]), ("MWC", [16, 128, 128]), ("gdn_w_out", [1024, 1024]), ("s5_qv", [128, NCH]),
                    ("s5_cre", [128, 4096]), ("s5_cim", [128, 4096]), ("s5_bre", [128, 4096]), ("s5_bim", [128, 4096])):
        G[nm] = P.din(nm, shp)
    for nm in ("s5_lre", "s5_lim", "s5_ls", "s5_ke", "s5_ka", "s5_kg"):
        G[nm] = [P.din(f"{nm}{d}", [128, 4096]) for d in range(2)]
    G["gdn_w_in"] = [P.din(f"gdn_w_in{q}", [1024, 1376]) for q in range(3)]
    G["outT"] = [P.dout(f"oT{c}", [128, T_L]) for c in range(8)]
    G["xres"] = P.dscr("xres", [1024, T])
    G["ppool"] = P.dscr("ppool", [512, T], BF16)
    G["p_tm"] = P.dscr("p_tm", [T, 512])
    G["y_tm"] = P.dscr("y_tm", [T, 512])
    G["pkvq"] = P.dscr("pkvq", [3072, T])
    G["sz_tm"] = P.dscr("sz_tm", [T_L, 1024])
    G["KT"] = P.dscr("KT", [1024, T], BF16)
    G["QT"] = P.dscr("QT", [1024, T], BF16)
    G["K_tm"] = P.dscr("K_tm", [T, 1024], BF16)
    G["V_tm"] = P.dscr("V_tm", [T, 1024], BF16)
    G["o_f"] = P.dscr("o_f", [T_L, 1024])
    G["o_b"] = P.dscr("o_b", [T_L, 1024])
    G["gb_d"] = P.dscr("gb_d", [128, (T // 128) * 32])
    G["VEC"] = P.sb([128, NV])
    G["BC"] = P.sb([128, NBC])
    G["MOD"] = P.sb([128, 2, 9, 8, 2])
    G["DER"] = P.sb([128, 512])
    P.dma("sp", G["VEC"][:], G["vec_d"], writes=["vec"])
    P.dma("sp", G["BC"][:], G["bc_d"], writes=["bc"])
    P.barrier()
    phase_mods(P, G)
    ffn_phase(P, G, 0, 0, 0, 0, T_L, first=True)
    phase_inproj0(P, G, T_L)
    phase_s5(P, G, T_L)
    phase_mixout0(P, G, T_L)
    ffn_phase(P, G, 1, 0, 2, 6, T_L)
    ffn_phase(P, G, 2, 1, 0, 0, T_L)
    phase_inproj1(P, G, T_L)
    phase_conv(P, G, T_L)
    phase_gdn(P, G, T_L)
    phase_mixout1(P, G, T_L)
    ffn_phase(P, G, 3, 1, 2, 6, T_L, lat_only=True, final=True)
    return P.finish()


def _s5_expand_sg(a):
    t = np.asarray(a, np.float32).T
    t = np.concatenate([t, t], 0)
    return np.ascontiguousarray(np.broadcast_to(t[:, :, None, None], (128, 32, 8, 16)).reshape(128, 4096))


def _const_tables(T_L):
    NL = T_L // 8
    cst = np.zeros((128, NCST), np.float32)
    idx = np.arange(128)
    cst[:, 0:128] = np.eye(128)
    cst[:, 128:256] = np.eye(128)[::-1]
    cst[:, 256:384] = 1.0
    cst[:, 384:512] = -1.0
    m, i = idx[:, None], idx[None, :]
    cst[:, 512:640] = (m <= i)
    cst[:, 640:768] = (m >= i)
    cst[:, 768:896] = (m > i)
    cst[:, 896:1024] = (m < i)
    cst[:, 1024:1152] = np.where(m <= i, 0.0, -30000.0)
    cst[:, 1152:1280] = np.where(m >= i, 0.0, -30000.0)
    cst[:, 1280:1408] = (m < i)
    cst[:, 1408:1536] = (m > i)
    sig = idx // 16
    for d in range(2):
        pos = sig if d == 0 else 7 - sig
        cst[:, 1536 + d * 128:1664 + d * 128] = (pos[:, None] <= pos[None, :])
    cst[:64, C_S5C] = 1.0
    cst[64:, C_S5C + 1] = 1.0
    cst[:32, C_J32:C_J32 + 32] = np.eye(32)[::-1]
    ks = {}
    tau = np.arange(8)
    for d in range(2):
        pos = tau if d == 0 else 7 - tau
        for nm, v in (("ke", pos + 1.0), ("ka", -(pos + 1.0)), ("kg", 7.0 - pos)):
            ks[f"s5_{nm}{d}"] = np.ascontiguousarray(np.broadcast_to(v.astype(np.float32)[None, None, :, None], (128, 32, 8, 16)).reshape(128, 4096))
    qv = np.concatenate([32 + np.arange(NL) + 1, np.arange(32) + 1]).astype(np.float32)
    ks["s5_qv"] = np.ascontiguousarray(np.broadcast_to(qv[None, :], (128, NL + 32)))
    def mi(L, w):
        left = w // 2
        right = w - 1 - left
        M = np.zeros((L, L), np.float64)
        for tp in range(L):
            lo, hi = max(tp - left, 0), min(tp + right + 1, L)
            M[lo:hi, tp] = 1.0 / (hi - lo)
        return (M - np.eye(L)).astype(np.float32)
    MWL = np.zeros((4, 128, 128), np.float32)
    MWC = np.zeros((16, 128, 128), np.float32)
    for gi, w in enumerate((2, 4, 8, 16)):
        m64 = mi(64, w)
        MWL[gi, :64, :64] = m64
        MWL[gi, 64:, 64:] = m64
        m256 = mi(256, w)
        for kb in range(2):
            for mb in range(2):
                MWC[gi * 4 + kb * 2 + mb] = m256[kb * 128:(kb + 1) * 128, mb * 128:(mb + 1) * 128]
    ks["MWL"], ks["MWC"] = MWL, MWC
    blk = lambda n: (idx[:, None] // n == idx[None, :] // n)
    bd32, bd64 = blk(32), blk(64)
    c4 = np.stack([bd32, bd64 & ~bd32, ~bd64, np.eye(128, dtype=bool)], 0).astype(np.float32)
    ks["cst4"] = np.ascontiguousarray(np.broadcast_to(c4[:, :, None, :], (4, 128, 4, 128)).transpose(1, 0, 2, 3).reshape(128, 2048))
    return cst, ks


def make_inputs(inp, T_L):
    f32 = np.float32
    cst, shared = _const_tables(T_L)
    cst[:, C_S5D:C_S5D + 32] = np.tile(np.asarray(inp["s5_d"][0], f32).reshape(32, 16).T, (8, 1))
    shared["cst"] = cst
    for l in range(2):
        for k in range(9):
            shared[f"wm{l}_{k}"] = np.ascontiguousarray(inp["w_mod"][l][:, k * 1024:(k + 1) * 1024])
    ffn = [(inp["ffn1_w13"][0], inp["ffn1_w2"][0]), (inp["ffn2_w13"][0], inp["ffn2_w2"][0]),
           (inp["ffn1_w13"][1], inp["ffn1_w2"][1]), (inp["ffn2_w13"][1], inp["ffn2_w2"][1])]
    for f, (w13, w2) in enumerate(ffn):
        for q in range(4):
            shared[f"w13_{f}_{q}"] = np.ascontiguousarray(w13[:, q * 1408:(q + 1) * 1408])
        for q in range(2):
            shared[f"w2_{f}_{q}"] = np.ascontiguousarray(w2[q * 1408:(q + 1) * 1408, :])
    shared["ev_w_in"] = np.ascontiguousarray(inp["ev_w_in"][0])
    shared["ev_w_out"] = np.ascontiguousarray(inp["ev_w_out"][0])
    shared["s5_w_glu"] = np.ascontiguousarray(inp["s5_w_glu"][0])
    shared["pool_w"] = np.ascontiguousarray(inp["pool_w"][0].reshape(512, 128))
    shared["gdn_w_out"] = np.ascontiguousarray(inp["gdn_w_out"][0])
    for q in range(3):
        shared[f"gdn_w_in{q}"] = np.ascontiguousarray(inp["gdn_w_in"][0][:, q * 1376:(q + 1) * 1376])
    for d in range(2):
        shared[f"s5_lre{d}"] = _s5_expand_sg(inp["s5_lambda_re"][0][d])
        shared[f"s5_lim{d}"] = _s5_expand_sg(inp["s5_lambda_im"][0][d])
        shared[f"s5_ls{d}"] = _s5_expand_sg(np.broadcast_to(np.asarray(inp["s5_log_step"][0][d])[:, None], (32, 64)))
    def c_exp(c):
        t = np.transpose(np.asarray(c, f32), (2, 0, 1))
        t = np.concatenate([t, t], 0)
        return np.ascontiguousarray(np.broadcast_to(t[:, :, None, :], (128, 32, 8, 16)).reshape(128, 4096))
    def b_exp(b):
        t = np.transpose(np.asarray(b, f32), (1, 0, 2))
        t = np.concatenate([t, t], 0)
        return np.ascontiguousarray(np.broadcast_to(t[:, :, None, :], (128, 32, 8, 16)).reshape(128, 4096))
    shared["s5_cre"], shared["s5_cim"] = c_exp(inp["s5_c_re"][0]), c_exp(inp["s5_c_im"][0])
    shared["s5_bre"], shared["s5_bim"] = b_exp(inp["s5_b_re"][0]), b_exp(inp["s5_b_im"][0])
    bc = np.zeros((128, NBC), f32)
    bc[:, B_ALOG:B_ALOG + 16] = np.asarray(inp["gdn_a_log"][0], f32).reshape(16)[None, :]
    bc[:, B_DTB:B_DTB + 16] = np.asarray(inp["gdn_dt_bias"][0], f32).reshape(16)[None, :]
    shared["bc"] = bc
    shared["ngbc"] = np.ascontiguousarray(np.broadcast_to(np.tile(np.asarray(inp["gdn_norm_g"][0], f32), 8)[None, :], (128, 1024)))
    vec0 = np.zeros((128, NV), f32)
    for l in range(2):
        for kn in range(3):
            vec0[:, V_NG + (l * 3 + kn) * 8:V_NG + (l * 3 + kn) * 8 + 8] = vec_cols(inp["norm_g"][l, kn])
        for k in range(9):
            vec0[:, V_BM + (l * 9 + k) * 8:V_BM + (l * 9 + k) * 8 + 8] = vec_cols(inp["b_mod"][l][k * 1024:(k + 1) * 1024])
    vec0[:, V_FNG:V_FNG + 8] = vec_cols(inp["final_norm_g"])
    vec0[:, V_BGLU:V_BGLU + 4] = vec_cols(inp["s5_b_glu"][0])
    vec0[:, V_PSC:V_PSC + 4] = vec_cols(inp["pool_scale"][0])
    cw = np.asarray(inp["gdn_conv_w"][0], f32)
    vec0[:, V_CONV:V_CONV + 168] = cw.reshape(7, 24, 128).transpose(2, 1, 0).reshape(128, 168)
    cctx = vec_cols(inp["c_ctx"])
    per_seq = []
    for b in range(4):
        m = dict(shared)
        vec = vec0.copy()
        cb = vec_cols(inp["c"][b])
        vec[:, V_CT:V_CT + 16] = np.stack([cb, cctx], -1).reshape(128, 16)
        m["vec"] = vec
        xT = np.ascontiguousarray(np.asarray(inp["x"][b][:T_L], f32).T)
        for c in range(8):
            m[f"xT{c}"] = np.ascontiguousarray(xT[c * 128:(c + 1) * 128])
        m["ctxT"] = np.ascontiguousarray(np.asarray(inp["ctx"][b], f32).T)
        per_seq.append(m)
    return per_seq


def kernel(**inputs):
    T_L = inputs["x"].shape[1]
    nc = build_program(T_L)
    in_maps = make_inputs(inputs, T_L)
    res = run_bass_kernel_spmd(nc, in_maps, core_ids=list(range(4)))
    out = np.zeros((4, T_L, D), np.float32)
    for b in range(4):
        r = res.results[b]
        oT = np.concatenate([r[f"oT{c}"] for c in range(8)], 0)
        out[b] = oT.T
    return out
```

```python
import numpy as np
from contextlib import ExitStack, contextmanager
import concourse.bass as bass
import concourse.mybir as mybir
from concourse.bass_utils import run_bass_kernel_spmd

F32 = mybir.dt.float32
BF16 = mybir.dt.bfloat16
I32 = mybir.dt.int32
AF = mybir.ActivationFunctionType
ALU = mybir.AluOpType

NDMASEM = 24
DEBUG_SCRATCH = False
ENG = ("pe", "act", "dve", "pool")


class Prog:
    def __init__(self):
        self.nc = bass.Bass("TRN2", target_bir_lowering=False)
        self.root = ExitStack()
        self.es = self.root
        nc = self.nc
        self.eng = {"pe": nc.tensor, "act": nc.scalar, "dve": nc.vector, "pool": nc.gpsimd, "sp": nc.sync}
        self.sem = {e: self.root.enter_context(nc.semaphore("s_" + e)) for e in ENG}
        self.cnt = {e: 0 for e in ENG}
        self.dsem = [self.root.enter_context(nc.semaphore(f"d{i}")) for i in range(NDMASEM)]
        self.dtot = [0] * NDMASEM
        self.dnext = 0
        self.seen = {e: {} for e in self.eng}
        self.lastw = {}
        self.readers = {}
        self.uid = 0

    def _semobj(self, k):
        return self.sem[k] if isinstance(k, str) else self.dsem[k]

    def _wait(self, e, tok):
        if tok is None:
            return
        k, val, src = tok
        if src == e and e == "pe":
            return
        if self.seen[e].get(k, 0) >= val:
            return
        self.eng[e].wait_ge(self._semobj(k), val)
        self.seen[e][k] = val

    def _deps(self, e, reads, writes, dma=False):
        for k in reads:
            self._wait(e, self.lastw.get(k))
        for k in writes:
            self._wait(e, self.lastw.get(k))
            for t in self.readers.get(k, ()):
                if t[2] == e and not dma:
                    continue
                self._wait(e, t)

    def _commit(self, tok, reads, writes):
        for k in reads:
            self.readers.setdefault(k, []).append(tok)
        for k in writes:
            self.lastw[k] = tok
            self.readers[k] = []

    def op(self, e, fn, reads=(), writes=()):
        self._deps(e, reads, writes)
        ins = fn(self.eng[e])
        self.cnt[e] += 1
        ins.then_inc(self.sem[e], 1)
        self._commit((e, self.cnt[e], e), reads, writes)

    def dma(self, q, out, in_, reads=(), writes=()):
        s = self.dnext
        self.dnext = (self.dnext + 1) % NDMASEM
        if self.dtot[s] > 0:
            self._wait(q, (s, self.dtot[s], None))
        self._deps(q, reads, writes, dma=True)
        ins = self.eng[q].dma_start(out=out, in_=in_)
        self.dtot[s] += 16
        ins.then_inc(self.dsem[s], 16)
        self._commit((s, self.dtot[s], None), reads, writes)

    def barrier(self):
        for e in ("pe", "act", "dve", "pool", "sp"):
            for e2 in ENG:
                if self.cnt[e2]:
                    self._wait(e, (e2, self.cnt[e2], None))
            for s in range(NDMASEM):
                if self.dtot[s]:
                    self._wait(e, (s, self.dtot[s], None))

    @contextmanager
    def scope(self):
        outer = self.es
        self.es = ExitStack()
        try:
            yield
        finally:
            self.barrier()
            self.es.close()
            self.es = outer

    def sb(self, shape, dtype=F32):
        self.uid += 1
        return self.es.enter_context(self.nc.sbuf_tensor(f"sb{self.uid}", list(shape), dtype))

    def ps(self, shape, dtype=F32):
        self.uid += 1
        return self.es.enter_context(self.nc.psum_tensor(f"ps{self.uid}", list(shape), dtype))

    def din(self, name, shape, dtype=F32):
        return self.nc.dram_tensor(name, list(shape), dtype, kind="ExternalInput").ap()

    def dout(self, name, shape, dtype=F32):
        return self.nc.dram_tensor(name, list(shape), dtype, kind="ExternalOutput").ap()

    def dscr(self, name, shape, dtype=F32):
        kind = "ExternalOutput" if DEBUG_SCRATCH else "Internal"
        return self.nc.dram_tensor(name, list(shape), dtype, kind=kind).ap()

    def finish(self):
        self.barrier()
        self.root.close()
        return self.nc


D = 1024
NFF = 22
TW = 256
T_C = 256


def vec_cols(v):
    v = np.asarray(v, np.float32).reshape(-1, 128)
    return np.ascontiguousarray(v.T)


def common_setup(P):
    C = {}
    C["ones"] = P.sb([128, 128])
    P.op("dve", lambda e: e.memset(C["ones"][:], 1.0), writes=["ones"])
    C["sq"] = P.sb([128, 8, TW])
    C["rstd"] = P.sb([128, TW])
    C["lnt"] = P.sb([128, TW])
    C["tmp"] = [P.sb([128, TW]) for _ in range(2)]
    C["h"] = P.sb([128, 8, TW], BF16)
    C["ss_ps"] = P.ps([128, TW])
    C["pa"] = [P.ps([128, 512]) for _ in range(2)]
    C["pb"] = [P.ps([128, TW]) for _ in range(2)]
    C["py"] = [P.ps([128, TW]) for _ in range(2)]
    C["n"] = 0
    return C


def ffn_setup(P, C, wd13, wd2, tag):
    S = {"tag": tag}
    S["w13"] = P.sb([128, 8, 5632], BF16)
    S["w2"] = P.sb([128, NFF, 1024], BF16)
    C["hid"] = P.sb([128, NFF, TW], BF16)
    C["sa"] = [P.sb([128, TW]) for _ in range(2)]
    for kc in range(8):
        for q in range(4):
            P.dma("pool", S["w13"][:, kc, q * 1408:(q + 1) * 1408], wd13[q][kc * 128:(kc + 1) * 128, :], writes=[("w13", kc, q)])
    for j in range(NFF):
        P.dma("pool", S["w2"][:, j, :], wd2[j // 11][(j % 11) * 128:(j % 11 + 1) * 128, :], writes=[("w2", j)])
    return S


def rms_rstd(P, C, xT, xkey):
    P.op("act", lambda e: e.activation(out=C["sq"][:], in_=xT[:], func=AF.Square), reads=[xkey], writes=["sq"])
    for c in range(8):
        P.op("pe", lambda e: e.matmul(C["ss_ps"][:], C["ones"][:], C["sq"][:, c, :], start=(c == 0), stop=(c == 7)),
             reads=["sq", "ones"], writes=["ss_ps"])
    P.op("act", lambda e: e.activation(out=C["lnt"][:], in_=C["ss_ps"][:], func=AF.Ln, bias=1e-6, scale=1.0 / 1024),
         reads=["ss_ps"], writes=["lnt"])
    P.op("act", lambda e: e.activation(out=C["rstd"][:], in_=C["lnt"][:], func=AF.Exp, scale=-0.5), reads=["lnt"], writes=["rstd"])


def modulate(P, C, xT, xkey, A, B, vkey, hout, hkey):
    rms_rstd(P, C, xT, xkey)
    for c in range(8):
        t = C["tmp"][c % 2]
        tk = ("tmp", c % 2)
        P.op("dve", lambda e: e.scalar_tensor_tensor(out=t[:], in0=xT[:, c, :], scalar=A[:, c:c + 1], in1=C["rstd"][:], op0=ALU.mult, op1=ALU.mult),
             reads=[xkey, "rstd", vkey], writes=[tk])
        P.op("act", lambda e: e.activation(out=hout[:, c, :], in_=t[:], func=AF.Identity, bias=B[:, c:c + 1], scale=1.0),
             reads=[tk, vkey], writes=[(hkey, c)])


def ffn_main(P, C, S, xT, xkey, h, hkey, G, vkey, mid=None):
    for j in range(NFF):
        n = C["n"]
        C["n"] += 1
        pa, pb, sa = C["pa"][n % 2], C["pb"][n % 2], C["sa"][n % 2]
        for c in range(8):
            P.op("pe", lambda e: e.matmul(pa[:, :TW], S["w13"][:, c, j * 128:(j + 1) * 128], h[:, c, :], start=(c == 0), stop=(c == 7)),
                 reads=[(hkey, c), ("w13", c, (j * 128) // 1408), ("w13", c, (j * 128 + 127) // 1408)], writes=[("pa", n % 2)])
        for c in range(8):
            o = 2816 + j * 128
            P.op("pe", lambda e: e.matmul(pb[:], S["w13"][:, c, o:o + 128], h[:, c, :], start=(c == 0), stop=(c == 7)),
                 reads=[(hkey, c), ("w13", c, o // 1408), ("w13", c, (o + 127) // 1408)], writes=[("pb", n % 2)])
        P.op("act", lambda e: e.activation(out=sa[:], in_=pa[:, :TW], func=AF.Silu), reads=[("pa", n % 2)], writes=[("sa", n % 2)])
        P.op("dve", lambda e: e.tensor_tensor(out=C["hid"][:, j, :], in0=sa[:], in1=pb[:], op=ALU.mult),
             reads=[("sa", n % 2), ("pb", n % 2)], writes=[("hid", j)])
        if j == 10 and mid is not None:
            mid()
    for oc in range(8):
        n = C["n"]
        C["n"] += 1
        py = C["py"][n % 2]
        for j in range(NFF):
            P.op("pe", lambda e: e.matmul(py[:], S["w2"][:, j, oc * 128:(oc + 1) * 128], C["hid"][:, j, :], start=(j == 0), stop=(j == NFF - 1)),
                 reads=[("hid", j), ("w2", j)], writes=[("py", n % 2)])
        P.op("dve", lambda e: e.scalar_tensor_tensor(out=xT[:, oc, :], in0=py[:], scalar=G[:, oc:oc + 1], in1=xT[:, oc, :], op0=ALU.mult, op1=ALU.add),
             reads=[("py", n % 2), vkey, xkey], writes=[xkey])


V_NG = 0
V_FNG = 48
V_BM = 56
V_BGLU = 200
V_PSC = 204
V_CONV = 208
V_CT = 376
NV = 392


def phase_mods(P, G):
    VEC, MOD = G["VEC"], G["MOD"]
    with P.scope():
        wsb = [P.sb([128, 8, 1024]) for _ in range(2)]
        s_sb = P.sb([128, 8, 2])
        pss = [P.ps([128, 8, 2]) for _ in range(2)]
        cview = VEC[:, V_CT:V_CT + 16].rearrange("p (a b) -> p a b", b=2)
        P.op("act", lambda e: e.activation(out=s_sb[:], in_=cview, func=AF.Silu), reads=["vec"], writes=["s_sb"])
        for l in range(2):
            for k in range(9):
                i = l * 9 + k
                w = wsb[i % 2]
                for kc in range(8):
                    P.dma("sp", w[:, kc, :], G["wm"][l][k][kc * 128:(kc + 1) * 128, :], writes=[("wm", i % 2, kc)])
                ps = pss[i % 2]
                for fc in range(8):
                    for kc in range(8):
                        P.op("pe", lambda e: e.matmul(ps[:, fc, :], w[:, kc, fc * 128:(fc + 1) * 128], s_sb[:, kc, :], start=(kc == 0), stop=(kc == 7)),
                             reads=["s_sb", ("wm", i % 2, kc)], writes=[("mps", i % 2)])
                    col = V_BM + i * 8 + fc
                    P.op("dve", lambda e: e.tensor_scalar(MOD[:, l, k, fc, :], ps[:, fc, :], VEC[:, col:col + 1], None, ALU.add),
                         reads=[("mps", i % 2), "vec"], writes=["mod"])


def derive(P, G, l, kn, k0):
    VEC, MOD, DER = G["VEC"], G["MOD"], G["DER"]
    out = []
    for m in range(2):
        base = G["dern"]
        G["dern"] += 16
        a = DER[:, base:base + 8]
        g = DER[:, base + 8:base + 16]
        ng = VEC[:, V_NG + (l * 3 + kn) * 8:V_NG + (l * 3 + kn) * 8 + 8]
        P.op("dve", lambda e: e.scalar_tensor_tensor(out=a, in0=MOD[:, l, k0 + 1, :, m], scalar=1.0, in1=ng, op0=ALU.add, op1=ALU.mult),
             reads=["mod", "vec"], writes=["der"])
        P.op("dve", lambda e: e.tensor_scalar(g, MOD[:, l, k0 + 2, :, m], 0.5, None, ALU.mult), reads=["mod"], writes=["der"])
        out.append((a, MOD[:, l, k0, :, m], g))
    return out


def tiles_of(T_L):
    return [(0, 1, -1)] + [(T_C + i * TW, 0, i) for i in range(T_L // TW)]


def load_x_tile(P, G, x, xk, s0, m, i):
    if m == 1:
        P.dma("sp", x[:], G["ctxT"].rearrange("(c p) t -> p c t", p=128), writes=[xk])
    else:
        for c in range(8):
            P.dma("sp", x[:, c, :], G["xT"][c][:, i * TW:(i + 1) * TW], writes=[xk])


def fm_tile(ap, s0):
    return ap[:, s0:s0 + TW].rearrange("(c p) t -> p c t", p=128)


def ffn_phase(P, G, fi, l, kn, k0, T_L, first=False, lat_only=False, final=False):
    with P.scope():
        C = common_setup(P)
        S = ffn_setup(P, C, G["w13"][fi], G["w2"][fi], "f")
        f1 = derive(P, G, l, kn, k0)
        xb = [P.sb([128, 8, TW]) for _ in range(2)]
        hb = [C["h"], P.sb([128, 8, TW], BF16)]
        yo = P.sb([128, 8, TW]) if final else None
        tl = [t for t in tiles_of(T_L) if not (lat_only and t[1] == 1)]

        def load(ti):
            s0, m, i = tl[ti]
            x, xk = xb[ti % 2], ("xt", ti % 2)
            if first:
                load_x_tile(P, G, x, xk, s0, m, i)
            else:
                P.dma("sp", x[:], fm_tile(G["xres"], s0), reads=[("xres", s0)], writes=[xk])

        def prep(ti):
            s0, m, i = tl[ti]
            A, B, _ = f1[m]
            modulate(P, C, xb[ti % 2], ("xt", ti % 2), A, B, "der", hb[ti % 2], ("hb", ti % 2))

        load(0)
        prep(0)
        for ti, (s0, m, i) in enumerate(tl):
            x, xk = xb[ti % 2], ("xt", ti % 2)
            nxt_ = ti + 1 if ti + 1 < len(tl) else None
            if nxt_ is not None:
                load(nxt_)
            ffn_main(P, C, S, x, xk, hb[ti % 2], ("hb", ti % 2), f1[m][2], "der", (lambda: prep(nxt_)) if nxt_ is not None else None)
            if not final:
                P.dma("sp", fm_tile(G["xres"], s0), x[:], reads=[xk], writes=[("xres", s0)])
            else:
                rms_rstd(P, C, x, xk)
                for c in range(8):
                    col = V_FNG + c
                    P.op("dve", lambda e: e.scalar_tensor_tensor(out=yo[:, c, :], in0=x[:, c, :], scalar=G["VEC"][:, col:col + 1], in1=C["rstd"][:], op0=ALU.mult, op1=ALU.mult),
                         reads=[xk, "rstd", "vec"], writes=[("yo", c)])
                    P.dma("sp", G["outT"][c][:, i * TW:(i + 1) * TW], yo[:, c, :], reads=[("yo", c)], writes=[("out", c, i)])


def phase_inproj0(P, G, T_L):
    with P.scope():
        C = common_setup(P)
        win = P.sb([128, 8, 1024], BF16)
        for kc in range(8):
            P.dma("pool", win[:, kc, :], G["ev_w_in"][kc * 128:(kc + 1) * 128, :], writes=[("win", kc)])
        mx = derive(P, G, 0, 1, 3)
        x = P.sb([128, 8, TW])
        xk = "xtile"
        h2 = P.sb([128, 8, TW], BF16)
        pb16 = [P.sb([128, TW], BF16) for _ in range(2)]
        ptm = [P.sb([128, 512]) for _ in range(2)]
        for (s0, m, i) in tiles_of(T_L):
            P.dma("sp", x[:], fm_tile(G["xres"], s0), reads=[("xres", s0)], writes=[xk])
            A, B, _ = mx[m]
            modulate(P, C, x, xk, A, B, "der", h2, "h2")
            for oc in range(4, 8):
                n = C["n"]
                C["n"] += 1
                ps_, po = C["pb"][n % 2], pb16[n % 2]
                for c in range(8):
                    P.op("pe", lambda e: e.matmul(ps_[:], win[:, c, oc * 128:(oc + 1) * 128], h2[:, c, :], start=(c == 0), stop=(c == 7)),
                         reads=[("h2", c), ("win", c)], writes=[("pb", n % 2)])
                P.op("act", lambda e: e.activation(out=po[:], in_=ps_[:], func=AF.Identity), reads=[("pb", n % 2)], writes=[("pb16", n % 2)])
                P.dma("sp", G["ppool"][(oc - 4) * 128:(oc - 3) * 128, s0:s0 + TW], po[:], reads=[("pb16", n % 2)], writes=[("ppool", s0, oc)])
            for hf in range(2):
                n = C["n"]
                C["n"] += 1
                ps_, po = C["pa"][n % 2], ptm[n % 2]
                for c in range(8):
                    P.op("pe", lambda e: e.matmul(ps_[:], h2[:, c, hf * 128:(hf + 1) * 128], win[:, c, 0:512], start=(c == 0), stop=(c == 7)),
                         reads=[("h2", c), ("win", c)], writes=[("pa", n % 2)])
                P.op("dve", lambda e: e.tensor_copy(po[:], ps_[:]), reads=[("pa", n % 2)], writes=[("ptm", n % 2)])
                P.dma("sp", G["p_tm"][s0 + hf * 128:s0 + (hf + 1) * 128, :], po[:], reads=[("ptm", n % 2)], writes=[("p_tm", s0, hf)])


TWO_PI = 6.283185307179586


def s5_cpow(P, T, K, a, th, N):
    t1, t2, u, rf, mag, Lre, Lim = T["t1"], T["t2"], T["u"], T["rf"], T["mag"], T["Lre"], T["Lim"]
    ti = T["ti"]
    P.op("dve", lambda e: e.tensor_tensor(out=t1[:, :N], in0=a[:, :N], in1=K[:, :N], op=ALU.mult), reads=["s5a", "s5k"], writes=["t1"])
    P.op("act", lambda e: e.activation(out=mag[:, :N], in_=t1[:, :N], func=AF.Exp), reads=["t1"], writes=["mag"])
    P.op("dve", lambda e: e.tensor_tensor(out=t2[:, :N], in0=th[:, :N], in1=K[:, :N], op=ALU.mult), reads=["s5a", "s5k"], writes=["t2"])
    for (off, dst, key) in ((0.0, Lim, "Lim"), (0.25, Lre, "Lre")):
        P.op("dve", lambda e: e.tensor_scalar(u[:, :N], t2[:, :N], 1.0 / TWO_PI, off, ALU.mult, ALU.add), reads=["t2"], writes=["u"])
        P.op("dve", lambda e: e.tensor_copy(ti[:, :N], u[:, :N]), reads=["u"], writes=["ti"])
        P.op("dve", lambda e: e.tensor_copy(rf[:, :N], ti[:, :N]), reads=["ti"], writes=["rf"])
        P.op("dve", lambda e: e.tensor_tensor(out=u[:, :N], in0=u[:, :N], in1=rf[:, :N], op=ALU.subtract), reads=["u", "rf"], writes=["u"])
        P.op("act", lambda e: e.activation(out=dst[:, :N], in_=u[:, :N], func=AF.Sin, scale=TWO_PI), reads=["u"], writes=[key])
        P.op("dve", lambda e: e.tensor_tensor(out=dst[:, :N], in0=dst[:, :N], in1=mag[:, :N], op=ALU.mult), reads=[key, "mag"], writes=[key])


def cmul(P, ore, oim, are, aim, bre, bim, tmp, rk, wk):
    P.op("dve", lambda e: e.tensor_tensor(out=ore, in0=are, in1=bre, op=ALU.mult), reads=rk, writes=wk)
    P.op("dve", lambda e: e.tensor_tensor(out=tmp, in0=aim, in1=bim, op=ALU.mult), reads=rk, writes=["cm_tmp"])
    P.op("dve", lambda e: e.tensor_tensor(out=ore, in0=ore, in1=tmp, op=ALU.subtract), reads=wk + ["cm_tmp"], writes=wk)
    P.op("dve", lambda e: e.tensor_tensor(out=oim, in0=are, in1=bim, op=ALU.mult), reads=rk, writes=wk)
    P.op("dve", lambda e: e.tensor_tensor(out=tmp, in0=aim, in1=bre, op=ALU.mult), reads=rk, writes=["cm_tmp"])
    P.op("dve", lambda e: e.tensor_tensor(out=oim, in0=oim, in1=tmp, op=ALU.add), reads=wk + ["cm_tmp"], writes=wk)


def s5_precompute(P, G, W):
    N = 1024
    with P.scope():
        names = ["lre", "lim", "ls", "K", "cre", "cim", "bre", "bim", "a", "th", "t1", "t2", "u", "rf", "mag", "Lre", "Lim",
                 "fre", "fim", "x1", "x2", "x3", "x4", "cm"]
        T = {n: P.sb([128, N]) for n in names}
        T["ti"] = P.sb([128, N], I32)
        idn = G["IDN_F"]
        pp = [P.ps([128, 128]) for _ in range(4)]
        npp = [0]
        mre, mim = G["S5C"][:, 0:1], G["S5C"][:, 1:2]
        for d in range(2):
            for hf in range(4):
                sl = slice(hf * N, (hf + 1) * N)
                for nm, src in (("lre", G["s5_lre"][d]), ("lim", G["s5_lim"][d]), ("ls", G["s5_ls"][d]), ("cre", G["s5_cre"]), ("cim", G["s5_cim"]),
                                ("bre", G["s5_bre"]), ("bim", G["s5_bim"])):
                    P.dma("sp", T[nm][:], src[:, sl], writes=["s5in_" + nm])
                ink = ["s5in_" + nm for nm in ("lre", "lim", "ls", "cre", "cim", "bre", "bim")]
                P.op("dve", lambda e: e.tensor_scalar(T["lre"][:], T["lre"][:], -1e-4, None, ALU.min), reads=ink, writes=["s5in_lre"])
                P.op("act", lambda e: e.activation(out=T["ls"][:], in_=T["ls"][:], func=AF.Exp), reads=ink, writes=["s5in_ls"])
                P.op("dve", lambda e: e.tensor_tensor(out=T["a"][:], in0=T["lre"][:], in1=T["ls"][:], op=ALU.mult), reads=["s5in_lre", "s5in_ls"], writes=["s5a"])
                P.op("dve", lambda e: e.tensor_tensor(out=T["th"][:], in0=T["lim"][:], in1=T["ls"][:], op=ALU.mult), reads=["s5in_lim", "s5in_ls", "s5a"], writes=["s5a"])
                av = T["a"][:].rearrange("p (g r) -> p g r", r=128)[:, :, 0]
                tv = T["th"][:].rearrange("p (g r) -> p g r", r=128)[:, :, 0]
                gs = slice(hf * 8, hf * 8 + 8)
                P.op("act", lambda e: e.activation(out=W["R8"][d][:, gs], in_=av, func=AF.Exp, scale=8.0), reads=["s5a"], writes=["R8"])
                P.op("dve", lambda e: e.tensor_scalar(T["u"][:, :8], tv, 8.0 / TWO_PI, None, ALU.mult), reads=["s5a"], writes=["u"])
                P.op("dve", lambda e: e.tensor_copy(T["ti"][:, :8], T["u"][:, :8]), reads=["u"], writes=["ti"])
                P.op("dve", lambda e: e.tensor_copy(T["rf"][:, :8], T["ti"][:, :8]), reads=["ti"], writes=["rf"])
                P.op("dve", lambda e: e.tensor_tensor(out=W["PSI"][d][:, gs], in0=T["u"][:, :8], in1=T["rf"][:, :8], op=ALU.subtract), reads=["u", "rf"], writes=["PSI"])
                P.op("dve", lambda e: e.memset(T["K"][:], 1.0), reads=["s5k"], writes=["s5k"])
                s5_cpow(P, T, T["K"], T["a"], T["th"], N)
                P.op("dve", lambda e: e.tensor_scalar(T["Lre"][:], T["Lre"][:], -1.0, None, ALU.add), reads=["Lre"], writes=["Lre"])
                P.op("dve", lambda e: e.tensor_tensor(out=T["x1"][:], in0=T["lre"][:], in1=T["lre"][:], op=ALU.mult), reads=["s5in_lre"], writes=["x1"])
                P.op("dve", lambda e: e.tensor_tensor(out=T["x2"][:], in0=T["lim"][:], in1=T["lim"][:], op=ALU.mult), reads=["s5in_lim"], writes=["x2"])
                P.op("dve", lambda e: e.tensor_tensor(out=T["x1"][:], in0=T["x1"][:], in1=T["x2"][:], op=ALU.add), reads=["x1", "x2"], writes=["x1"])
                P.op("dve", lambda e: e.reciprocal(T["x1"][:], T["x1"][:]), reads=["x1"], writes=["x1"])
                P.op("dve", lambda e: e.tensor_scalar(T["x2"][:], T["lim"][:], -1.0, None, ALU.mult), reads=["s5in_lim", "x2"], writes=["x2"])
                cmul(P, T["fre"][:], T["fim"][:], T["Lre"][:], T["Lim"][:], T["lre"][:], T["x2"][:], T["cm"][:], ["Lre", "Lim", "s5in_lre", "x2"], ["f"])
                P.op("dve", lambda e: e.tensor_tensor(out=T["fre"][:], in0=T["fre"][:], in1=T["x1"][:], op=ALU.mult), reads=["f", "x1"], writes=["f"])
                P.op("dve", lambda e: e.tensor_tensor(out=T["fim"][:], in0=T["fim"][:], in1=T["x1"][:], op=ALU.mult), reads=["f", "x1"], writes=["f"])
                P.dma("sp", T["K"][:], G["s5_ke"][d][:, sl], reads=["s5k"], writes=["s5k"])
                s5_cpow(P, T, T["K"], T["a"], T["th"], N)
                cmul(P, T["x1"][:], T["x2"][:], T["cre"][:], T["cim"][:], T["Lre"][:], T["Lim"][:], T["cm"][:], ["s5in_cre", "s5in_cim", "Lre", "Lim"], ["x1", "x2"])
                P.op("dve", lambda e: e.tensor_scalar(T["x2"][:], T["x2"][:], mim, None, ALU.mult), reads=["x2", "s5c"], writes=["x2"])
                P.op("dve", lambda e: e.scalar_tensor_tensor(out=T["x3"][:], in0=T["x1"][:], scalar=mre, in1=T["x2"][:], op0=ALU.mult, op1=ALU.subtract),
                     reads=["x1", "x2", "s5c"], writes=["x3"])
                Ev = W["E"][d][:, hf * 8:(hf + 1) * 8, :].rearrange("p g r -> p (g r)")
                P.op("act", lambda e: e.activation(out=Ev, in_=T["x3"][:], func=AF.Identity), reads=["x3"], writes=["WE"])
                for which in ("A", "G"):
                    P.dma("sp", T["K"][:], (G["s5_ka"] if which == "A" else G["s5_kg"])[d][:, sl], reads=["s5k"], writes=["s5k"])
                    s5_cpow(P, T, T["K"], T["a"], T["th"], N)
                    cmul(P, T["x1"][:], T["x2"][:], T["Lre"][:], T["Lim"][:], T["fre"][:], T["fim"][:], T["cm"][:], ["Lre", "Lim", "f"], ["x1", "x2"])
                    cmul(P, T["Lre"][:], T["Lim"][:], T["x1"][:], T["x2"][:], T["bre"][:], T["bim"][:], T["cm"][:], ["x1", "x2", "s5in_bre", "s5in_bim"], ["Lre", "Lim"])
                    if which == "A":
                        P.op("dve", lambda e: e.tensor_scalar(T["x1"][:], T["Lim"][:], mim, None, ALU.mult), reads=["Lim", "s5c"], writes=["x1"])
                        P.op("dve", lambda e: e.scalar_tensor_tensor(out=T["x4"][:], in0=T["Lre"][:], scalar=mre, in1=T["x1"][:], op0=ALU.mult, op1=ALU.add),
                             reads=["Lre", "x1", "s5c"], writes=["x4"])
                        for gg in range(8):
                            g = hf * 8 + gg
                            ps = pp[npp[0] % 4]
                            pk = ("s5pp", npp[0] % 4)
                            npp[0] += 1
                            P.op("pe", lambda e: e.matmul(ps[:], T["x4"][:, gg * 128:(gg + 1) * 128], T["x3"][:, gg * 128:(gg + 1) * 128], start=True, stop=True),
                                 reads=["x4", "x3"], writes=[pk])
                            if d == 0:
                                P.op("dve", lambda e: e.tensor_tensor(out=T["cm"][:, :128], in0=ps[:], in1=G["S5MASK"][d][:], op=ALU.mult), reads=[pk, "s5c"], writes=["cm_tmp"])
                                P.op("dve", lambda e: e.scalar_tensor_tensor(out=W["KIN"][d][:, g, :], in0=G["IDN_F"][:], scalar=G["S5D"][:, g:g + 1], in1=T["cm"][:, :128],
                                                                             op0=ALU.mult, op1=ALU.add), reads=["cm_tmp", "s5c"], writes=["WK"])
                            else:
                                P.op("dve", lambda e: e.tensor_tensor(out=W["KIN"][d][:, g, :], in0=ps[:], in1=G["S5MASK"][d][:], op=ALU.mult), reads=[pk, "s5c"], writes=["WK"])
                    else:
                        P.op("dve", lambda e: e.tensor_scalar(T["x1"][:], T["Lim"][:], mim, None, ALU.mult), reads=["Lim", "s5c"], writes=["x1"])
                        P.op("dve", lambda e: e.scalar_tensor_tensor(out=T["x4"][:], in0=T["Lre"][:], scalar=mre, in1=T["x1"][:], op0=ALU.mult, op1=ALU.add),
                             reads=["Lre", "x1", "s5c"], writes=["x4"])
                        P.op("dve", lambda e: e.tensor_scalar(T["x1"][:], T["Lre"][:], mim, None, ALU.mult), reads=["Lre", "s5c"], writes=["x1"])
                        P.op("dve", lambda e: e.scalar_tensor_tensor(out=T["x3"][:], in0=T["Lim"][:], scalar=mre, in1=T["x1"][:], op0=ALU.mult, op1=ALU.subtract),
                             reads=["Lim", "x1", "s5c"], writes=["x3"])
                        for (srcT, dstW, rk) in ((T["x4"], W["GA"][d], "x4"), (T["x3"], W["GB"][d], "x3")):
                            for gg in range(8):
                                g = hf * 8 + gg
                                ps = pp[npp[0] % 4]
                                pk = ("s5pp", npp[0] % 4)
                                npp[0] += 1
                                P.op("pe", lambda e: e.matmul(ps[:], srcT[:, gg * 128:(gg + 1) * 128], idn[:], start=True, stop=True), reads=[rk, "s5c"], writes=[pk])
                                P.op("act", lambda e: e.activation(out=dstW[:, g, :], in_=ps[:], func=AF.Identity), reads=[pk], writes=["WG"])


def phase_s5(P, G, T_L):
    NBL = T_L // 1024
    NL = T_L // 8
    NCH = NL + 32
    NBLK = NBL + 1
    segs = [(c0, min(512, NL - c0)) for c0 in range(0, NL, 512)] + [(NL, 32)]
    with P.scope():
        load_cst(P, G)
        W = {k: [P.sb([128, 32, 128], BF16) for _ in range(2)] for k in ("GA", "GB", "E", "KIN")}
        W["R8"] = [P.sb([128, 32]) for _ in range(2)]
        W["PSI"] = [P.sb([128, 32]) for _ in range(2)]
        s5_precompute(P, G, W)
        wk = ["WE", "WK", "WG", "R8", "PSI"]
        idb, jb, j32b = G["IDN_B"], G["JREV_B"], G["J32_B"]
        UT = P.sb([128, NBLK, 8, 128], BF16)
        YT = P.sb([128, NBLK, 8, 128])
        Usb = [P.sb([128, NCH], BF16) for _ in range(2)]
        QV = P.sb([128, NCH])
        P.dma("sp", QV[:], G["s5_qv"][:, :NCH], writes=["qv"])
        cosT, sinT, inA, inB, zA, zB = (P.sb([128, NCH]) for _ in range(6))
        tt = [P.sb([128, NCH]) for _ in range(4)]
        ti = P.sb([128, NCH], I32)
        R8t = P.sb([128, NCH])
        onesN = P.sb([128, NCH])
        P.op("pool", lambda e: e.memset(onesN[:], 1.0), writes=["onesN"])
        Xin = P.sb([128, NCH], BF16)
        Ysb = P.sb([128, NCH])
        T1 = [P.sb([128, 128]) for _ in range(2)]
        PU = [P.ps([128, 512]) for _ in range(2)]
        PV = [P.ps([128, 512]) for _ in range(4)]
        PY = [P.ps([128, 512]) for _ in range(2)]
        cnt = {"u": 0, "v": 0, "y": 0, "t1": 0}

        def blk_info(b):
            return (0, 32) if b == NBL else (T_C + b * 1024, 128)

        def blk_col(b, d):
            if b == NBL:
                return NL
            return (b if d == 0 else NBL - 1 - b) * 128

        for cc in range(4):
            for b in range(NBLK):
                t0, nb = blk_info(b)
                for gl in range(8):
                    src = G["p_tm"][t0:t0 + 8 * nb, cc * 128 + gl * 16:cc * 128 + gl * 16 + 16].rearrange("(n s) j -> n s j", s=8)
                    dst = UT[:nb, b, gl, :].rearrange("n (s j) -> n s j", j=16)
                    P.dma("pool", dst, src, reads=[("p_tm", "all")], writes=[("UT", b, gl)])
            for gl in range(8):
                g = cc * 8 + gl
                for d in range(2):
                    U = Usb[d]
                    uk = ("Usb", d)
                    for (c0, w) in segs:
                        pu = PU[cnt["u"] % 2]
                        puk = ("PU", cnt["u"] % 2)
                        cnt["u"] += 1
                        blks = [b for b in range(NBLK) if c0 <= blk_col(b, d) < c0 + w]
                        for b in blks:
                            t0, nb = blk_info(b)
                            col = blk_col(b, d) - c0
                            rhs = (idb[:nb, :nb] if d == 0 else (j32b[:, :] if nb == 32 else jb[:, :]))
                            P.op("pe", lambda e: e.matmul(pu[:, col:col + nb], UT[:nb, b, gl, :], rhs, start=True, stop=True),
                                 reads=[("UT", b, gl), "s5c"], writes=[puk])
                        P.op("act", lambda e: e.activation(out=U[:, c0:c0 + w], in_=pu[:, :w], func=AF.Identity), reads=[puk], writes=[uk])
                    psi = W["PSI"][d][:, g:g + 1]
                    P.op("dve", lambda e: e.tensor_scalar(tt[2][:], QV[:], psi, None, ALU.mult), reads=["qv"] + wk, writes=["tt2"])
                    for (off, dst, key) in ((0.0, sinT, "sinT"), (0.25, cosT, "cosT")):
                        P.op("dve", lambda e: e.tensor_scalar(tt[0][:], tt[2][:], off, None, ALU.add), reads=["tt2"], writes=["tt0"])
                        P.op("dve", lambda e: e.tensor_copy(ti[:], tt[0][:]), reads=["tt0"], writes=["ti5"])
                        P.op("dve", lambda e: e.tensor_copy(tt[1][:], ti[:]), reads=["ti5"], writes=["tt1"])
                        P.op("pool", lambda e: e.tensor_tensor(out=tt[0][:], in0=tt[0][:], in1=tt[1][:], op=ALU.subtract), reads=["tt0", "tt1"], writes=["tt0"])
                        P.op("act", lambda e: e.activation(out=dst[:], in_=tt[0][:], func=AF.Sin, scale=TWO_PI), reads=["tt0"], writes=[key])
                    P.op("pool", lambda e: e.tensor_scalar(R8t[:], onesN[:], W["R8"][d][:, g:g + 1], None, ALU.mult), reads=["onesN"] + wk, writes=["R8t"])
                    for (c0, w) in segs:
                        i = cnt["v"] % 2
                        cnt["v"] += 1
                        pva, pvb = PV[2 * i], PV[2 * i + 1]
                        ka, kb = ("PV", 2 * i), ("PV", 2 * i + 1)
                        P.op("pe", lambda e: e.matmul(pva[:, :w], W["GA"][d][:, g, :], U[:, c0:c0 + w], start=True, stop=True), reads=[uk] + wk, writes=[ka])
                        P.op("pe", lambda e: e.matmul(pvb[:, :w], W["GB"][d][:, g, :], U[:, c0:c0 + w], start=True, stop=True), reads=[uk] + wk, writes=[kb])
                        s = slice(c0, c0 + w)
                        P.op("dve", lambda e: e.tensor_tensor(out=tt[0][:, s], in0=pva[:, :w], in1=cosT[:, s], op=ALU.mult), reads=[ka, "cosT"], writes=["tt0"])
                        P.op("dve", lambda e: e.tensor_tensor(out=tt[1][:, s], in0=pvb[:, :w], in1=sinT[:, s], op=ALU.mult), reads=[kb, "sinT"], writes=["tt1"])
                        P.op("dve", lambda e: e.tensor_tensor(out=tt[2][:, s], in0=pvb[:, :w], in1=cosT[:, s], op=ALU.mult), reads=[kb, "cosT"], writes=["tt2"])
                        P.op("dve", lambda e: e.tensor_tensor(out=tt[3][:, s], in0=pva[:, :w], in1=sinT[:, s], op=ALU.mult), reads=[ka, "sinT"], writes=["tt3"])
                    P.op("pool", lambda e: e.tensor_tensor(out=inA[:], in0=tt[0][:], in1=tt[1][:], op=ALU.add), reads=["tt0", "tt1"], writes=["inA"])
                    P.op("pool", lambda e: e.tensor_tensor(out=inB[:], in0=tt[2][:], in1=tt[3][:], op=ALU.subtract), reads=["tt2", "tt3"], writes=["inB"])
                    for (src, z, key) in ((inA, zA, "zA"), (inB, zB, "zB")):
                        P.op("dve", lambda e: e.tensor_tensor_scan(out=z[:, NL:NCH], data0=R8t[:, NL:NCH], data1=src[:, NL:NCH], initial=0.0, op0=ALU.mult, op1=ALU.add),
                             reads=["R8t", "inA", "inB"], writes=[key])
                        P.op("dve", lambda e: e.tensor_tensor_scan(out=z[:, 0:NL], data0=R8t[:, 0:NL], data1=src[:, 0:NL], initial=z[:, NCH - 1:NCH], op0=ALU.mult, op1=ALU.add),
                             reads=["R8t", "inA", "inB", key], writes=[key])
                    P.op("pool", lambda e: e.tensor_tensor(out=tt[0][:], in0=zA[:], in1=cosT[:], op=ALU.mult), reads=["zA", "cosT"], writes=["tt0"])
                    P.op("pool", lambda e: e.tensor_tensor(out=tt[1][:], in0=zB[:], in1=sinT[:], op=ALU.mult), reads=["zB", "sinT"], writes=["tt1"])
                    P.op("dve", lambda e: e.memset(Xin[:, NL:NL + 1], 0.0), reads=[], writes=["Xin"])
                    P.op("dve", lambda e: e.tensor_tensor(out=Xin[:, NL + 1:NCH], in0=tt[0][:, NL:NCH - 1], in1=tt[1][:, NL:NCH - 1], op=ALU.subtract), reads=["tt0", "tt1"], writes=["Xin"])
                    P.op("dve", lambda e: e.tensor_tensor(out=Xin[:, 1:NL], in0=tt[0][:, 0:NL - 1], in1=tt[1][:, 0:NL - 1], op=ALU.subtract), reads=["tt0", "tt1"], writes=["Xin"])
                    P.op("dve", lambda e: e.tensor_tensor(out=Xin[:, 0:1], in0=tt[0][:, NCH - 1:NCH], in1=tt[1][:, NCH - 1:NCH], op=ALU.subtract), reads=["tt0", "tt1"], writes=["Xin"])
                    for (c0, w) in segs:
                        py = PY[cnt["y"] % 2]
                        pyk = ("PY", cnt["y"] % 2)
                        cnt["y"] += 1
                        P.op("pe", lambda e: e.matmul(py[:, :w], W["E"][d][:, g, :], Xin[:, c0:c0 + w], start=True, stop=False), reads=["Xin"] + wk, writes=[pyk])
                        P.op("pe", lambda e: e.matmul(py[:, :w], W["KIN"][d][:, g, :], U[:, c0:c0 + w], start=False, stop=True), reads=[uk] + wk, writes=[pyk])
                        P.op("act", lambda e: e.activation(out=Ysb[:, c0:c0 + w], in_=py[:, :w], func=AF.Identity), reads=[pyk], writes=["Ysb"])
                    for b in range(NBLK):
                        t0, nb = blk_info(b)
                        col = blk_col(b, d)
                        pu = PU[cnt["u"] % 2]
                        puk = ("PU", cnt["u"] % 2)
                        cnt["u"] += 1
                        P.op("pe", lambda e: e.matmul(pu[:nb, :128], Ysb[:, col:col + nb], G["IDN_F"][:], start=True, stop=True), reads=["Ysb", "s5c"], writes=[puk])
                        if d == 0:
                            P.op("dve", lambda e: e.tensor_copy(YT[:nb, b, gl, :], pu[:nb, :128]), reads=[puk], writes=[("YT", b, gl)])
                        else:
                            t1 = T1[cnt["t1"] % 2]
                            t1k = ("T1", cnt["t1"] % 2)
                            cnt["t1"] += 1
                            P.op("act", lambda e: e.activation(out=t1[:nb, :], in_=pu[:nb, :128], func=AF.Identity), reads=[puk], writes=[t1k])
                            pu2 = PU[cnt["u"] % 2]
                            pu2k = ("PU", cnt["u"] % 2)
                            cnt["u"] += 1
                            jf = G["J32_F"] if nb == 32 else G["JREV_F"]
                            P.op("pe", lambda e: e.matmul(pu2[:nb, :128], jf[:, :], t1[:nb, :], start=True, stop=True), reads=[t1k, "s5c"], writes=[pu2k])
                            P.op("dve", lambda e: e.tensor_tensor(out=YT[:nb, b, gl, :], in0=pu2[:nb, :128], in1=YT[:nb, b, gl, :], op=ALU.add),
                                 reads=[pu2k, ("YT", b, gl)], writes=[("YT", b, gl)])
            for b in range(NBLK):
                t0, nb = blk_info(b)
                for gl in range(8):
                    dst = G["y_tm"][t0:t0 + 8 * nb, cc * 128 + gl * 16:cc * 128 + gl * 16 + 16].rearrange("(n s) j -> n s j", s=8)
                    src = YT[:nb, b, gl, :].rearrange("n (s j) -> n s j", j=16)
                    P.dma("sp", dst, src, reads=[("YT", b, gl)], writes=[("y_tm", "all")])


def phase_mixout0(P, G, T_L):
    with P.scope():
        load_cst(P, G)
        C = common_setup(P)
        wout = P.sb([128, 8, 1024], BF16)
        for kc in range(8):
            P.dma("pool", wout[:, kc, :], G["ev_w_out"][kc * 128:(kc + 1) * 128, :], writes=[("wout", kc)])
        wglu = P.sb([128, 4, 512], BF16)
        for kc in range(4):
            P.dma("pool", wglu[:, kc, :], G["s5_w_glu"][kc * 128:(kc + 1) * 128, :], writes=[("wglu", kc)])
        poolw = P.sb([128, 4, 128], BF16)
        for gi in range(4):
            P.dma("pool", poolw[:, gi, :], G["pool_w"][gi * 128:(gi + 1) * 128, :], writes=[("poolw", gi)])
        mwl = P.sb([128, 4, 128], BF16)
        mwc = P.sb([128, 16, 128], BF16)
        P.dma("pool", mwl[:], G["MWL"].rearrange("g t u -> t g u"), writes=["mw"])
        P.dma("pool", mwc[:], G["MWC"].rearrange("g t u -> t g u"), writes=["mw"])
        x = P.sb([128, 8, TW])
        xk = "xtile"
        ytm = P.sb([128, 2, 512])
        w1 = P.sb([128, 2, 512])
        w2_ = P.sb([128, 2, 512])
        gT = P.sb([128, 4, TW], BF16)
        ymix = P.sb([128, 8, TW], BF16)
        sig = [P.sb([128, TW]) for _ in range(2)]
        u = P.sb([128, 4, TW], BF16)
        ztm = [P.sb([128, 128], BF16) for _ in range(2)]
        MOD, VEC = G["MOD"], G["VEC"]
        for (s0, m, i) in tiles_of(T_L):
            P.dma("sp", x[:], fm_tile(G["xres"], s0), reads=[("xres", s0)], writes=[xk])
            P.dma("sp", ytm[:], G["y_tm"][s0:s0 + TW, :].rearrange("(h t) c -> t h c", t=128), reads=[("y_tm", "all")], writes=["ytm"])
            P.dma("sp", u[:], G["ppool"][:, s0:s0 + TW].rearrange("(g p) t -> p g t", p=128), reads=[("ppool", "all")], writes=["u"])
            P.op("pool", lambda e: e.tensor_tensor(out=w1[:], in0=ytm[:], in1=ytm[:], op=ALU.mult), reads=["ytm"], writes=["w1"])
            P.op("dve", lambda e: e.tensor_scalar(w1[:], w1[:], 0.044715, 1.0, ALU.mult, ALU.add), reads=["w1"], writes=["w1"])
            P.op("dve", lambda e: e.tensor_tensor(out=w1[:], in0=w1[:], in1=ytm[:], op=ALU.mult), reads=["w1", "ytm"], writes=["w1"])
            P.op("act", lambda e: e.activation(out=w2_[:], in_=w1[:], func=AF.Sigmoid, scale=1.5957691216), reads=["w1"], writes=["w2_"])
            P.op("dve", lambda e: e.tensor_tensor(out=w2_[:], in0=w2_[:], in1=ytm[:], op=ALU.mult), reads=["w2_", "ytm"], writes=["w2_"])
            for c in range(4):
                n = C["n"]
                C["n"] += 1
                ps_ = C["pb"][n % 2]
                for hf in range(2):
                    P.op("pe", lambda e: e.matmul(ps_[:, hf * 128:(hf + 1) * 128], w2_[:, hf, c * 128:(c + 1) * 128], G["IDN_F"][:], start=True, stop=True),
                         reads=["w2_", "s5c"], writes=[("pb", n % 2)])
                P.op("act", lambda e: e.activation(out=gT[:, c, :], in_=ps_[:], func=AF.Identity), reads=[("pb", n % 2)], writes=[("gT", c)])
            for oc in range(4):
                n = C["n"]
                C["n"] += 1
                ps_, sg = C["pb"][n % 2], sig[n % 2]
                for c in range(4):
                    P.op("pe", lambda e: e.matmul(ps_[:], wglu[:, c, oc * 128:(oc + 1) * 128], gT[:, c, :], start=(c == 0), stop=(c == 3)),
                         reads=[("gT", c), ("wglu", c)], writes=[("pb", n % 2)])
                col = V_BGLU + oc
                P.op("act", lambda e: e.activation(out=sg[:], in_=ps_[:], func=AF.Sigmoid, bias=VEC[:, col:col + 1], scale=1.0), reads=[("pb", n % 2), "vec"], writes=[("sig", n % 2)])
                P.op("dve", lambda e: e.tensor_tensor(out=ymix[:, oc, :], in0=gT[:, oc, :], in1=sg[:], op=ALU.mult), reads=[("gT", oc), ("sig", n % 2)], writes=[("ymix", oc)])
            for gi in range(4):
                n = C["n"]
                C["n"] += 1
                py = C["py"][n % 2]
                for tb in range(2):
                    pz = C["pa"][tb]
                    P.op("pe", lambda e: e.matmul(pz[:, :128], u[:, gi, tb * 128:(tb + 1) * 128], poolw[:, gi, :], start=True, stop=True),
                         reads=["u", ("poolw", gi)], writes=[("pa", tb)])
                    P.op("act", lambda e: e.activation(out=ztm[tb][:], in_=pz[:, :128], func=AF.Identity), reads=[("pa", tb)], writes=[("ztm", tb)])
                if m == 0:
                    for tb in range(2):
                        P.op("pe", lambda e: e.matmul(py[:, tb * 128:(tb + 1) * 128], ztm[tb][:], mwl[:, gi, :], start=True, stop=True),
                             reads=[("ztm", tb), "mw"], writes=[("py", n % 2)])
                else:
                    for mb in range(2):
                        for kb in range(2):
                            P.op("pe", lambda e: e.matmul(py[:, mb * 128:(mb + 1) * 128], ztm[kb][:], mwc[:, gi * 4 + kb * 2 + mb, :], start=(kb == 0), stop=(kb == 1)),
                                 reads=[("ztm", kb), "mw"], writes=[("py", n % 2)])
                col = V_PSC + gi
                P.op("dve", lambda e: e.tensor_scalar(ymix[:, 4 + gi, :], py[:], VEC[:, col:col + 1], None, ALU.mult), reads=[("py", n % 2), "vec"], writes=[("ymix", 4 + gi)])
            for oc in range(8):
                n = C["n"]
                C["n"] += 1
                py = C["py"][n % 2]
                for c in range(8):
                    P.op("pe", lambda e: e.matmul(py[:], wout[:, c, oc * 128:(oc + 1) * 128], ymix[:, c, :], start=(c == 0), stop=(c == 7)),
                         reads=[("ymix", c), ("wout", c)], writes=[("py", n % 2)])
                P.op("dve", lambda e: e.scalar_tensor_tensor(out=x[:, oc, :], in0=py[:], scalar=MOD[:, 0, 5, oc, m:m + 1], in1=x[:, oc, :], op0=ALU.mult, op1=ALU.add),
                     reads=[("py", n % 2), "mod", xk], writes=[xk])
            P.dma("sp", fm_tile(G["xres"], s0), x[:], reads=[xk], writes=[("xres", s0)])


B_ALOG = 0
B_DTB = 16
NBC = 32


def gdn_col0(ch):
    return ch * 128 if ch < 16 else 2080 + (ch - 16) * 128


def phase_inproj1(P, G, T_L):
    with P.scope():
        C = common_setup(P)
        w = P.sb([128, 8, 4128], BF16)
        for kc in range(8):
            for q in range(3):
                P.dma("pool", w[:, kc, q * 1376:(q + 1) * 1376], G["gdn_w_in"][q][kc * 128:(kc + 1) * 128, :], writes=[("gw", kc, q)])
        wkeys = [("gw", kc, q) for kc in range(8) for q in range(3)]
        mx = derive(P, G, 1, 1, 3)
        x = P.sb([128, 8, TW])
        xk = "xtile"
        h2 = P.sb([128, 8, TW], BF16)
        po = [P.sb([128, TW]) for _ in range(2)]
        pz = [P.sb([128, 512]) for _ in range(2)]
        gt = P.sb([128, 16])
        nea = P.sb([128, 16])
        BC = G["BC"]
        GB = P.sb([128, (T_C + T_L) // 128, 32])
        P.op("act", lambda e: e.activation(out=nea[:], in_=BC[:, B_ALOG:B_ALOG + 16], func=AF.Exp), reads=["bc"], writes=["nea"])
        P.op("dve", lambda e: e.tensor_scalar(nea[:], nea[:], -1.0, None, ALU.mult), reads=["nea"], writes=["nea"])
        for (s0, m, i) in tiles_of(T_L):
            P.dma("sp", x[:], fm_tile(G["xres"], s0), reads=[("xres", s0)], writes=[xk])
            A, B, _ = mx[m]
            modulate(P, C, x, xk, A, B, "der", h2, "h2")
            hk = [("h2", c) for c in range(8)]
            for ch in range(24):
                n = C["n"]
                C["n"] += 1
                ps_, o = C["pb"][n % 2], po[n % 2]
                c0 = gdn_col0(ch)
                for c in range(8):
                    P.op("pe", lambda e: e.matmul(ps_[:], w[:, c, c0:c0 + 128], h2[:, c, :], start=(c == 0), stop=(c == 7)), reads=hk + wkeys, writes=[("pb", n % 2)])
                P.op("act" if ch % 2 else "dve", (lambda e: e.activation(out=o[:], in_=ps_[:], func=AF.Identity)) if ch % 2 else (lambda e: e.tensor_copy(o[:], ps_[:])),
                     reads=[("pb", n % 2)], writes=[("po", n % 2)])
                P.dma("sp", G["pkvq"][ch * 128:(ch + 1) * 128, s0:s0 + TW], o[:], reads=[("po", n % 2)], writes=[("pkvq", s0, ch)])
            for hf in range(2):
                blk = (s0 + hf * 128) // 128
                if m == 0:
                    for zc in range(2):
                        n = C["n"]
                        C["n"] += 1
                        ps_, o = C["pa"][n % 2], pz[n % 2]
                        for c in range(8):
                            P.op("pe", lambda e: e.matmul(ps_[:], h2[:, c, hf * 128:(hf + 1) * 128], w[:, c, 3104 + zc * 512:3104 + (zc + 1) * 512], start=(c == 0), stop=(c == 7)),
                                 reads=hk + wkeys, writes=[("pa", n % 2)])
                        P.op("act", lambda e: e.activation(out=o[:], in_=ps_[:], func=AF.Silu), reads=[("pa", n % 2)], writes=[("pz", n % 2)])
                        P.dma("sp", G["sz_tm"][s0 - T_C + hf * 128:s0 - T_C + (hf + 1) * 128, zc * 512:(zc + 1) * 512], o[:], reads=[("pz", n % 2)], writes=[("sz", s0, hf, zc)])
                n = C["n"]
                C["n"] += 1
                ps_ = C["py"][n % 2]
                for c in range(8):
                    P.op("pe", lambda e: e.matmul(ps_[:, :32], h2[:, c, hf * 128:(hf + 1) * 128], w[:, c, 2048:2080], start=(c == 0), stop=(c == 7)),
                         reads=hk + wkeys, writes=[("py", n % 2)])
                P.op("dve", lambda e: e.tensor_tensor(out=gt[:], in0=ps_[:, 0:16], in1=BC[:, B_DTB:B_DTB + 16], op=ALU.add), reads=[("py", n % 2), "bc"], writes=["gt"])
                P.op("act", lambda e: e.activation(out=gt[:], in_=gt[:], func=AF.Exp), reads=["gt"], writes=["gt"])
                P.op("act", lambda e: e.activation(out=gt[:], in_=gt[:], func=AF.Ln, bias=1.0, scale=1.0), reads=["gt"], writes=["gt"])
                P.op("dve", lambda e: e.tensor_tensor(out=GB[:, blk, 0:16], in0=gt[:], in1=nea[:], op=ALU.mult), reads=["gt", "nea"], writes=[("GB", blk)])
                P.op("act", lambda e: e.activation(out=GB[:, blk, 16:32], in_=ps_[:, 16:32], func=AF.Sigmoid), reads=[("py", n % 2)], writes=[("GB", blk)])
        P.dma("sp", G["gb_d"], GB[:].rearrange("p a b -> p (a b)"), reads=[("GB", b_) for b_ in range((T_C + T_L) // 128)], writes=["gb_d"])


def phase_conv(P, G, T_L):
    T = T_C + T_L
    NBK = T // 128
    LOFF = T_C + 9
    BW = T + 12
    with P.scope():
        load_cst(P, G)
        ones = P.sb([128, 128])
        P.op("dve", lambda e: e.memset(ones[:], 1.0), writes=["ones"])
        buf = [P.sb([128, BW]) for _ in range(2)]
        for b in range(2):
            P.op("pool", lambda e: e.memset(buf[b][:], 0.0), writes=[("cbuf", b)])
        acc = P.sb([128, T])
        actb = P.sb([128, T], BF16)
        sq = P.sb([128, 512])
        lnt = P.sb([128, 512])
        rstd = P.sb([128, 512])
        tmT = [P.sb([128, 4, 128], BF16) for _ in range(2)]
        pss = P.ps([128, 512])
        ptr = [P.ps([128, 4, 128]) for _ in range(2)]
        VEC = G["VEC"]
        segs = [(0, 3, T_C), (T_C, LOFF, T_L)]
        nt = 0
        for ch in range(24):
            b = buf[ch % 2]
            bk = ("cbuf", ch % 2)
            eng = "dve"
            for (s0, b0, L) in segs:
                P.dma("sp", b[:, b0:b0 + L], G["pkvq"][ch * 128:(ch + 1) * 128, s0:s0 + L], reads=[("pkvq", "all")], writes=[bk])
            for (s0, b0, L) in segs:
                for tap in range(7):
                    col = V_CONV + ch * 7 + tap
                    src = b[:, b0 + tap - 3:b0 + tap - 3 + L]
                    if tap == 0:
                        P.op(eng, lambda e: e.tensor_scalar(acc[:, s0:s0 + L], src, VEC[:, col:col + 1], None, ALU.mult), reads=[bk, "vec"], writes=["acc"])
                    else:
                        P.op(eng, lambda e: e.scalar_tensor_tensor(out=acc[:, s0:s0 + L], in0=src, scalar=VEC[:, col:col + 1], in1=acc[:, s0:s0 + L], op0=ALU.mult, op1=ALU.add),
                             reads=[bk, "vec", "acc"], writes=["acc"])
            P.op("act", lambda e: e.activation(out=acc[:], in_=acc[:], func=AF.Silu), reads=["acc"], writes=["acc"])
            kind = "k" if ch < 8 else ("v" if ch < 16 else "q")
            if kind == "v":
                P.op("act", lambda e: e.activation(out=actb[:], in_=acc[:], func=AF.Identity), reads=["acc"], writes=["actb"])
            else:
                qs = 1.0 if kind == "k" else 128.0 ** -0.5
                for t0 in range(0, T, 512):
                    L = min(512, T - t0)
                    P.op("act", lambda e: e.activation(out=sq[:, :L], in_=acc[:, t0:t0 + L], func=AF.Square), reads=["acc"], writes=["csq"])
                    P.op("pe", lambda e: e.matmul(pss[:, :L], ones[:], sq[:, :L], start=True, stop=True), reads=["csq", "ones"], writes=["pss"])
                    P.op("act", lambda e: e.activation(out=lnt[:, :L], in_=pss[:, :L], func=AF.Ln, bias=1e-6, scale=1.0), reads=["pss"], writes=["clnt"])
                    P.op("act", lambda e: e.activation(out=rstd[:, :L], in_=lnt[:, :L], func=AF.Exp, scale=-0.5), reads=["clnt"], writes=["crstd"])
                    P.op("dve", lambda e: e.scalar_tensor_tensor(out=actb[:, t0:t0 + L], in0=acc[:, t0:t0 + L], scalar=qs, in1=rstd[:, :L], op0=ALU.mult, op1=ALU.mult),
                         reads=["acc", "crstd"], writes=["actb"])
                hh = ch if kind == "k" else ch - 16
                dst = G["KT"] if kind == "k" else G["QT"]
                P.dma("sp", dst[hh * 128:(hh + 1) * 128, :], actb[:], reads=["actb"], writes=[("KQT", kind, hh)])
            if kind != "q":
                hh = ch if kind == "k" else ch - 8
                dst = G["K_tm"] if kind == "k" else G["V_tm"]
                for b0 in range(0, NBK, 4):
                    nbk = min(4, NBK - b0)
                    pt = ptr[nt % 2]
                    tm = tmT[nt % 2]
                    pk, tk = ("ptr", nt % 2), ("tmT", nt % 2)
                    nt += 1
                    for j in range(nbk):
                        P.op("pe", lambda e: e.matmul(pt[:, j, :], actb[:, (b0 + j) * 128:(b0 + j + 1) * 128], G["IDN_B"][:], start=True, stop=True),
                             reads=["actb", "s5c"], writes=[pk])
                    P.op("dve" if nt % 2 else "act", (lambda e: e.tensor_copy(tm[:, :nbk, :], pt[:, :nbk, :])) if nt % 2 else
                         (lambda e: e.activation(out=tm[:, :nbk, :], in_=pt[:, :nbk, :], func=AF.Identity)), reads=[pk], writes=[tk])
                    P.dma("sp", dst[b0 * 128:(b0 + nbk) * 128, hh * 128:(hh + 1) * 128].rearrange("(j t) c -> t j c", t=128), tm[:, :nbk, :], reads=[tk], writes=[("KVtm", kind, hh, b0)])


def phase_gdn(P, G, T_L):
    NBK = (T_C + T_L) // 128
    NCB = T_C // 128
    with P.scope():
        load_cst(P, G)
        c4 = P.sb([128, 4, 512])
        P.dma("sp", c4[:], G["cst4"].rearrange("p (a b) -> p a b", b=512), writes=["s5c"])
        G["BD32x4"], G["OFF64x4"], G["OFF128x4"], G["IDNx4"] = c4[:, 0, :], c4[:, 1, :], c4[:, 2, :], c4[:, 3, :]
        GB = P.sb([128, NBK, 32])
        P.dma("sp", GB[:].rearrange("p a b -> p (a b)"), G["gb_d"], reads=["gb_d"], writes=[("GB", b_) for b_ in range(NBK)])
        idf, idb = G["IDN_F"], G["IDN_B"]
        onesf, nonesf = G["ONES_F"], G["NEGONES_F"]
        KTc = [P.sb([128, 8, 128], BF16) for _ in range(2)]
        QTc = [P.sb([128, 8, 128], BF16) for _ in range(2)]
        Ktm = [P.sb([128, 1024], BF16) for _ in range(2)]
        Vtm = [P.sb([128, 1024], BF16) for _ in range(2)]
        EX = P.sb([128, 24])
        negA = P.sb([128, 8])
        GT = P.sb([128, 4, 128])
        DT = P.sb([128, 4, 128])
        DTs = P.sb([128, 4, 128])
        Wt = P.sb([128, 4, 128])
        Wl = P.sb([128, 4, 128])
        W1t = P.sb([128, 4, 128])
        W1l = P.sb([128, 4, 128])
        W2l = P.sb([128, 4, 128])
        Ak = [P.sb([128, 4, 128]) for _ in range(2)]
        Bk = [P.sb([128, 4, 128]) for _ in range(2)]
        PtG = P.sb([128, 4, 128])
        PlG = P.sb([128, 4, 128])
        Ptb = P.sb([128, 8, 128], BF16)
        Aqk = P.sb([128, 8, 128], BF16)
        R = P.sb([128, 8, 128], BF16)
        vnew = P.sb([128, 8, 128], BF16)
        Kd = P.sb([128, 8, 128], BF16)
        O2 = P.sb([128, 4, 128])
        otile = [P.sb([128, 1024]) for _ in range(2)]
        Sf = P.sb([128, 8, 128])
        Sb = P.sb([128, 8, 128], BF16)
        pool = [P.ps([128, 4, 128]) for _ in range(8)]
        pn = [0]
        ce = [0]

        def nxt():
            i = pn[0] % 8
            pn[0] += 1
            return pool[i], ("gps", i)

        def ev():
            ce[0] += 1
            return ("dve", "pool")[ce[0] % 2]

        def load_chunk(blk, slot, lat):
            P.dma("sp", KTc[slot][:], G["KT"][:, blk * 128:(blk + 1) * 128].rearrange("(h p) t -> p h t", p=128), reads=[("KQT", "all")], writes=[("KTc", slot)])
            P.dma("sp", Ktm[slot][:], G["K_tm"][blk * 128:(blk + 1) * 128, :], reads=[("KVtm", "all")], writes=[("Ktm", slot)])
            P.dma("sp", Vtm[slot][:], G["V_tm"][blk * 128:(blk + 1) * 128, :], reads=[("KVtm", "all")], writes=[("Vtm", slot)])
            if lat:
                P.dma("sp", QTc[slot][:], G["QT"][:, blk * 128:(blk + 1) * 128].rearrange("(h p) t -> p h t", p=128), reads=[("KQT", "all")], writes=[("QTc", slot)])

        nchunk = 0
        for d in range(2):
            TRI, TRIS, MNEG, STR = G["TRI"][d], G["TRIS"][d], G["MASKNEG"][d], G["STRICT01"][d]
            order = list(range(NBK)) if d == 0 else ([NCB - 1 - i for i in range(NCB)] + [NBK - 1 - i for i in range(NBK - NCB)])
            P.op("pool", lambda e: e.memset(Sf[:], 0.0), reads=["Sf"], writes=["Sf"])
            P.op("pool", lambda e: e.memset(Sb[:], 0.0), reads=["Sb"], writes=["Sb"])
            load_chunk(order[0], nchunk % 2, order[0] >= NCB)
            for oi, blk in enumerate(order):
                slot = nchunk % 2
                nchunk += 1
                lat = blk >= NCB
                if oi + 1 < len(order):
                    load_chunk(order[oi + 1], nchunk % 2, order[oi + 1] >= NCB)
                kT, kTk = KTc[slot], ("KTc", slot)
                qT, qTk = QTc[slot], ("QTc", slot)
                gc = GB[:, blk, d * 8:(d + 1) * 8]
                bc = GB[:, blk, 16 + d * 8:16 + (d + 1) * 8]
                gbk = ("GB", blk)
                ps1, k1 = nxt()
                p1 = ps1[:].rearrange("p a b -> p (a b)")
                P.op("pe", lambda e: e.matmul(p1[:, 0:8], TRI[:], gc, start=True, stop=True), reads=[gbk, "s5c"], writes=[k1])
                P.op("pe", lambda e: e.matmul(p1[:, 8:16], TRIS[:], gc, start=True, stop=True), reads=[gbk, "s5c"], writes=[k1])
                P.op("pe", lambda e: e.matmul(p1[:, 16:24], onesf[:], gc, start=True, stop=True), reads=[gbk, "s5c"], writes=[k1])
                P.op("act", lambda e: e.activation(out=EX[:], in_=p1[:, 0:24], func=AF.Exp), reads=[k1], writes=["EX"])
                P.op("dve", lambda e: e.tensor_scalar(negA[:], EX[:, 0:8], -1.0, None, ALU.mult), reads=["EX"], writes=["negA"])
                for hg in range(2):
                    hs = [hg * 4 + j for j in range(4)]
                    for j, h in enumerate(hs):
                        P.op(ev(), lambda e: e.tensor_scalar(GT[:, j, :], TRI[:], gc[:, h:h + 1], None, ALU.mult), reads=[gbk, "s5c"], writes=[("GT", j)])
                    pd, kd_ = nxt()
                    for j, h in enumerate(hs):
                        P.op("pe", lambda e: e.matmul(pd[:, j, :], onesf[:], GT[:, j, :], start=True, stop=False), reads=[("GT", j), "s5c"], writes=[kd_])
                        P.op("pe", lambda e: e.matmul(pd[:, j, :], GT[:, j, :], nonesf[:], start=False, stop=True), reads=[("GT", j), "s5c"], writes=[kd_])
                    for j in range(4):
                        P.op("dve", lambda e: e.tensor_tensor(out=DT[:, j, :], in0=pd[:, j, :], in1=MNEG[:], op=ALU.add), reads=[kd_, "s5c"], writes=["DT"])
                    P.op("act", lambda e: e.activation(out=DT[:], in_=DT[:], func=AF.Exp), reads=["DT"], writes=["DT"])
                    for j in range(4):
                        P.op("pool", lambda e: e.tensor_tensor(out=DTs[:, j, :], in0=DT[:, j, :], in1=STR[:], op=ALU.mult), reads=["DT", "s5c"], writes=["DTs"])
                    pk, kk = nxt()
                    for j, h in enumerate(hs):
                        P.op("pe", lambda e: e.matmul(pk[:, j, :], kT[:, h, :], kT[:, h, :], start=True, stop=True), reads=[kTk], writes=[kk])
                    for j, h in enumerate(hs):
                        P.op("dve", lambda e: e.scalar_tensor_tensor(out=Wt[:, j, :], in0=pk[:, j, :], scalar=bc[:, h:h + 1], in1=DTs[:, j, :], op0=ALU.mult, op1=ALU.mult),
                             reads=[kk, gbk, "DTs"], writes=["Wt"])
                    pw, kw = nxt()
                    for j in range(4):
                        P.op("pe", lambda e: e.matmul(pw[:, j, :], Wt[:, j, :], idf[:], start=True, stop=True), reads=["Wt", "s5c"], writes=[kw])
                    P.op("act", lambda e: e.activation(out=Wl[:], in_=pw[:], func=AF.Identity), reads=[kw], writes=["Wl"])
                    if lat:
                        pq, kq = nxt()
                        for j, h in enumerate(hs):
                            P.op("pe", lambda e: e.matmul(pq[:, j, :], kT[:, h, :], qT[:, h, :], start=True, stop=True), reads=[kTk, qTk], writes=[kq])
                        P.op("dve", lambda e: e.tensor_tensor(out=Aqk[:, hg * 4:hg * 4 + 4, :], in0=pq[:], in1=DT[:], op=ALU.mult), reads=[kq, "DT"], writes=[("Aqk", hg)])
                    BD, MO1, MO2, I4 = G["BD32x4"], G["OFF64x4"], G["OFF128x4"], G["IDNx4"]
                    f = lambda t: t[:].rearrange("p a b -> p (a b)")
                    P.op("pool", lambda e: e.tensor_tensor(out=f(Ak[0]), in0=f(Wt), in1=BD, op=ALU.mult), reads=["Wt", "s5c"], writes=[("Ak", 0)])
                    P.op("pool", lambda e: e.tensor_tensor(out=f(Bk[0]), in0=f(Wl), in1=BD, op=ALU.mult), reads=["Wl", "s5c"], writes=[("Bk", 0)])
                    P.op("pool", lambda e: e.tensor_tensor(out=f(W1t), in0=f(Wt), in1=MO1, op=ALU.mult), reads=["Wt", "s5c"], writes=["W1t"])
                    P.op("pool", lambda e: e.tensor_tensor(out=f(W1l), in0=f(Wl), in1=MO1, op=ALU.mult), reads=["Wl", "s5c"], writes=["W1l"])
                    P.op("pool", lambda e: e.tensor_tensor(out=f(W2l), in0=f(Wl), in1=MO2, op=ALU.mult), reads=["Wl", "s5c"], writes=["W2l"])
                    P.op("dve", lambda e: e.tensor_tensor(out=f(PtG), in0=I4, in1=f(Ak[0]), op=ALU.subtract), reads=[("Ak", 0), "s5c"], writes=["PtG"])
                    P.op("dve", lambda e: e.tensor_tensor(out=f(PlG), in0=I4, in1=f(Bk[0]), op=ALU.subtract), reads=[("Bk", 0), "s5c"], writes=["PlG"])
                    Ap, Apk, Bp, Bpk = Ak[0], ("Ak", 0), Bk[0], ("Bk", 0)
                    for lev in range(1, 5):
                        Bn, Bnk = Bk[lev % 2], ("Bk", lev % 2)
                        pb_, kb_ = nxt()
                        for j in range(4):
                            P.op("pe", lambda e: e.matmul(pb_[:, j, :], Ap[:, j, :], Bp[:, j, :], start=True, stop=True), reads=[Apk, Bpk], writes=[kb_])
                        if lev < 4:
                            An, Ank = Ak[lev % 2], ("Ak", lev % 2)
                            pa_, ka_ = nxt()
                            for j in range(4):
                                P.op("pe", lambda e: e.matmul(pa_[:, j, :], Bp[:, j, :], Ap[:, j, :], start=True, stop=True), reads=[Apk, Bpk], writes=[ka_])
                        P.op("act", lambda e: e.activation(out=Bn[:], in_=pb_[:], func=AF.Identity), reads=[kb_], writes=[Bnk])
                        if lev < 4:
                            P.op("dve", lambda e: e.tensor_copy(An[:], pa_[:]), reads=[ka_], writes=[Ank])
                        pu_, ku_ = nxt()
                        pl_, kl_ = nxt()
                        for j in range(4):
                            P.op("pe", lambda e: e.matmul(pu_[:, j, :], Bn[:, j, :], PtG[:, j, :], start=True, stop=True), reads=[Bnk, "PtG"], writes=[ku_])
                            P.op("pe", lambda e: e.matmul(pl_[:, j, :], PtG[:, j, :], Bn[:, j, :], start=True, stop=True), reads=[Bnk, "PtG"], writes=[kl_])
                        P.op("dve", lambda e: e.tensor_tensor(out=PtG[:], in0=pu_[:], in1=PtG[:], op=ALU.add), reads=[ku_, "PtG"], writes=["PtG"])
                        P.op("dve", lambda e: e.tensor_tensor(out=PlG[:], in0=pl_[:], in1=PlG[:], op=ALU.add), reads=[kl_, "PlG"], writes=["PlG"])
                        if lev < 4:
                            Ap, Apk = An, Ank
                        Bp, Bpk = Bn, Bnk
                    px, kx = nxt()
                    px2, kx2 = nxt()
                    for j in range(4):
                        P.op("pe", lambda e: e.matmul(px[:, j, :], W1l[:, j, :], PtG[:, j, :], start=True, stop=True), reads=["W1l", "PtG"], writes=[kx])
                        P.op("pe", lambda e: e.matmul(px2[:, j, :], W1t[:, j, :], PlG[:, j, :], start=True, stop=True), reads=["W1t", "PlG"], writes=[kx2])
                    P.op("act", lambda e: e.activation(out=Ak[0][:], in_=px[:], func=AF.Identity), reads=[kx], writes=[("Ak", 0)])
                    P.op("dve", lambda e: e.tensor_copy(Ak[1][:], px2[:]), reads=[kx2], writes=[("Ak", 1)])
                    py_, ky_ = nxt()
                    py2, ky2 = nxt()
                    for j in range(4):
                        P.op("pe", lambda e: e.matmul(py_[:, j, :], PlG[:, j, :], Ak[0][:, j, :], start=True, stop=True), reads=["PlG", ("Ak", 0)], writes=[ky_])
                        P.op("pe", lambda e: e.matmul(py2[:, j, :], PtG[:, j, :], Ak[1][:, j, :], start=True, stop=True), reads=["PtG", ("Ak", 1)], writes=[ky2])
                    P.op("dve", lambda e: e.tensor_tensor(out=PtG[:], in0=PtG[:], in1=py_[:], op=ALU.subtract), reads=[ky_, "PtG"], writes=["PtG"])
                    P.op("dve", lambda e: e.tensor_tensor(out=PlG[:], in0=PlG[:], in1=py2[:], op=ALU.subtract), reads=[ky2, "PlG"], writes=["PlG"])
                    px, kx = nxt()
                    for j in range(4):
                        P.op("pe", lambda e: e.matmul(px[:, j, :], W2l[:, j, :], PtG[:, j, :], start=True, stop=True), reads=["W2l", "PtG"], writes=[kx])
                    P.op("act", lambda e: e.activation(out=Ak[0][:], in_=px[:], func=AF.Identity), reads=[kx], writes=[("Ak", 0)])
                    py_, ky_ = nxt()
                    for j in range(4):
                        P.op("pe", lambda e: e.matmul(py_[:, j, :], PlG[:, j, :], Ak[0][:, j, :], start=True, stop=True), reads=["PlG", ("Ak", 0)], writes=[ky_])
                    P.op("dve", lambda e: e.tensor_tensor(out=Ptb[:, hg * 4:hg * 4 + 4, :], in0=PtG[:], in1=py_[:], op=ALU.subtract), reads=[ky_, "PtG"], writes=[("Ptb", hg)])
                ot = otile[slot]
                otk = ("otile", slot)
                for hg in range(2):
                    hs = [hg * 4 + j for j in range(4)]
                    pm, km = nxt()
                    for j, h in enumerate(hs):
                        P.op("pe", lambda e: e.matmul(pm[:, j, :], kT[:, h, :], Sb[:, h, :], start=True, stop=True), reads=[kTk, "Sb"], writes=[km])
                    if lat:
                        po1, ko1 = nxt()
                        for j, h in enumerate(hs):
                            P.op("pe", lambda e: e.matmul(po1[:, j, :], qT[:, h, :], Sb[:, h, :], start=True, stop=True), reads=[qTk, "Sb"], writes=[ko1])
                    for j, h in enumerate(hs):
                        P.op("dve", lambda e: e.scalar_tensor_tensor(out=R[:, h, :], in0=pm[:, j, :], scalar=negA[:, h:h + 1], in1=Vtm[slot][:, h * 128:(h + 1) * 128], op0=ALU.mult, op1=ALU.add),
                             reads=[km, "negA", ("Vtm", slot)], writes=[("R", hg)])
                        P.op("pool", lambda e: e.tensor_scalar(Kd[:, h, :], Ktm[slot][:, h * 128:(h + 1) * 128], EX[:, 8 + h:9 + h], None, ALU.mult), reads=[("Ktm", slot), "EX"], writes=[("Kd", hg)])
                    pv, kv = nxt()
                    for j, h in enumerate(hs):
                        P.op("pe", lambda e: e.matmul(pv[:, j, :], Ptb[:, h, :], R[:, h, :], start=True, stop=True), reads=[("Ptb", hg), ("R", hg)], writes=[kv])
                    for j, h in enumerate(hs):
                        P.op("dve", lambda e: e.tensor_scalar(vnew[:, h, :], pv[:, j, :], bc[:, h:h + 1], None, ALU.mult), reads=[kv, gbk], writes=[("vnew", hg)])
                    if lat:
                        po2, ko2 = nxt()
                        for j, h in enumerate(hs):
                            P.op("pe", lambda e: e.matmul(po2[:, j, :], Aqk[:, h, :], vnew[:, h, :], start=True, stop=True), reads=[("Aqk", hg), ("vnew", hg)], writes=[ko2])
                        P.op("act", lambda e: e.activation(out=O2[:], in_=po2[:], func=AF.Identity), reads=[ko2], writes=["O2"])
                        for j, h in enumerate(hs):
                            P.op("dve", lambda e: e.scalar_tensor_tensor(out=ot[:, h * 128:(h + 1) * 128], in0=po1[:, j, :], scalar=EX[:, h:h + 1], in1=O2[:, j, :], op0=ALU.mult, op1=ALU.add),
                                 reads=[ko1, "EX", "O2"], writes=[otk])
                    pd2, kd2 = nxt()
                    for j, h in enumerate(hs):
                        P.op("pe", lambda e: e.matmul(pd2[:, j, :], Kd[:, h, :], vnew[:, h, :], start=True, stop=True), reads=[("Kd", hg), ("vnew", hg)], writes=[kd2])
                    for j, h in enumerate(hs):
                        P.op("dve", lambda e: e.scalar_tensor_tensor(out=Sf[:, h, :], in0=Sf[:, h, :], scalar=EX[:, 16 + h:17 + h], in1=pd2[:, j, :], op0=ALU.mult, op1=ALU.add),
                             reads=["Sf", "EX", kd2], writes=["Sf"])
                P.op("act", lambda e: e.activation(out=Sb[:], in_=Sf[:], func=AF.Identity), reads=["Sf"], writes=["Sb"])
                if lat:
                    od = G["o_f"] if d == 0 else G["o_b"]
                    P.dma("sp", od[(blk - NCB) * 128:(blk - NCB + 1) * 128, :], ot[:], reads=[otk], writes=[("o", d, blk)])


def phase_mixout1(P, G, T_L):
    with P.scope():
        load_cst(P, G)
        C = common_setup(P)
        wout = P.sb([128, 8, 1024], BF16)
        for kc in range(8):
            P.dma("pool", wout[:, kc, :], G["gdn_w_out"][kc * 128:(kc + 1) * 128, :], writes=[("wout", kc)])
        x = P.sb([128, 8, TW])
        xk = "xtile"
        of = P.sb([128, 2, 1024])
        ob = P.sb([128, 2, 1024])
        sz = P.sb([128, 2, 1024])
        sq = P.sb([128, 2, 1024])
        ss = P.sb([128, 16])
        ymix = P.sb([128, 8, TW], BF16)
        MOD = G["MOD"]
        NGB = P.sb([128, 1024])
        P.dma("sp", NGB[:], G["ngbc"], writes=["bc"])
        for (s0, m, i) in tiles_of(T_L):
            if m == 1:
                continue
            l0 = s0 - T_C
            P.dma("sp", x[:], fm_tile(G["xres"], s0), reads=[("xres", s0)], writes=[xk])
            P.dma("sp", of[:], G["o_f"][l0:l0 + TW, :].rearrange("(h t) c -> t h c", t=128), reads=[("o", "all")], writes=["of"])
            P.dma("sp", ob[:], G["o_b"][l0:l0 + TW, :].rearrange("(h t) c -> t h c", t=128), reads=[("o", "all")], writes=["ob"])
            P.dma("sp", sz[:], G["sz_tm"][l0:l0 + TW, :].rearrange("(h t) c -> t h c", t=128), reads=[("sz", "all")], writes=["szt"])
            P.op("pool", lambda e: e.tensor_tensor(out=of[:], in0=of[:], in1=ob[:], op=ALU.add), reads=["of", "ob"], writes=["of"])
            P.op("dve", lambda e: e.tensor_tensor(out=sq[:], in0=of[:], in1=of[:], op=ALU.mult), reads=["of"], writes=["sq1"])
            P.op("dve", lambda e: e.tensor_reduce(out=ss[:], in_=sq[:].rearrange("p a (h v) -> p (a h) v", v=128), axis=mybir.AxisListType.X, op=ALU.add), reads=["sq1"], writes=["ss1"])
            P.op("act", lambda e: e.activation(out=ss[:], in_=ss[:], func=AF.Ln, bias=1e-6, scale=1.0 / 128), reads=["ss1"], writes=["ss1"])
            P.op("act", lambda e: e.activation(out=ss[:], in_=ss[:], func=AF.Exp, scale=-0.5), reads=["ss1"], writes=["ss1"])
            for hf in range(2):
                for h in range(8):
                    P.op("dve" if h % 2 else "pool", lambda e: e.tensor_scalar(of[:, hf, h * 128:(h + 1) * 128], of[:, hf, h * 128:(h + 1) * 128], ss[:, hf * 8 + h:hf * 8 + h + 1], None, ALU.mult),
                         reads=["of", "ss1"], writes=["of"])
                P.op("dve", lambda e: e.tensor_tensor(out=of[:, hf, :], in0=of[:, hf, :], in1=NGB[:], op=ALU.mult), reads=["of", "bc"], writes=["of"])
            P.op("pool", lambda e: e.tensor_tensor(out=of[:], in0=of[:], in1=sz[:], op=ALU.mult), reads=["of", "szt"], writes=["of"])
            for c in range(8):
                n = C["n"]
                C["n"] += 1
                ps_ = C["pb"][n % 2]
                for hf in range(2):
                    P.op("pe", lambda e: e.matmul(ps_[:, hf * 128:(hf + 1) * 128], of[:, hf, c * 128:(c + 1) * 128], G["IDN_F"][:], start=True, stop=True),
                         reads=["of", "s5c"], writes=[("pb", n % 2)])
                P.op("act", lambda e: e.activation(out=ymix[:, c, :], in_=ps_[:], func=AF.Identity), reads=[("pb", n % 2)], writes=[("ymix", c)])
            for oc in range(8):
                n = C["n"]
                C["n"] += 1
                py = C["py"][n % 2]
                for c in range(8):
                    P.op("pe", lambda e: e.matmul(py[:], wout[:, c, oc * 128:(oc + 1) * 128], ymix[:, c, :], start=(c == 0), stop=(c == 7)),
                         reads=[("ymix", c), ("wout", c)], writes=[("py", n % 2)])
                P.op("dve", lambda e: e.scalar_tensor_tensor(out=x[:, oc, :], in0=py[:], scalar=MOD[:, 1, 5, oc, 0:1], in1=x[:, oc, :], op0=ALU.mult, op1=ALU.add),
                     reads=[("py", n % 2), "mod", xk], writes=[xk])
            P.dma("sp", fm_tile(G["xres"], s0), x[:], reads=[xk], writes=[("xres", s0)])


CST_MAP = {"IDN_F": 0, "JREV_F": 128, "ONES_F": 256, "NEGONES_F": 384, "TRI0": 512, "TRI1": 640, "TRIS0": 768, "TRIS1": 896,
           "MNEG0": 1024, "MNEG1": 1152, "STR0": 1280, "STR1": 1408, "S5M0": 1536, "S5M1": 1664}
C_S5C = 1792
C_S5D = 1796
C_J32 = 1828
NCST = 1860


def load_cst(P, G):
    cst = P.sb([128, NCST])
    P.dma("sp", cst[:], G["cst"], writes=["s5c"])
    for k, o in CST_MAP.items():
        G[k] = cst[:, o:o + 128]
    G["TRI"] = [G["TRI0"], G["TRI1"]]
    G["TRIS"] = [G["TRIS0"], G["TRIS1"]]
    G["MASKNEG"] = [G["MNEG0"], G["MNEG1"]]
    G["STRICT01"] = [G["STR0"], G["STR1"]]
    G["S5MASK"] = [G["S5M0"], G["S5M1"]]
    G["S5C"] = cst[:, C_S5C:C_S5C + 4]
    G["S5D"] = cst[:, C_S5D:C_S5D + 32]
    G["J32_F"] = cst[0:32, C_J32:C_J32 + 32]
    cb = P.sb([128, 288], BF16)
    P.op("act", lambda e: e.activation(out=cb[:, 0:256], in_=cst[:, 0:256], func=AF.Identity), reads=["s5c"], writes=["s5c"])
    P.op("act", lambda e: e.activation(out=cb[0:32, 256:288], in_=cst[0:32, C_J32:C_J32 + 32], func=AF.Identity), reads=["s5c"], writes=["s5c"])
    G["IDN_B"] = cb[:, 0:128]
    G["JREV_B"] = cb[:, 128:256]
    G["J32_B"] = cb[0:32, 256:288]


def build_program(T_L):
    T = T_C + T_L
    NCH = T // 8
    P = Prog()
    G = {"dern": 0}
    G["xT"] = [P.din(f"xT{c}", [128, T_L]) for c in range(8)]
    G["ctxT"] = P.din("ctxT", [1024, T_C])
    G["vec_d"] = P.din("vec", [128, NV])
    G["bc_d"] = P.din("bc", [128, NBC])
    G["cst"] = P.din("cst", [128, NCST])
    G["ngbc"] = P.din("ngbc", [128, 1024])
    G["cst4"] = P.din("cst4", [128, 2048])
    G["wm"] = [[P.din(f"wm{l}_{k}", [1024, 1024]) for k in range(9)] for l in range(2)]
    G["w13"] = [[P.din(f"w13_{f}_{q}", [1024, 1408]) for q in range(4)] for f in range(4)]
    G["w2"] = [[P.din(f"w2_{f}_{q}", [1408, 1024]) for q in range(2)] for f in range(4)]
    for nm, shp in (("ev_w_in", [1024, 1024]), ("ev_w_out", [1024, 1024]), ("s5_w_glu", [512, 512]), ("pool_w", [512, 128]),
                    ("MWL", [4, 128, 128]), ("MWC", [16, 128, 128]), ("gdn_w_out", [1024, 1024]), ("s5_qv", [128, NCH]),
                    ("s5_cre", [128, 4096]), ("s5_cim", [128, 4096]), ("s5_bre", [128, 4096]), ("s5_bim", [128, 4096])):
        G[nm] = P.din(nm, shp)
    for nm in ("s5_lre", "s5_lim", "s5_ls", "s5_ke", "s5_ka", "s5_kg"):
        G[nm] = [P.din(f"{nm}{d}", [128, 4096]) for d in range(2)]
    G["gdn_w_in"] = [P.din(f"gdn_w_in{q}", [1024, 1376]) for q in range(3)]
    G["outT"] = [P.dout(f"oT{c}", [128, T_L]) for c in range(8)]
    G["xres"] = P.dscr("xres", [1024, T])
    G["ppool"] = P.dscr("ppool", [512, T], BF16)
    G["p_tm"] = P.dscr("p_tm", [T, 512])
    G["y_tm"] = P.dscr("y_tm", [T, 512])
    G["pkvq"] = P.dscr("pkvq", [3072, T])
    G["sz_tm"] = P.dscr("sz_tm", [T_L, 1024])
    G["KT"] = P.dscr("KT", [1024, T], BF16)
    G["QT"] = P.dscr("QT", [1024, T], BF16)
    G["K_tm"] = P.dscr("K_tm", [T, 1024], BF16)
    G["V_tm"] = P.dscr("V_tm", [T, 1024], BF16)
    G["o_f"] = P.dscr("o_f", [T_L, 1024])
    G["o_b"] = P.dscr("o_b", [T_L, 1024])
    G["gb_d"] = P.dscr("gb_d", [128, (T // 128) * 32])
    G["VEC"] = P.sb([128, NV])
    G["BC"] = P.sb([128, NBC])
    G["MOD"] = P.sb([128, 2, 9, 8, 2])
    G["DER"] = P.sb([128, 512])
    P.dma("sp", G["VEC"][:], G["vec_d"], writes=["vec"])
    P.dma("sp", G["BC"][:], G["bc_d"], writes=["bc"])
    P.barrier()
    phase_mods(P, G)
    ffn_phase(P, G, 0, 0, 0, 0, T_L, first=True)
    phase_inproj0(P, G, T_L)
    phase_s5(P, G, T_L)
    phase_mixout0(P, G, T_L)
    ffn_phase(P, G, 1, 0, 2, 6, T_L)
    ffn_phase(P, G, 2, 1, 0, 0, T_L)
    phase_inproj1(P, G, T_L)
    phase_conv(P, G, T_L)
    phase_gdn(P, G, T_L)
    phase_mixout1(P, G, T_L)
    ffn_phase(P, G, 3, 1, 2, 6, T_L, lat_only=True, final=True)
    return P.finish()


def _s5_expand_sg(a):
    t = np.asarray(a, np.float32).T
    t = np.concatenate([t, t], 0)
    return np.ascontiguousarray(np.broadcast_to(t[:, :, None, None], (128, 32, 8, 16)).reshape(128, 4096))


def _const_tables(T_L):
    NL = T_L // 8
    cst = np.zeros((128, NCST), np.float32)
    idx = np.arange(128)
    cst[:, 0:128] = np.eye(128)
    cst[:, 128:256] = np.eye(128)[::-1]
    cst[:, 256:384] = 1.0
    cst[:, 384:512] = -1.0
    m, i = idx[:, None], idx[None, :]
    cst[:, 512:640] = (m <= i)
    cst[:, 640:768] = (m >= i)
    cst[:, 768:896] = (m > i)
    cst[:, 896:1024] = (m < i)
    cst[:, 1024:1152] = np.where(m <= i, 0.0, -30000.0)
    cst[:, 1152:1280] = np.where(m >= i, 0.0, -30000.0)
    cst[:, 1280:1408] = (m < i)
    cst[:, 1408:1536] = (m > i)
    sig = idx // 16
    for d in range(2):
        pos = sig if d == 0 else 7 - sig
        cst[:, 1536 + d * 128:1664 + d * 128] = (pos[:, None] <= pos[None, :])
    cst[:64, C_S5C] = 1.0
    cst[64:, C_S5C + 1] = 1.0
    cst[:32, C_J32:C_J32 + 32] = np.eye(32)[::-1]
    ks = {}
    tau = np.arange(8)
    for d in range(2):
        pos = tau if d == 0 else 7 - tau
        for nm, v in (("ke", pos + 1.0), ("ka", -(pos + 1.0)), ("kg", 7.0 - pos)):
            ks[f"s5_{nm}{d}"] = np.ascontiguousarray(np.broadcast_to(v.astype(np.float32)[None, None, :, None], (128, 32, 8, 16)).reshape(128, 4096))
    qv = np.concatenate([32 + np.arange(NL) + 1, np.arange(32) + 1]).astype(np.float32)
    ks["s5_qv"] = np.ascontiguousarray(np.broadcast_to(qv[None, :], (128, NL + 32)))
    def mi(L, w):
        left = w // 2
        right = w - 1 - left
        M = np.zeros((L, L), np.float64)
        for tp in range(L):
            lo, hi = max(tp - left, 0), min(tp + right + 1, L)
            M[lo:hi, tp] = 1.0 / (hi - lo)
        return (M - np.eye(L)).astype(np.float32)
    MWL = np.zeros((4, 128, 128), np.float32)
    MWC = np.zeros((16, 128, 128), np.float32)
    for gi, w in enumerate((2, 4, 8, 16)):
        m64 = mi(64, w)
        MWL[gi, :64, :64] = m64
        MWL[gi, 64:, 64:] = m64
        m256 = mi(256, w)
        for kb in range(2):
            for mb in range(2):
                MWC[gi * 4 + kb * 2 + mb] = m256[kb * 128:(kb + 1) * 128, mb * 128:(mb + 1) * 128]
    ks["MWL"], ks["MWC"] = MWL, MWC
    blk = lambda n: (idx[:, None] // n == idx[None, :] // n)
    bd32, bd64 = blk(32), blk(64)
    c4 = np.stack([bd32, bd64 & ~bd32, ~bd64, np.eye(128, dtype=bool)], 0).astype(np.float32)
    ks["cst4"] = np.ascontiguousarray(np.broadcast_to(c4[:, :, None, :], (4, 128, 4, 128)).transpose(1, 0, 2, 3).reshape(128, 2048))
    return cst, ks


def make_inputs(inp, T_L):
    f32 = np.float32
    cst, shared = _const_tables(T_L)
    cst[:, C_S5D:C_S5D + 32] = np.tile(np.asarray(inp["s5_d"][0], f32).reshape(32, 16).T, (8, 1))
    shared["cst"] = cst
    for l in range(2):
        for k in range(9):
            shared[f"wm{l}_{k}"] = np.ascontiguousarray(inp["w_mod"][l][:, k * 1024:(k + 1) * 1024])
    ffn = [(inp["ffn1_w13"][0], inp["ffn1_w2"][0]), (inp["ffn2_w13"][0], inp["ffn2_w2"][0]),
           (inp["ffn1_w13"][1], inp["ffn1_w2"][1]), (inp["ffn2_w13"][1], inp["ffn2_w2"][1])]
    for f, (w13, w2) in enumerate(ffn):
        for q in range(4):
            shared[f"w13_{f}_{q}"] = np.ascontiguousarray(w13[:, q * 1408:(q + 1) * 1408])
        for q in range(2):
            shared[f"w2_{f}_{q}"] = np.ascontiguousarray(w2[q * 1408:(q + 1) * 1408, :])
    shared["ev_w_in"] = np.ascontiguousarray(inp["ev_w_in"][0])
    shared["ev_w_out"] = np.ascontiguousarray(inp["ev_w_out"][0])
    shared["s5_w_glu"] = np.ascontiguousarray(inp["s5_w_glu"][0])
    shared["pool_w"] = np.ascontiguousarray(inp["pool_w"][0].reshape(512, 128))
    shared["gdn_w_out"] = np.ascontiguousarray(inp["gdn_w_out"][0])
    for q in range(3):
        shared[f"gdn_w_in{q}"] = np.ascontiguousarray(inp["gdn_w_in"][0][:, q * 1376:(q + 1) * 1376])
    for d in range(2):
        shared[f"s5_lre{d}"] = _s5_expand_sg(inp["s5_lambda_re"][0][d])
        shared[f"s5_lim{d}"] = _s5_expand_sg(inp["s5_lambda_im"][0][d])
        shared[f"s5_ls{d}"] = _s5_expand_sg(np.broadcast_to(np.asarray(inp["s5_log_step"][0][d])[:, None], (32, 64)))
    def c_exp(c):
        t = np.transpose(np.asarray(c, f32), (2, 0, 1))
        t = np.concatenate([t, t], 0)
        return np.ascontiguousarray(np.broadcast_to(t[:, :, None, :], (128, 32, 8, 16)).reshape(128, 4096))
    def b_exp(b):
        t = np.transpose(np.asarray(b, f32), (1, 0, 2))
        t = np.concatenate([t, t], 0)
        return np.ascontiguousarray(np.broadcast_to(t[:, :, None, :], (128, 32, 8, 16)).reshape(128, 4096))
    shared["s5_cre"], shared["s5_cim"] = c_exp(inp["s5_c_re"][0]), c_exp(inp["s5_c_im"][0])
    shared["s5_bre"], shared["s5_bim"] = b_exp(inp["s5_b_re"][0]), b_exp(inp["s5_b_im"][0])
    bc = np.zeros((128, NBC), f32)
    bc[:, B_ALOG:B_ALOG + 16] = np.asarray(inp["gdn_a_log"][0], f32).reshape(16)[None, :]
    bc[:, B_DTB:B_DTB + 16] = np.asarray(inp["gdn_dt_bias"][0], f32).reshape(16)[None, :]
    shared["bc"] = bc
    shared["ngbc"] = np.ascontiguousarray(np.broadcast_to(np.tile(np.asarray(inp["gdn_norm_g"][0], f32), 8)[None, :], (128, 1024)))
    vec0 = np.zeros((128, NV), f32)
    for l in range(2):
        for kn in range(3):
            vec0[:, V_NG + (l * 3 + kn) * 8:V_NG + (l * 3 + kn) * 8 + 8] = vec_cols(inp["norm_g"][l, kn])
        for k in range(9):
            vec0[:, V_BM + (l * 9 + k) * 8:V_BM + (l * 9 + k) * 8 + 8] = vec_cols(inp["b_mod"][l][k * 1024:(k + 1) * 1024])
    vec0[:, V_FNG:V_FNG + 8] = vec_cols(inp["final_norm_g"])
    vec0[:, V_BGLU:V_BGLU + 4] = vec_cols(inp["s5_b_glu"][0])
    vec0[:, V_PSC:V_PSC + 4] = vec_cols(inp["pool_scale"][0])
    cw = np.asarray(inp["gdn_conv_w"][0], f32)
    vec0[:, V_CONV:V_CONV + 168] = cw.reshape(7, 24, 128).transpose(2, 1, 0).reshape(128, 168)
    cctx = vec_cols(inp["c_ctx"])
    per_seq = []
    for b in range(4):
        m = dict(shared)
        vec = vec0.copy()
        cb = vec_cols(inp["c"][b])
        vec[:, V_CT:V_CT + 16] = np.stack([cb, cctx], -1).reshape(128, 16)
        m["vec"] = vec
        xT = np.ascontiguousarray(np.asarray(inp["x"][b][:T_L], f32).T)
        for c in range(8):
            m[f"xT{c}"] = np.ascontiguousarray(xT[c * 128:(c + 1) * 128])
        m["ctxT"] = np.ascontiguousarray(np.asarray(inp["ctx"][b], f32).T)
        per_seq.append(m)
    return per_seq


def kernel(**inputs):
    T_L = inputs["x"].shape[1]
    nc = build_program(T_L)
    in_maps = make_inputs(inputs, T_L)
    res = run_bass_kernel_spmd(nc, in_maps, core_ids=list(range(4)))
    out = np.zeros((4, T_L, D), np.float32)
    for b in range(4):
        r = res.results[b]
        oT = np.concatenate([r[f"oT{c}"] for c in range(8)], 0)
        out[b] = oT.T
    return out
```
